# Optimizing a Trainium2 kernel written in Bass

```python
import jax, jax.numpy as jnp
from jax import lax
import numpy as np

D_MODEL = 2048
BATCH = 2
SEQ = 8192
DEPTH = 2

GRID_W = 64
HEAD_DIM = 128
N_Q_HEADS = 8
N_KV_HEADS = 2
ATTN_W = N_Q_HEADS * HEAD_DIM
KV_W = N_KV_HEADS * HEAD_DIM
CONV_W = D_MODEL // 4
CONV_K = 3
SGU_GROUPS = 4
SGU_GC = 128
SGU_W = SGU_GROUPS * SGU_GC
CHUNK = 128
Q_BLOCK = 128
MIX_W = ATTN_W + CONV_W + SGU_W
ROPE_THETA = 10000.0
AXIS_ROT = HEAD_DIM // 2
EPS = 1e-6
IN_SPLITS = (ATTN_W, KV_W, KV_W, ATTN_W,
             CONV_W, CONV_W, CONV_W, CONV_W,
             SGU_W, SGU_W, SGU_W)
IN_W = 2 * ATTN_W + 2 * KV_W + 4 * CONV_W + 3 * SGU_W

kernel_name = "hybrid_parallel_attn_conv_sgu_encoder"


def rms_norm(x, w):
    xf = x.astype(jnp.float32)
    y = xf * lax.rsqrt(jnp.mean(xf * xf, axis=-1, keepdims=True) + EPS)
    return (y * w.astype(jnp.float32)).astype(x.dtype)


def axial_rope_tables(seq_len):
    rows = seq_len // GRID_W
    row = jnp.repeat(jnp.arange(rows, dtype=jnp.float32), GRID_W)
    col = jnp.tile(jnp.arange(GRID_W, dtype=jnp.float32), rows)
    inv_freq = ROPE_THETA ** (-jnp.arange(0, AXIS_ROT, 2, dtype=jnp.float32) / AXIS_ROT)
    ang = jnp.stack([row, col], axis=-1)[:, :, None] * inv_freq
    ang = jnp.broadcast_to(ang[:, :, None, :], (seq_len, 2, 2, AXIS_ROT // 2)).reshape(seq_len, HEAD_DIM)
    return jnp.cos(ang), jnp.sin(ang)


def rotate_half_axial(x):
    xa = x.reshape(x.shape[:-1] + (2, 2, AXIS_ROT // 2))
    rot = jnp.stack([-xa[..., 1, :], xa[..., 0, :]], axis=-2)
    return rot.reshape(x.shape)


def attention_branch(q, k, v, q_norm_w, k_norm_w, cos, sin):
    B, S, _ = q.shape
    g = N_Q_HEADS // N_KV_HEADS
    q = rms_norm(q.reshape(B, S, N_Q_HEADS, HEAD_DIM), q_norm_w).astype(jnp.float32)
    k = rms_norm(k.reshape(B, S, N_KV_HEADS, HEAD_DIM), k_norm_w).astype(jnp.float32)
    v = v.reshape(B, S, N_KV_HEADS, HEAD_DIM)
    c, s_ = cos[None, :, None, :], sin[None, :, None, :]
    q = q * c + rotate_half_axial(q) * s_
    k = k * c + rotate_half_axial(k) * s_
    n_blk = S // Q_BLOCK
    qb = q.reshape(B, n_blk, Q_BLOCK, N_KV_HEADS, g, HEAD_DIM).transpose(1, 0, 2, 3, 4, 5)
    scale = HEAD_DIM ** -0.5

    def attend(q_blk):
        scores = jnp.einsum('bqkgd,bskd->bkgqs', q_blk, k) * scale
        p = jax.nn.softmax(scores, axis=-1)
        return jnp.einsum('bkgqs,bskd->bqkgd', p.astype(v.dtype), v)

    o = lax.map(attend, qb)
    return o.transpose(1, 0, 2, 3, 4, 5).reshape(B, S, ATTN_W)


def short_conv_branch(c_in, c_b, c_c, conv_w):
    h = c_c * c_in
    hp = jnp.pad(h, ((0, 0), (1, 1), (0, 0)))
    y = hp[:, :-2] * conv_w[:, 0] + hp[:, 1:-1] * conv_w[:, 1] + hp[:, 2:] * conv_w[:, 2]
    return c_b * y


def sgu_branch(u, v, sgu_norm_w, sgu_w, sgu_b):
    B, S, _ = u.shape
    u = jax.nn.gelu(u, approximate=False)
    v = jax.nn.gelu(v, approximate=False)
    v = rms_norm(v.reshape(B, S, SGU_GROUPS, SGU_GC), sgu_norm_w.reshape(SGU_GROUPS, SGU_GC))
    vc = v.reshape(B, S // CHUNK, CHUNK, SGU_GROUPS, SGU_GC)
    s = jnp.einsum('gpq,bnqgc->bnpgc', sgu_w, vc) + sgu_b.T[:, :, None]
    return u * s.reshape(B, S, SGU_W)


def hybrid_layer(x, cos, sin, norm_w, w_in, q_norm_w, k_norm_w, conv_w, sgu_norm_w, sgu_w, sgu_b,
                 branch_norm_w, w_out):
    h = rms_norm(x, norm_w)
    proj = jnp.einsum('bsd,de->bse', h, w_in)
    cuts, acc = [], 0
    for width in IN_SPLITS[:-1]:
        acc += width
        cuts.append(acc)
    (q, k, v, g_attn, c_in, c_b, c_c, g_conv, s_u, s_v, g_sgu) = jnp.split(proj, cuts, axis=-1)

    o_attn = attention_branch(q, k, v, q_norm_w, k_norm_w, cos, sin)
    o_conv = short_conv_branch(c_in, c_b, c_c, conv_w)
    o_sgu = sgu_branch(s_u, s_v, sgu_norm_w, sgu_w, sgu_b)

    o_attn = rms_norm(o_attn, branch_norm_w[:ATTN_W]) * jax.nn.silu(g_attn)
    o_conv = rms_norm(o_conv, branch_norm_w[ATTN_W:ATTN_W + CONV_W]) * jax.nn.silu(g_conv)
    o_sgu = rms_norm(o_sgu, branch_norm_w[ATTN_W + CONV_W:]) * jax.nn.silu(g_sgu)
    mixed = jnp.concatenate([o_attn, o_conv, o_sgu], axis=-1)
    return x + jnp.einsum('bse,ed->bsd', mixed, w_out)


def setup_inputs(seed: int = 0) -> dict:
    key = jax.random.key(seed)
    ks = jax.random.split(key, 12)

    def nrm(k, shape, scale):
        return jax.random.normal(k, shape, jnp.float32) * scale

    return {
        "x": nrm(ks[0], (BATCH, SEQ, D_MODEL), 1.0),
        "norm_w": 1.0 + nrm(ks[1], (DEPTH, D_MODEL), 0.02),
        "w_in": nrm(ks[2], (DEPTH, D_MODEL, IN_W), D_MODEL ** -0.5),
        "q_norm_w": 1.0 + nrm(ks[3], (DEPTH, HEAD_DIM), 0.02),
        "k_norm_w": 1.0 + nrm(ks[4], (DEPTH, HEAD_DIM), 0.02),
        "conv_w": nrm(ks[5], (DEPTH, CONV_W, CONV_K), CONV_K ** -0.5),
        "sgu_norm_w": 1.0 + nrm(ks[6], (DEPTH, SGU_W), 0.02),
        "sgu_w": nrm(ks[7], (DEPTH, SGU_GROUPS, CHUNK, CHUNK), CHUNK ** -0.5),
        "sgu_b": 1.0 + nrm(ks[8], (DEPTH, SGU_GROUPS, CHUNK), 0.01),
        "branch_norm_w": 1.0 + nrm(ks[9], (DEPTH, MIX_W), 0.02),
        "w_out": nrm(ks[10], (DEPTH, MIX_W, D_MODEL), MIX_W ** -0.5),
        "final_norm_w": 1.0 + nrm(ks[11], (D_MODEL,), 0.02),
    }


def reference(x, norm_w, w_in, q_norm_w, k_norm_w, conv_w, sgu_norm_w, sgu_w, sgu_b,
              branch_norm_w, w_out, final_norm_w):
    cos, sin = axial_rope_tables(x.shape[1])
    for layer in range(DEPTH):
        x = hybrid_layer(x, cos, sin, norm_w[layer], w_in[layer], q_norm_w[layer], k_norm_w[layer],
                         conv_w[layer], sgu_norm_w[layer], sgu_w[layer], sgu_b[layer],
                         branch_norm_w[layer], w_out[layer])
    return rms_norm(x, final_norm_w)
```

```python
import re
import numpy as np
from contextlib import ExitStack
import concourse.bass as bass
import concourse.mybir as mybir
from concourse.bass_utils import run_bass_kernel_spmd

F32 = mybir.dt.float32
BF16 = mybir.dt.bfloat16
AF = mybir.ActivationFunctionType
ALU = mybir.AluOpType
AX = mybir.AxisListType

D = 2048
INW = 6144
HD = 128
EPS = 1e-6
NCORES = 8
NR = 4


class Tok:
    __slots__ = ("sem", "val", "key")

    def __init__(self, sem, val, key):
        self.sem, self.val, self.key = sem, val, key


def _add(d, tok):
    o = d.get(tok.key)
    if o is None or o.val < tok.val:
        d[tok.key] = tok


class Buf:
    def __init__(self, name, sem_key=None):
        self.name = name
        self.w = {}
        self.r = {}
        self.sem_key = sem_key or name


def alias(name, olds):
    b = Buf(name, sem_key=re.sub(r"^([A-Za-z]+)\d+", r"\1", name))
    for o in olds:
        for t in list(o.w.values()) + list(o.r.values()):
            _add(b.r, t)
    return b


class Eng:
    def __init__(self, K, name, is_pe=False):
        self.K = K
        self.name = name
        self.key = "e_" + name
        self.sem = K.newsem("p_" + name)
        self.cnt = 0
        self.waited = {}
        self.is_pe = is_pe
        self.prog = []

    def wait(self, tok):
        if self.waited.get(tok.key, 0) >= tok.val:
            return
        self.waited[tok.key] = tok.val
        sem, val = tok.sem, tok.val
        self.prog.append(lambda e: e.wait_ge(sem, val))

    def deps(self, reads, writes, extra, is_dma=False):
        for b in reads:
            for tok in b.w.values():
                if tok.key == self.key and self.is_pe:
                    continue
                self.wait(tok)
        for b in writes:
            for tok in b.w.values():
                if tok.key == self.key:
                    continue
                if is_dma and tok.key.startswith("d_"):
                    continue
                self.wait(tok)
            for tok in b.r.values():
                if tok.key == self.key:
                    continue
                self.wait(tok)
        for tok in extra:
            self.wait(tok)

    def op(self, fn, reads=(), writes=(), extra=()):
        self.deps(reads, writes, extra)
        self.cnt += 1
        sem = self.sem
        self.prog.append(lambda e: fn(e).then_inc(sem, 1))
        tok = Tok(sem, self.cnt, self.key)
        for b in writes:
            b.w = {tok.key: tok}
            b.r = {}
        for b in reads:
            if b not in writes:
                _add(b.r, tok)
        return tok

    def dma(self, out, in_, sem_buf, reads=(), writes=(), extra=()):
        self.deps(reads, writes, extra, is_dma=True)
        state = self.K.dsems.setdefault(sem_buf.sem_key, [None, 0])
        if state[0] is None:
            state[0] = self.K.newsem("d_" + sem_buf.sem_key)
        state[1] += 16
        sem = state[0]
        self.prog.append(lambda e: e.dma_start(out=out, in_=in_).then_inc(sem, 16))
        tok = Tok(sem, state[1], "d_" + sem_buf.sem_key)
        for b in writes:
            if b.r:
                b.w = {}
                b.r = {}
            _add(b.w, tok)
        for b in reads:
            _add(b.r, tok)
        return tok

    def collective(self, groups, src, dst, reads, writes, name):
        self.deps(reads, writes, ())
        sem = self.K.newsem("cc_" + name)
        self.prog.append(
            lambda e: e.collective_compute("AllGather", ALU.bypass, replica_groups=groups,
                                           ins=[src], outs=[dst], dma_qos="P3").then_inc(sem))
        tok = Tok(sem, 1, "cc_" + name)
        for b in writes:
            b.w = {tok.key: tok}
            b.r = {}
        for b in reads:
            _add(b.r, tok)
        return tok


class Kern:
    def __init__(self, nc, es):
        self.nc = nc
        self.es = es
        self.nsem = 0
        self.dsems = {}
        self.pe = Eng(self, "pe", is_pe=True)
        self.act = Eng(self, "act")
        self.dve = Eng(self, "dve")
        self.pool = Eng(self, "pool")
        self.sp = Eng(self, "sp")

    def newsem(self, name):
        self.nsem += 1
        return self.es.enter_context(self.nc.semaphore(f"{name}_{self.nsem}"))

    def sb(self, name, shape, dt):
        return self.es.enter_context(self.nc.sbuf_tensor(name, list(shape), dt))

    def ps(self, name, shape, dt):
        return self.es.enter_context(self.nc.psum_tensor(name, list(shape), dt))


def build_program(TOK, DEPTH=2, dbg=False):
    NT = TOK // 128
    S = NR * TOK
    NKT = S // 128
    KCH = 8 if NKT >= 8 else NKT
    NCH = NKT // KCH
    DV = 258

    nc = bass.Bass("TRN2", target_bir_lowering=False, dynamic_dma_scratch_size=8192)
    es = ExitStack()
    K = Kern(nc, es)
    pe, act, dve, pool, sp = K.pe, K.act, K.dve, K.pool, K.sp

    def din(name, shape, dt=F32):
        return nc.dram_tensor(name, list(shape), dt, kind="ExternalInput").ap()

    def dint(name, shape, dt):
        return nc.dram_tensor(name, list(shape), dt, kind="Internal").ap()

    x_in = din("x", [TOK, D])
    w_in = din("w_in", [DEPTH, D, INW])
    w_out = din("w_out", [DEPTH, D, D])
    nw_pp = din("nw_pp", [DEPTH, 128, 16])
    bw_pp = din("bw_pp", [DEPTH, 128, 16])
    qw = din("qw", [DEPTH, 1, 128])
    kw = din("kw", [DEPTH, 1, 128])
    qws = din("qws", [DEPTH, 1, 128])
    kws = din("kws", [DEPTH, 1, 128])
    cw = din("cw", [DEPTH, 3, 512])
    snw = din("snw", [DEPTH, 1, 512])
    sb_pp = din("sb_pp", [DEPTH, 128, 4])
    wsT = din("wsT", [DEPTH, 128, 4, 128])
    fw = din("fw", [1, D])
    cos_t = din("cos_t", [TOK, 128])
    sin_t = din("sin_t", [TOK, 128])
    sel = din("sel", [8, 2])
    ident_d = din("ident", [128, 128])
    y_out = nc.dram_tensor("y", [TOK, D], F32, kind="ExternalOutput").ap()

    x1 = dint("x1", [TOK, D], F32)
    qT_s = [dint(f"qT_s{l}", [NT, 128, 8, 128], BF16) for l in range(DEPTH)]
    g_s = [dint(f"g_s{l}", [TOK, 1024], BF16) for l in range(DEPTH)]
    mT_s = [dint(f"mT_s{l}", [NT, 128, 8, 128], BF16) for l in range(DEPTH)]
    kt_src = [dint(f"kt_src{l}", [256, TOK], BF16) for l in range(DEPTH)]
    kt_dst = [dint(f"kt_dst{l}", [NR * 256, TOK], BF16) for l in range(DEPTH)]
    NH = NT // 2
    v_src = [[dint(f"v_src{l}_{i}", [128, NH * DV], BF16) for i in range(2)] for l in range(DEPTH)]
    v_dst = [[dint(f"v_dst{l}_{i}", [NR * 128, NH * DV], BF16) for i in range(2)] for l in range(DEPTH)]
    xe_src = [dint(f"xe_src{l}", [2, D], F32) for l in range(DEPTH)]
    xe_dst = [dint(f"xe_dst{l}", [NR * 2, D], F32) for l in range(DEPTH)]
    groups = [[0, 1, 2, 3], [4, 5, 6, 7]]

    x_src_b = [[Buf(f"xs{l}_{t}") for t in range(NT)] for l in range(DEPTH + 1)]
    qT_b = [[Buf(f"qTb{l}_{t}") for t in range(NT)] for l in range(DEPTH)]
    g_b = [[Buf(f"gb{l}_{t}") for t in range(NT)] for l in range(DEPTH)]
    mT_b = [[Buf(f"mTb{l}_{t}") for t in range(NT)] for l in range(DEPTH)]
    ktsrc_b = [Buf(f"ktsrc{l}") for l in range(DEPTH)]
    ktdst_b = [Buf(f"ktdst{l}") for l in range(DEPTH)]
    vsrc_b = [[Buf(f"vsrc{l}_{i}") for i in range(2)] for l in range(DEPTH)]
    vdst_b = [[Buf(f"vdst{l}_{i}") for i in range(2)] for l in range(DEPTH)]
    xesrc_b = [Buf(f"xesrc{l}") for l in range(DEPTH)]
    xedst_b = [Buf(f"xedst{l}") for l in range(DEPTH)]
    y_b = Buf("y")

    R1 = K.sb("R1", [128, 16, 2048], BF16)
    R2N = max(32768 + 0, 2 * S + NKT * DV)
    R2N = max(R2N, 16384 + 8192 + 2 * TOK + NT * DV)
    R2 = K.sb("R2", [128, R2N], BF16)
    R3N = 27 * 1024
    R3 = K.sb("R3", [128, R3N], BF16)
    xt = [K.sb(f"xt{i}", [128, D], F32) for i in range(2)]
    cs = [K.sb(f"cs{i}", [128, 2, 128], F32) for i in range(3)]
    ident = K.sb("ident_sb", [128, 128], F32)
    selt = K.sb("selt", [8, 2], F32)
    xg = xt[1][0:8, :]
    xh = xt[0][0:2, :]
    hTh = K.sb("hTh", [128, 16, 2], BF16)
    hh = K.sb("hh", [2, 512], F32)
    hh_ci = hh
    nwt = K.sb("nwt", [128, 16], F32)
    bwt = K.sb("bwt", [128, 16], F32)
    qwb = K.sb("qwb", [128, 128], F32)
    kwb = K.sb("kwb", [128, 128], F32)
    qwsb = K.sb("qwsb", [128, 128], F32)
    kwsb = K.sb("kwsb", [128, 128], F32)
    cwb = K.sb("cwb", [128, 3, 512], F32)
    snwb = K.sb("snwb", [128, 512], F32)
    sbt = K.sb("sbt", [128, 4], F32)
    wst = K.sb("wst", [128, 4, 128], BF16)
    neghalf = K.sb("neghalf", [128, 4], F32)
    ss = K.sb("ss", [128, 44], F32)
    rs = K.sb("rs", [128, 44], F32)
    rx = K.sb("rx", [128, 16], F32)
    e2x = K.sb("e2x", [128, 16], F32)
    negc = K.sb("negc", [128, 4], F32)
    stat_i = [0]

    def r3(off, n):
        return R3[:, off:off + n]
    wring = [r3(i * 8192, 8192).rearrange("p (k n) -> p k n", k=16) for i in range(2)]
    tA = [[r3(16384 + (s * 4 + i) * 1024, 1024).bitcast(F32) for i in range(4)] for s in range(2)]
    tA += [[xt[s][:, i * 512:(i + 1) * 512] for i in range(4)] for s in range(2)]
    gst = [r3(24576 + i * 512, 512) for i in range(2)]
    qst = [r3(25600 + i * 512, 512).rearrange("p (h t) -> p h t", h=4) for i in range(2)]
    mst = [r3(26624 + i * 512, 512).rearrange("p (h t) -> p h t", h=4) for i in range(2)]
    qTt = [r3(i * 1024, 1024).rearrange("p (h t) -> p h t", h=8) for i in range(2)]
    gt = [r3(2048 + i * 1024, 1024) for i in range(2)]
    mTt = [r3(4096 + i * 1024, 1024).rearrange("p (h t) -> p h t", h=8) for i in range(2)]
    PT = [r3(6144 + i * 1024, 1024) for i in range(3)]
    ao = [r3(9216 + i * 2048, 2048).bitcast(F32) for i in range(2)]
    mTa = [r3(13312 + i * 1024, 1024).rearrange("p (h t) -> p h t", h=8) for i in range(2)]
    junkB = r3(15360, 2048)
    fwb = r3(17408, 4096).bitcast(F32)
    ia = R2[:, 0:16384].bitcast(F32).rearrange("p (t c) -> p t c", c=512)
    ia16 = R2[:, 0:8192].rearrange("p (t c) -> p t c", c=512)
    ib16 = R2[:, 16384:24576].rearrange("p (t c) -> p t c", c=512)
    junkA = R2[:, 16384:16384 + 2048]
    KTst = R2[:, 24576:24576 + 2 * TOK].rearrange("p (h t) -> p h t", h=2)
    Vst = R2[:, 24576 + 2 * TOK:24576 + 2 * TOK + NT * DV].rearrange("p (t c) -> p t c", c=DV)
    KT = R2[:, 0:2 * S].rearrange("p (h s) -> p h s", h=2)
    Vaug = R2[:, 2 * S:2 * S + NKT * DV].rearrange("p (k d) -> p k d", d=DV)

    pp = [K.ps(f"pp{i}", [128, 1024], F32) for i in range(4)]
    bankb = [Buf(f"bank{i}") for i in range(8)]

    def bk(i):
        return pp[i // 2][:, (i % 2) * 512:(i % 2 + 1) * 512]

    B = {}

    def bf(name):
        if name not in B:
            B[name] = Buf(name)
        return B[name]

    def newstat():
        i = stat_i[0] % 10
        stat_i[0] += 1
        return i

    def pbc(ap):
        return ap.partition_broadcast(128).rearrange("p o n -> p (o n)")

    def rstd_from_ss(ss_ap, rs_ap, inv_n, ssb, rsb, eps_ap=None, eps_b=None):
        n = ss_ap.shape[1]
        epsv = EPS if eps_ap is None else eps_ap
        pool.op(lambda e: e.tensor_scalar(out=rs_ap, in0=ss_ap, scalar1=inv_n, scalar2=epsv,
                                          op0=ALU.mult, op1=ALU.add), reads=[ssb] + ([eps_b] if eps_b else []), writes=[rsb])
        pool.op(lambda e: e.tensor_tensor(out=rs_ap, in0=rs_ap, in1=neghalf[:, 0:n], op=ALU.pow),
                reads=[rsb, bf("neghalf")], writes=[rsb])

    sp.dma(ident[:], ident_d[:, :], bf("ident"), writes=[bf("ident")])
    sp.dma(selt[:], sel[:, :], bf("selt"), writes=[bf("selt")])
    dve.op(lambda e: e.memset(neghalf[:], -0.5), writes=[bf("neghalf")])

    wring_b = [Buf("wring0"), Buf("wring1")]
    R1_old = []
    R2_old = []
    R3_old = []

    GROUPS = [("KV", 1024), ("QA", 0), ("QB", 512), ("GA0", 1536), ("GA1", 2048),
              ("CI", 2560), ("CC", 3584), ("CG", 4096), ("CB", 3072),
              ("SV", 5120), ("SG", 5632), ("SU", 4608)]

    def load_wgroup(l, gi):
        slot = gi % 2
        c0 = GROUPS[gi][1]
        for half in range(2):
            src = w_in[l, half * 1024:(half + 1) * 1024, c0:c0 + 512].rearrange("(k p) n -> p k n", p=128)
            pool.dma(wring[slot][:, half * 8:(half + 1) * 8, :], src, wring_b[slot], writes=[wring_b[slot]])

    def run_pipeline(ntiles, make_stages):
        pl = []
        it = 0
        while True:
            if it < ntiles:
                pl.append(make_stages(it))
            live = False
            maxk = max(len(x) for x in pl)
            for k in range(maxk - 1, -1, -1):
                tt = it - k
                if 0 <= tt < len(pl) and k < len(pl[tt]):
                    pl[tt][k]()
            if it >= ntiles - 1 and all(it - tt >= len(pl[tt]) - 1 for tt in range(len(pl))):
                break
            it += 1


    for l in range(DEPTH):
        xs_ap = x_in if l == 0 else x1
        xsb = x_src_b[l]
        for (t_sb, name, src) in [(nwt, "nwt", nw_pp[l]), (bwt, "bwt", bw_pp[l]), (sbt, "sbt", sb_pp[l])]:
            sp.dma(t_sb[:], src, bf(name), writes=[bf(name)])
        sp.dma(qwb[:], pbc(qw[l]), bf("qwb"), writes=[bf("qwb")])
        sp.dma(kwb[:], pbc(kw[l]), bf("kwb"), writes=[bf("kwb")])
        sp.dma(qwsb[:], pbc(qws[l]), bf("qwsb"), writes=[bf("qwsb")])
        sp.dma(kwsb[:], pbc(kws[l]), bf("kwsb"), writes=[bf("kwsb")])
        sp.dma(snwb[:], pbc(snw[l]), bf("snwb"), writes=[bf("snwb")])
        for j in range(3):
            sp.dma(cwb[:, j, :], pbc(cw[l, j:j + 1, :]), bf("cwb"), writes=[bf("cwb")])
        pool.dma(wst[:], wsT[l], bf("wst"), writes=[bf("wst")])
        dve.op(lambda e: e.tensor_reduce(out=negc[:, 1:2], in_=qwb[:], axis=AX.X, op=ALU.max, apply_absolute_value=True),
               reads=[bf("qwb")], writes=[bf("negc")])
        dve.op(lambda e: e.tensor_reduce(out=negc[:, 2:3], in_=kwb[:], axis=AX.X, op=ALU.max, apply_absolute_value=True),
               reads=[bf("kwb")], writes=[bf("negc")])
        dve.op(lambda e: e.tensor_scalar(out=negc[:, 0:1], in0=negc[:, 1:2], scalar1=-float(HD) ** 0.5, scalar2=negc[:, 2:3],
                                         op0=ALU.mult, op1=ALU.mult), reads=[bf("negc")], writes=[bf("negc")])
        dve.op(lambda e: e.tensor_scalar(out=negc[:, 0:1], in0=negc[:, 0:1], scalar1=80.0, scalar2=0.0,
                                         op0=ALU.add, op1=ALU.min), reads=[bf("negc")], writes=[bf("negc")])

        hT_b = [alias(f"hT{l}_{t}", R1_old) for t in range(NT)]
        KTst_b = alias(f"KTst{l}", R2_old)
        Vst_b = alias(f"Vst{l}", R2_old)
        dve.op(lambda e: e.memset(Vst[:, :, 128:129], 1.0), writes=[Vst_b])
        ia_b = [alias(f"ia{l}_{t}", R2_old) for t in range(NT)]
        ib_b = [alias(f"ib{l}_{t}", R2_old) for t in range(NT)]
        ia16_b = [alias(f"iah{l}_{t}", R2_old) for t in range(NT)]
        junkA_b = ib_b[0]
        regionsA = ([(t * 2048, (t + 1) * 2048, ia_b[t]) for t in range(NT)]
                    + [(t * 1024, (t + 1) * 1024, ia16_b[t]) for t in range(NT)]
                    + [(32768 + t * 1024, 32768 + (t + 1) * 1024, ib_b[t]) for t in range(NT)]
                    + [(49152, 49152 + 4 * TOK, KTst_b), (49152 + 4 * TOK, 49152 + 4 * TOK + NT * DV * 2, Vst_b)])
        if l > 0:
            wring_b = [alias(f"wring{l}_0", R3_old), alias(f"wring{l}_1", R3_old)]
        tA_b = [[alias(f"tA{l}_{s}_{i}", R3_old) for i in range(4)] for s in range(2)]
        gst_b = [alias(f"gst{l}_{i}", R3_old) for i in range(2)]
        qst_b = [alias(f"qst{l}_{i}", R3_old) for i in range(2)]
        mst_b = [alias(f"mst{l}_{i}", R3_old) for i in range(2)]

        load_wgroup(l, 0)
        load_wgroup(l, 1)

        sp.dma(xe_src[l][0:1, :], xs_ap[0:1, :], xesrc_b[l], reads=[xsb[0]], writes=[xesrc_b[l]])
        sp.dma(xe_src[l][1:2, :], xs_ap[TOK - 1:TOK, :], xesrc_b[l], reads=[xsb[NT - 1]], writes=[xesrc_b[l]])
        pool.collective(groups, xe_src[l].opt(), xe_dst[l].opt(), [xesrc_b[l]], [xedst_b[l]], f"xe{l}")

        rx_b = [Buf(f"rx{l}_{t}") for t in range(NT)]

        def a0_stages(t):
            sl = t % 2
            xb = bf(f"xt{sl}")

            def s0():
                sp.dma(xt[sl][:], xs_ap[t * 128:(t + 1) * 128, :], xb, reads=[xsb[t]], writes=[xb])
                si = newstat()
                ssb = bf(f"ss{si}")
                act.op(lambda e: e.activation(out=junkA, in_=xt[sl][:], func=AF.Square, accum_out=ss[:, si * 4:si * 4 + 1]),
                       reads=[xb], writes=[junkA_b, ssb])
                pool.op(lambda e: e.tensor_scalar(out=rx[:, t:t + 1], in0=ss[:, si * 4:si * 4 + 1], scalar1=1.0 / D,
                                                  scalar2=EPS, op0=ALU.mult, op1=ALU.add), reads=[ssb], writes=[rx_b[t]])
                pool.op(lambda e: e.tensor_scalar(out=e2x[:, t:t + 1], in0=rx[:, t:t + 1], scalar1=EPS, scalar2=None,
                                                  op0=ALU.mult), reads=[rx_b[t]], writes=[rx_b[t]])
                pool.op(lambda e: e.tensor_tensor(out=rx[:, t:t + 1], in0=rx[:, t:t + 1], in1=neghalf[:, 0:1], op=ALU.pow),
                        reads=[rx_b[t], bf("neghalf")], writes=[rx_b[t]])

            def s2():
                for j in range(4):
                    pb = 4 + (j % 2)
                    for i in range(4):
                        k = 4 * j + i
                        pe.op(lambda e, k=k, i=i, pb=pb: e.transpose(
                            out=bk(pb)[:, i * 128:(i + 1) * 128], in_=xt[sl][:, k * 128:(k + 1) * 128], identity=ident[:]),
                            reads=[xb, bf("ident")], writes=[bankb[pb]])
                    dve.op(lambda e, j=j, pb=pb: e.tensor_tensor(
                        out=R1[:, 4 * j:4 * j + 4, t * 128:(t + 1) * 128],
                        in0=bk(pb).rearrange("p (i t) -> p i t", i=4),
                        in1=nwt[:, 4 * j:4 * j + 4].unsqueeze(2).broadcast_to([128, 4, 128]), op=ALU.mult),
                        reads=[bankb[pb], bf("nwt")], writes=[hT_b[t]])
            return [s0, s2]

        run_pipeline(NT, a0_stages)

        xeb = bf("xt0")
        sp.dma(xg, xe_dst[l][:, :], bf("xt1"), reads=[xedst_b[l]], writes=[bf("xt1")])
        for n in range(4):
            pe.op(lambda e, n=n: e.matmul(bk(4)[0:2, :],
                                          lhsT=selt[:, :], rhs=xg[:, n * 512:(n + 1) * 512], start=True, stop=True),
                  reads=[bf("selt"), bf("xt1")], writes=[bankb[4]])
            dve.op(lambda e, n=n: e.tensor_copy(out=xh[:, n * 512:(n + 1) * 512], in_=bk(4)[0:2, :]),
                   reads=[bankb[4]], writes=[xeb])
        sh_b, rh_b = bf("ssh"), bf("rsh")
        dve.op(lambda e: e.scalar_tensor_tensor(out=junkA[0:2, :], in0=xh, scalar=1.0, in1=xh,
                                                op0=ALU.mult, op1=ALU.mult, accum_out=ss[0:2, 40:41]),
               reads=[xeb], writes=[junkA_b, sh_b])
        pool.op(lambda e: e.tensor_scalar(out=rs[0:2, 40:41], in0=ss[0:2, 40:41], scalar1=1.0 / D, scalar2=EPS,
                                          op0=ALU.mult, op1=ALU.add), reads=[sh_b], writes=[rh_b])
        pool.op(lambda e: e.tensor_tensor(out=rs[0:2, 40:41], in0=rs[0:2, 40:41], in1=neghalf[0:2, 0:1], op=ALU.pow),
                reads=[rh_b, bf("neghalf")], writes=[rh_b])
        act.op(lambda e: e.activation(out=xh, in_=xh, func=AF.Copy, scale=rs[0:2, 40:41]),
               reads=[xeb, rh_b], writes=[xeb])
        for j in range(4):
            for i in range(4):
                k = 4 * j + i
                pe.op(lambda e, k=k, i=i: e.transpose(out=bk(5)[:, i * 2:i * 2 + 2], in_=xh[:, k * 128:(k + 1) * 128],
                                                      identity=ident[0:2, 0:2]),
                      reads=[xeb, bf("ident")], writes=[bankb[5]])
            dve.op(lambda e, j=j: e.tensor_tensor(
                out=hTh[:, 4 * j:4 * j + 4, :],
                in0=bk(5)[:, 0:8].rearrange("p (i t) -> p i t", i=4),
                in1=nwt[:, 4 * j:4 * j + 4].unsqueeze(2).broadcast_to([128, 4, 2]), op=ALU.mult),
                reads=[bankb[5], bf("nwt")], writes=[bf("hTh")])


        tA_b += [[alias(f"tAx{l}_{s}_{i}", [bf(f"xt{s}")]) for i in range(4)] for s in range(2)]
        bank_ctr = [0]
        slot_ctr = [0]
        cs_ctr = [0]

        def proj_mm(gi, t):
            b = bank_ctr[0] % 4
            bank_ctr[0] += 1
            ws = gi % 2
            for k in range(16):
                pe.op(lambda e, k=k, b=b, ws=ws, t=t: e.matmul(
                    bk(b), lhsT=R1[:, k, t * 128:(t + 1) * 128], rhs=wring[ws][:, k, :],
                    start=(k == 0), stop=(k == 15)),
                    reads=[hT_b[t], wring_b[ws]], writes=[bankb[b]])
            return b

        def transposes_out(src_ap, src_b, nblk, dst_sb, dst_b, scale_tab, evac_eng, dram_ap, dram_b, pbank):
            for i in range(nblk):
                pe.op(lambda e, i=i: e.transpose(out=bk(pbank)[:, i * 128:(i + 1) * 128],
                                                 in_=src_ap[:, i * 128:(i + 1) * 128], identity=ident[:]),
                      reads=[src_b, bf("ident")], writes=[bankb[pbank]])
            pin = bk(pbank)[:, 0:nblk * 128].rearrange("p (i t) -> p i t", i=nblk)
            if scale_tab is None:
                act.op(lambda e: e.activation(out=dst_sb, in_=pin, func=AF.Copy),
                       reads=[bankb[pbank]], writes=[dst_b])
            else:
                for i in range(nblk):
                    act.op(lambda e, i=i: e.activation(out=dst_sb[:, i, :], in_=pin[:, i, :], func=AF.Copy,
                                                       scale=scale_tab[:, i:i + 1]),
                           reads=[bankb[pbank], bf("bwt")], writes=[dst_b])
            if dram_ap is not None:
                sp.dma(dram_ap, dst_sb, dst_b, reads=[dst_b], writes=[dram_b])

        def load_cs(t, wtab, wtab_b, wstab, wstab_b):
            csl = cs_ctr[0] % 3
            cs_ctr[0] += 1
            csb = bf(f"cs{csl}")
            sp.dma(cs[csl][:, 0, :], cos_t[t * 128:(t + 1) * 128, :], csb, writes=[csb])
            sp.dma(cs[csl][:, 1, :], sin_t[t * 128:(t + 1) * 128, :], csb, writes=[csb])
            pool.op(lambda e: e.tensor_tensor(out=cs[csl][:, 0, :], in0=cs[csl][:, 0, :], in1=wtab[:], op=ALU.mult),
                    reads=[csb, wtab_b], writes=[csb])
            pool.op(lambda e: e.tensor_tensor(out=cs[csl][:, 1, :], in0=cs[csl][:, 1, :], in1=wstab[:], op=ALU.mult),
                    reads=[csb, wstab_b], writes=[csb])
            return csl

        def group_pre(gi, gname):
            if gi >= 1 and gi + 1 < len(GROUPS):
                load_wgroup(l, gi + 1)
            if gname in ("CI", "CC"):
                hb = 4
                for k in range(16):
                    pe.op(lambda e, k=k, ws=gi % 2: e.matmul(bk(hb)[0:2, :], lhsT=hTh[:, k, :], rhs=wring[ws][:, k, :],
                                                             start=(k == 0), stop=(k == 15)),
                          reads=[bf("hTh"), wring_b[gi % 2]], writes=[bankb[hb]])
                if gname == "CI":
                    act.op(lambda e: e.activation(out=hh_ci[:], in_=bk(hb)[0:2, :], func=AF.Copy),
                           reads=[bankb[hb]], writes=[bf("hh")])
                else:
                    dve.op(lambda e: e.tensor_tensor(out=hh[:], in0=bk(hb)[0:2, :], in1=hh_ci[:], op=ALU.mult),
                           reads=[bankb[hb], bf("hh")], writes=[bf("hh")])

        def make_stages(t, gi, gname):
            s4 = slot_ctr[0] % 4
            s = slot_ctr[0] % 2
            slot_ctr[0] += 1
            tb, tbb = tA[s4], tA_b[s4]
            t0, t1, t2, t3 = tb
            st = {}
            v3 = lambda a: a.rearrange("p (h d) -> p h d", d=128)

            def rope_s0(ncol, wt, wtb, wst_, wstb):
                b = st["b"]
                csl = load_cs(t, wt, wtb, wst_, wstb)
                si = newstat()
                st["csl"], st["si"] = csl, si
                act.op(lambda e: e.activation(out=t0[:, 0:ncol], in_=bk(b)[:, 0:ncol], func=AF.Square),
                       reads=[bankb[b]], writes=[tbb[0]])

            def rope_s0b(ncol):
                nh = ncol // 128
                si = st["si"]
                ssb, rsb = bf(f"ss{si}"), bf(f"rs{si}")
                dve.op(lambda e: e.tensor_reduce(out=ss[:, si * 4:si * 4 + nh], in_=v3(t0[:, 0:ncol]),
                                                 axis=AX.X, op=ALU.add), reads=[tbb[0]], writes=[ssb])
                rstd_from_ss(ss[:, si * 4:si * 4 + nh], rs[:, si * 4:si * 4 + nh], 1.0 / 128, ssb, rsb,
                             eps_ap=e2x[:, t:t + 1], eps_b=rx_b[t])

            def rope_s1a(ncol):
                nh = ncol // 128
                b, si = st["b"], st["si"]
                rsb = bf(f"rs{si}")
                for h in range(nh):
                    act.op(lambda e, h=h: e.activation(out=t1[:, h * 128:(h + 1) * 128], in_=bk(b)[:, h * 128:(h + 1) * 128],
                                                       func=AF.Copy, scale=rs[:, si * 4 + h:si * 4 + h + 1]),
                           reads=[bankb[b], rsb], writes=[tbb[1]])

            def rope_s1(ncol):
                nh = ncol // 128
                b, csl, si = st["b"], st["csl"], st["si"]
                csb, rsb = bf(f"cs{csl}"), bf(f"rs{si}")
                dve.op(lambda e: e.tensor_tensor(out=v3(t2[:, 0:ncol]), in0=v3(t1[:, 0:ncol]),
                                                 in1=cs[csl][:, 0, :].unsqueeze(1).broadcast_to([128, nh, 128]),
                                                 op=ALU.mult), reads=[tbb[1], csb], writes=[tbb[2]])
                for hf in range(2):
                    o_v = t0[:, 0:ncol].rearrange("p (h a two f) -> p h a two f", a=2, two=2, f=32)[:, :, :, hf, :]
                    i_v = t1[:, 0:ncol].rearrange("p (h a two f) -> p h a two f", a=2, two=2, f=32)[:, :, :, 1 - hf, :]
                    s_v = cs[csl][:, 1, :].rearrange("p (a two f) -> p a two f", a=2, two=2)[:, :, hf, :]
                    dve.op(lambda e, o_v=o_v, i_v=i_v, s_v=s_v: e.tensor_tensor(
                        out=o_v, in0=i_v, in1=s_v.unsqueeze(1).broadcast_to([128, nh, 2, 32]), op=ALU.mult),
                        reads=[tbb[1], csb], writes=[tbb[0]])
                dve.op(lambda e: e.tensor_tensor(out=t3[:, 0:ncol], in0=t2[:, 0:ncol], in1=t0[:, 0:ncol], op=ALU.add),
                       reads=[tbb[2], tbb[0]], writes=[tbb[3]])

            def mm():
                st["b"] = proj_mm(gi, t)

            if gname == "KV":
                def s0():
                    mm()
                    b = st["b"]
                    act.op(lambda e: e.activation(
                        out=Vst[:, t, 0:258].rearrange("p (h d) -> p h d", h=2)[:, :, 0:128],
                        in_=bk(b)[:, 256:512].rearrange("p (h d) -> p h d", h=2), func=AF.Copy, scale=rx[:, t:t + 1]),
                        reads=[bankb[b], rx_b[t]], writes=[Vst_b])
                    rope_s0(256, kwb, bf("kwb"), kwsb, bf("kwsb"))
                return [s0, lambda: rope_s0b(256), lambda: rope_s1a(256), lambda: rope_s1(256),
                        lambda: transposes_out(t3[:, 0:256], tbb[3], 2, KTst[:, :, t * 128:(t + 1) * 128], KTst_b,
                                               None, act, None, None, 4 + (t % 2))]
            if gname in ("QA", "QB"):
                h0 = 0 if gname == "QA" else 4

                def s0():
                    mm()
                    rope_s0(512, qwb, bf("qwb"), qwsb, bf("qwsb"))
                return [s0, lambda: rope_s0b(512), lambda: rope_s1a(512), lambda: rope_s1(512),
                        lambda: transposes_out(t3, tbb[3], 4, qst[s][:], qst_b[s], None, act,
                                               qT_s[l][t, :, h0:h0 + 4, :], qT_b[l][t], 4 + (t % 2))]
            if gname in ("GA0", "GA1"):
                cc = 0 if gname == "GA0" else 512

                def s0():
                    mm()
                    b = st["b"]
                    act.op(lambda e: e.activation(out=gst[s], in_=bk(b), func=AF.Silu, scale=rx[:, t:t + 1]),
                           reads=[bankb[b], rx_b[t]], writes=[gst_b[s]])
                    sp.dma(g_s[l][t * 128:(t + 1) * 128, cc:cc + 512], gst[s], gst_b[s],
                           reads=[gst_b[s]], writes=[g_b[l][t]])
                return [s0]
            if gname == "CI":
                def s0():
                    mm()
                    b = st["b"]
                    act.op(lambda e: e.activation(out=ia[:, t, :], in_=bk(b), func=AF.Copy, scale=rx[:, t:t + 1]),
                           reads=[bankb[b], rx_b[t]], writes=[ia_b[t]])
                return [s0]
            if gname == "CC":
                def s0():
                    mm()
                    b = st["b"]
                    dve.op(lambda e: e.scalar_tensor_tensor(out=ia[:, t, :], in0=bk(b), scalar=rx[:, t:t + 1],
                                                            in1=ia[:, t, :], op0=ALU.mult, op1=ALU.mult),
                           reads=[bankb[b], ia_b[t], rx_b[t]], writes=[ia_b[t]])
                return [s0]
            if gname == "CG":
                def s0():
                    mm()
                    b = st["b"]
                    act.op(lambda e: e.activation(out=ib16[:, t, :], in_=bk(b), func=AF.Silu, scale=rx[:, t:t + 1]),
                           reads=[bankb[b], rx_b[t]], writes=[ib_b[t]])
                return [s0]
            if gname == "SG":
                def s0():
                    mm()
                    b = st["b"]
                    act.op(lambda e: e.activation(out=ia16[:, t, :], in_=bk(b), func=AF.Silu, scale=rx[:, t:t + 1]),
                           reads=[bankb[b], rx_b[t]], writes=[ia16_b[t], ia_b[t // 2]])
                return [s0]
            if gname == "CB":
                def s0():
                    sp.dma(t0[1:127, :], ia[0:126, t, :], tbb[0], reads=[ia_b[t]], writes=[tbb[0]])
                    sp.dma(t0[127:128, :], ia[126:127, t, :], tbb[0], reads=[ia_b[t]], writes=[tbb[0]])
                    if t > 0:
                        sp.dma(t0[0:1, :], ia[127:128, t - 1, :], tbb[0], reads=[ia_b[t - 1]], writes=[tbb[0]])
                    else:
                        sp.dma(t0[0:1, :], hh[0:1, :], tbb[0], reads=[bf("hh")], writes=[tbb[0]])
                    sp.dma(t1[1:127, :], ia[2:128, t, :], tbb[1], reads=[ia_b[t]], writes=[tbb[1]])
                    sp.dma(t1[0:1, :], ia[1:2, t, :], tbb[1], reads=[ia_b[t]], writes=[tbb[1]])
                    if t < NT - 1:
                        sp.dma(t1[127:128, :], ia[0:1, t + 1, :], tbb[1], reads=[ia_b[t + 1]], writes=[tbb[1]])
                    else:
                        sp.dma(t1[127:128, :], hh[1:2, :], tbb[1], reads=[bf("hh")], writes=[tbb[1]])
                    mm()
                    pool.op(lambda e: e.tensor_tensor(out=t2, in0=ia[:, t, :], in1=cwb[:, 1, :], op=ALU.mult),
                            reads=[ia_b[t], bf("cwb")], writes=[tbb[2]])

                def s1():
                    b = st["b"]
                    dve.op(lambda e: e.tensor_tensor(out=t0, in0=t0, in1=cwb[:, 0, :], op=ALU.mult),
                           reads=[tbb[0], bf("cwb")], writes=[tbb[0]])
                    dve.op(lambda e: e.tensor_tensor(out=t1, in0=t1, in1=cwb[:, 2, :], op=ALU.mult),
                           reads=[tbb[1], bf("cwb")], writes=[tbb[1]])
                    dve.op(lambda e: e.tensor_tensor(out=t1, in0=t1, in1=t2, op=ALU.add),
                           reads=[tbb[1], tbb[2]], writes=[tbb[1]])
                    dve.op(lambda e: e.tensor_tensor(out=t0, in0=t0, in1=t1, op=ALU.add),
                           reads=[tbb[0], tbb[1]], writes=[tbb[0]])
                    dve.op(lambda e: e.scalar_tensor_tensor(out=t0, in0=bk(b), scalar=rx[:, t:t + 1], in1=t0,
                                                            op0=ALU.mult, op1=ALU.mult),
                           reads=[bankb[b], tbb[0], rx_b[t]], writes=[tbb[0]])

                def s1b():
                    si = newstat()
                    st["si"] = si
                    ssb, rsb = bf(f"ss{si}"), bf(f"rs{si}")
                    act.op(lambda e: e.activation(out=t1, in_=t0, func=AF.Square, accum_out=ss[:, si * 4:si * 4 + 1]),
                           reads=[tbb[0]], writes=[tbb[1], ssb])
                    rstd_from_ss(ss[:, si * 4:si * 4 + 1], rs[:, si * 4:si * 4 + 1], 1.0 / 512, ssb, rsb)

                def s2():
                    si = st["si"]
                    dve.op(lambda e: e.scalar_tensor_tensor(
                        out=t3, in0=t0, scalar=rs[:, si * 4:si * 4 + 1], in1=ib16[:, t, :],
                        op0=ALU.mult, op1=ALU.mult), reads=[tbb[0], bf(f"rs{si}"), ib_b[t]], writes=[tbb[3]])
                return [s0, s1, s1b, s2,
                        lambda: transposes_out(t3, tbb[3], 4, mst[s][:], mst_b[s], bwt[:, 8:12], dve,
                                               mT_s[l][t, :, 0:4, :], mT_b[l][t], 4 + (t % 2))]
            if gname == "SV":
                def s0():
                    mm()
                    b = st["b"]
                    act.op(lambda e: e.activation(out=t0, in_=bk(b), func=AF.Gelu, scale=rx[:, t:t + 1]),
                           reads=[bankb[b], rx_b[t]], writes=[tbb[0]])
                    act.op(lambda e: e.activation(out=t1, in_=t0, func=AF.Square),
                           reads=[tbb[0]], writes=[tbb[1]])

                def s0b():
                    si = newstat()
                    st["si"] = si
                    ssb, rsb = bf(f"ss{si}"), bf(f"rs{si}")
                    dve.op(lambda e: e.tensor_reduce(out=ss[:, si * 4:si * 4 + 4], in_=v3(t1), axis=AX.X, op=ALU.add),
                           reads=[tbb[1]], writes=[ssb])
                    rstd_from_ss(ss[:, si * 4:si * 4 + 4], rs[:, si * 4:si * 4 + 4], 1.0 / 128, ssb, rsb)

                def s1():
                    si = st["si"]
                    dve.op(lambda e: e.tensor_tensor(
                        out=v3(t2), in0=v3(t0),
                        in1=rs[:, si * 4:si * 4 + 4].unsqueeze(2).broadcast_to([128, 4, 128]), op=ALU.mult),
                        reads=[tbb[0], bf(f"rs{si}")], writes=[tbb[2]])

                def s2():
                    dve.op(lambda e: e.tensor_tensor(out=ib16[:, t, :], in0=t2, in1=snwb[:], op=ALU.mult),
                           reads=[tbb[2], bf("snwb")], writes=[ib_b[t]])
                return [s0, s0b, s1, s2]
            if gname == "SU":
                sbk = 6 + (t % 2)

                def s0():
                    for g in range(4):
                        pe.op(lambda e, g=g: e.matmul(bk(sbk)[:, g * 128:(g + 1) * 128], lhsT=wst[:, g, :],
                                                      rhs=ib16[:, t, g * 128:(g + 1) * 128], start=True, stop=True),
                              reads=[bf("wst"), ib_b[t]], writes=[bankb[sbk]])
                    mm()
                    b = st["b"]
                    act.op(lambda e: e.activation(out=t0, in_=bk(b), func=AF.Gelu, scale=rx[:, t:t + 1]),
                           reads=[bankb[b], rx_b[t]], writes=[tbb[0]])

                def s1():
                    for g in range(4):
                        dve.op(lambda e, g=g: e.scalar_tensor_tensor(
                            out=t1[:, g * 128:(g + 1) * 128], in0=bk(sbk)[:, g * 128:(g + 1) * 128],
                            scalar=sbt[:, g:g + 1], in1=t0[:, g * 128:(g + 1) * 128], op0=ALU.add, op1=ALU.mult),
                            reads=[bankb[sbk], bf("sbt"), tbb[0]], writes=[tbb[1]])

                def s1b():
                    si = newstat()
                    st["si"] = si
                    ssb, rsb = bf(f"ss{si}"), bf(f"rs{si}")
                    act.op(lambda e: e.activation(out=t2, in_=t1, func=AF.Square, accum_out=ss[:, si * 4:si * 4 + 1]),
                           reads=[tbb[1]], writes=[tbb[2], ssb])
                    rstd_from_ss(ss[:, si * 4:si * 4 + 1], rs[:, si * 4:si * 4 + 1], 1.0 / 512, ssb, rsb)

                def s2():
                    si = st["si"]
                    dve.op(lambda e: e.scalar_tensor_tensor(
                        out=t3, in0=t1, scalar=rs[:, si * 4:si * 4 + 1], in1=ia16[:, t, :],
                        op0=ALU.mult, op1=ALU.mult), reads=[tbb[1], bf(f"rs{si}"), ia16_b[t]], writes=[tbb[3]])
                return [s0, s1, s1b, s2,
                        lambda: transposes_out(t3, tbb[3], 4, mst[s][:], mst_b[s], bwt[:, 12:16], dve,
                                               mT_s[l][t, :, 4:8, :], mT_b[l][t], 4 + (t % 2))]
            raise AssertionError(gname)

        def olds_for(lo, hi):
            return [b_ for (a_, z_, b_) in regionsA if a_ < hi and lo < z_]

        pv = lambda r_: (r_ + 2) % NR
        KTv_b = [[None] * NR for _ in range(2)]
        Vv_b = [None] * NR

        def load_kt(h, r_):
            lo = (h * S + r_ * TOK) * 2
            KTv_b[h][r_] = alias(f"KT{l}_{h}_{r_}", olds_for(lo, lo + 2 * TOK))
            sp.dma(KT[:, h, r_ * TOK:(r_ + 1) * TOK], kt_dst[l][(r_ * 2 + h) * 128:(r_ * 2 + h + 1) * 128, :],
                   KTv_b[h][r_], reads=[ktdst_b[l]], writes=[KTv_b[h][r_]])

        def load_v(r_):
            p_ = pv(r_)
            lo = (2 * S + p_ * NT * DV) * 2
            Vv_b[p_] = alias(f"Vv{l}_{p_}", olds_for(lo, lo + NT * DV * 2))
            for i_ in range(2):
                o_ = 2 * S + p_ * NT * DV + i_ * NH * DV
                sp.dma(R2[:, o_:o_ + NH * DV], v_dst[l][i_][r_ * 128:(r_ + 1) * 128, :],
                       Vv_b[p_], reads=[vdst_b[l][i_]], writes=[Vv_b[p_]])

        def group_post(gi, gname):
            if gname == "CB":
                for r_ in range(NR):
                    load_kt(1, r_)
                load_v(0)
                load_v(1)
            if gname == "KV":
                sp.dma(kt_src[l].rearrange("(h d) t -> d h t", h=2), KTst, KTst_b, reads=[KTst_b], writes=[ktsrc_b[l]])
                for i_ in range(2):
                    o_ = 24576 + 2 * TOK + i_ * NH * DV
                    sp.dma(v_src[l][i_][:, :], R2[:, o_:o_ + NH * DV], Vst_b, reads=[Vst_b], writes=[vsrc_b[l][i_]])
            if gname == "KV":
                pool.collective(groups, kt_src[l].opt(), kt_dst[l].opt(), [ktsrc_b[l]], [ktdst_b[l]], f"kt{l}")
                for i_ in range(2):
                    pool.collective(groups, v_src[l][i_].opt(), v_dst[l][i_].opt(), [vsrc_b[l][i_]], [vdst_b[l][i_]], f"v{l}_{i_}")

        items = [(gi, gname, t) for gi, (gname, c0) in enumerate(GROUPS) for t in range(NT)]

        def make_item(idx):
            gi, gname, t = items[idx]
            stages = list(make_stages(t, gi, gname))
            if t == 0:
                f0 = stages[0]
                stages[0] = (lambda f0=f0, gi=gi, gname=gname: (group_pre(gi, gname), f0()))
            if t == NT - 1:
                fl = stages[-1]
                stages[-1] = (lambda fl=fl, gi=gi, gname=gname: (fl(), group_post(gi, gname)))
            return stages

        run_pipeline(len(items), make_item)

        for s_ in range(2):
            B[f"xt{s_}"] = alias(f"xtB{l}_{s_}", tA_b[2 + s_])
        R1_old = hT_b
        R2_old = ia_b + ib_b + ia16_b + [KTst_b, Vst_b]
        R3_old = wring_b + [x for s_ in tA_b[0:2] for x in s_] + gst_b + qst_b + mst_b
        wout_b = [alias(f"wout{l}_{q}", R1_old) for q in range(4)]
        qTt_b = [alias(f"qTt{l}_{i}", R3_old) for i in range(2)]
        gt_b = [alias(f"gt{l}_{i}", R3_old) for i in range(2)]
        mTt_b = [alias(f"mTt{l}_{i}", R3_old) for i in range(2)]
        PT_b = [alias(f"PT{l}_{i}", R3_old) for i in range(3)]
        ao_b = [alias(f"ao{l}_{i}", R3_old) for i in range(2)]
        mTa_b = [alias(f"mTa{l}_{i}", R3_old) for i in range(2)]
        junkB_b = alias(f"junkB{l}", R3_old)
        fwb_b = alias(f"fwb{l}", R3_old)

        def load_wout(half):
            for n in (2 * half, 2 * half + 1):
                for hf in range(2):
                    src = w_out[l, hf * 1024:(hf + 1) * 1024, n * 512:(n + 1) * 512].rearrange("(k p) c -> p k c", p=128)
                    pool.dma(R1[:, hf * 8:(hf + 1) * 8, n * 512:(n + 1) * 512], src, wout_b[n], writes=[wout_b[n]])

        KV_REST = True
        last = (l == DEPTH - 1)
        if last:
            sp.dma(fwb, pbc(fw), fwb_b, writes=[fwb_b])

        def loads_B(t):
            sl = t % 2
            sp.dma(qTt[sl][:], qT_s[l][t], qTt_b[sl], reads=[qT_b[l][t]], writes=[qTt_b[sl]])
            sp.dma(gt[sl], g_s[l][t * 128:(t + 1) * 128, :], gt_b[sl], reads=[g_b[l][t]], writes=[gt_b[sl]])
            sp.dma(mTt[sl][:], mT_s[l][t], mTt_b[sl], reads=[mT_b[l][t]], writes=[mTt_b[sl]])
            xb = bf(f"xt{sl}")
            sp.dma(xt[sl][:], xs_ap[t * 128:(t + 1) * 128, :], xb, reads=[xsb[t]], writes=[xb])

        def post1(t):
            sl = t % 2
            a = ao[sl]
            si = newstat()
            ssb, rsb = bf(f"ss{si}"), bf(f"rs{si}")
            dve.op(lambda e: e.scalar_tensor_tensor(out=junkB[:, 0:1024], in0=a, scalar=1.0, in1=a,
                                                    op0=ALU.mult, op1=ALU.mult, accum_out=ss[:, si * 4:si * 4 + 1]),
                   reads=[ao_b[sl]], writes=[junkB_b, ssb])
            rstd_from_ss(ss[:, si * 4:si * 4 + 1], rs[:, si * 4:si * 4 + 1], 1.0 / 1024, ssb, rsb)
            dve.op(lambda e: e.scalar_tensor_tensor(out=a, in0=a, scalar=rs[:, si * 4:si * 4 + 1], in1=gt[sl],
                                                    op0=ALU.mult, op1=ALU.mult),
                   reads=[ao_b[sl], rsb, gt_b[sl]], writes=[ao_b[sl]])
            for j in range(2):
                pb = 6 + j
                for i in range(4):
                    k = 4 * j + i
                    pe.op(lambda e, k=k, i=i, pb=pb: e.transpose(out=bk(pb)[:, i * 128:(i + 1) * 128],
                                                                 in_=a[:, k * 128:(k + 1) * 128], identity=ident[:]),
                          reads=[ao_b[sl], bf("ident")], writes=[bankb[pb]])
                dve.op(lambda e, j=j, pb=pb: e.tensor_tensor(
                    out=mTa[sl][:, 4 * j:4 * j + 4, :], in0=bk(pb).rearrange("p (i t) -> p i t", i=4),
                    in1=bwt[:, 4 * j:4 * j + 4].unsqueeze(2).broadcast_to([128, 4, 128]), op=ALU.mult),
                    reads=[bankb[pb], bf("bwt")], writes=[mTa_b[sl]])

        def post2(t):
            sl = t % 2
            xb = bf(f"xt{sl}")
            for n in range(4):
                pb = 6 + (n % 2)
                for k in range(16):
                    lhsT = mTa[sl][:, k, :] if k < 8 else mTt[sl][:, k - 8, :]
                    lb = mTa_b[sl] if k < 8 else mTt_b[sl]
                    pe.op(lambda e, k=k, n=n, pb=pb, lhsT=lhsT: e.matmul(
                        bk(pb), lhsT=lhsT, rhs=R1[:, k, n * 512:(n + 1) * 512], start=(k == 0), stop=(k == 15)),
                        reads=[lb, wout_b[n]], writes=[bankb[pb]])
                dve.op(lambda e, n=n, pb=pb: e.tensor_tensor(out=xt[sl][:, n * 512:(n + 1) * 512], in0=bk(pb),
                                                             in1=xt[sl][:, n * 512:(n + 1) * 512], op=ALU.add),
                       reads=[bankb[pb], xb], writes=[xb])
            if not last:
                sp.dma(x1[t * 128:(t + 1) * 128, :], xt[sl][:], xb, reads=[xb], writes=[x_src_b[l + 1][t]])
            else:
                si = newstat()
                ssb, rsb = bf(f"ss{si}"), bf(f"rs{si}")
                dve.op(lambda e: e.scalar_tensor_tensor(out=junkB, in0=xt[sl][:], scalar=1.0, in1=xt[sl][:],
                                                        op0=ALU.mult, op1=ALU.mult, accum_out=ss[:, si * 4:si * 4 + 1]),
                       reads=[xb], writes=[junkB_b, ssb])
                rstd_from_ss(ss[:, si * 4:si * 4 + 1], rs[:, si * 4:si * 4 + 1], 1.0 / D, ssb, rsb)
                dve.op(lambda e: e.scalar_tensor_tensor(out=xt[sl][:], in0=xt[sl][:], scalar=rs[:, si * 4:si * 4 + 1],
                                                        in1=fwb, op0=ALU.mult, op1=ALU.mult),
                       reads=[xb, rsb, fwb_b], writes=[xb])
                sp.dma(y_out[t * 128:(t + 1) * 128, :], xt[sl][:], xb, reads=[xb], writes=[y_b])

        NC2 = NKT // 2
        steps = [(t, g, c) for t in range(NT) for g in (1, 0) for c in range(NC2)]
        scale = float(HD) ** -0.5

        def emit_qk(i):
            t, g, c = steps[i]
            sl = t % 2
            sb_ = i % 2
            for j in range(2):
                kt_i = c * 2 + j
                pe.op(lambda e, j=j, kt_i=kt_i, g=g, sl=sl, sb_=sb_: e.matmul(
                    pp[sb_][:, j * 512:(j + 1) * 512], lhsT=KT[:, g, kt_i * 128:(kt_i + 1) * 128],
                    rhs=qTt[sl][:, 4 * g:4 * g + 4, :].rearrange("p h t -> p (h t)"), start=True, stop=True),
                    reads=[KTv_b[g][kt_i // NT], qTt_b[sl]], writes=[bankb[2 * sb_ + j]])

        def emit_exp(i):
            sb_ = i % 2
            p = i % 3
            act.op(lambda e: e.activation(out=PT[p], in_=pp[sb_][:, :], func=AF.Exp, scale=scale, bias=negc[:, 0:1]),
                   reads=[bankb[2 * sb_], bankb[2 * sb_ + 1], bf("negc")], writes=[PT_b[p]])

        def emit_pv(i):
            t, g, c = steps[i]
            p = i % 3
            for j in range(2):
                kt_i = c * 2 + j
                for h4 in range(4):
                    ob = 4 + h4 // 2
                    c0_ = (h4 % 2) * 256
                    pe.op(lambda e, j=j, kt_i=kt_i, h4=h4, ob=ob, c0_=c0_: e.matmul(
                        bk(ob)[:, c0_:c0_ + 129], lhsT=PT[p][:, j * 512 + h4 * 128:j * 512 + (h4 + 1) * 128],
                        rhs=Vaug[:, pv(kt_i // NT) * NT + kt_i % NT, 128 * g:128 * g + 129],
                        start=(c == 0 and j == 0 and h4 % 2 == 0), stop=(c == NC2 - 1 and j == 1),
                        skip_group_check=True),
                        reads=[PT_b[p], Vv_b[pv(kt_i // NT)]], writes=[bankb[ob]])
            if c == NC2 - 1:
                sl = t % 2
                rb = bf("rden")
                for h4 in range(4):
                    ob = 4 + h4 // 2
                    c0_ = (h4 % 2) * 256
                    h = 4 * g + h4
                    dcol = c0_ + (128 if g == 0 else 0)
                    vcol = c0_ + (0 if g == 0 else 1)
                    dve.op(lambda e, ob=ob, dcol=dcol, h4=h4: e.reciprocal(out=rs[:, 41 + (h4 % 2):42 + (h4 % 2)],
                                                                         in_=bk(ob)[:, dcol:dcol + 1]),
                           reads=[bankb[ob]], writes=[rb])
                    dve.op(lambda e, ob=ob, vcol=vcol, h=h, h4=h4: e.tensor_scalar(
                        out=ao[sl][:, h * 128:(h + 1) * 128], in0=bk(ob)[:, vcol:vcol + 128],
                        scalar1=rs[:, 41 + (h4 % 2):42 + (h4 % 2)], scalar2=None,
                        op0=ALU.mult), reads=[bankb[ob], rb], writes=[ao_b[sl]])

        loads_B(0)
        load_v(2)
        load_v(3)
        for r_ in range(NR):
            load_kt(0, r_)
        emit_qk(0)
        if len(steps) > 1:
            emit_qk(1)
        for i, (t, g, c) in enumerate(steps):
            emit_exp(i)
            if i + 2 < len(steps):
                emit_qk(i + 2)
            emit_pv(i)
            st_ = i % (2 * NC2)
            if t == 0 and st_ == min(8, 2 * NC2 - 4):
                load_wout(0)
            if t == 0 and st_ == min(24, 2 * NC2 - 3):
                load_wout(1)
            if st_ == 1 and t > 0:
                post1(t - 1)
            if st_ == 3 and t > 0:
                post2(t - 1)
            if st_ == 4 and t + 1 < NT:
                loads_B(t + 1)
        post1(NT - 1)
        post2(NT - 1)

        R1_old = wout_b
        R2_old = [b_ for row in KTv_b for b_ in row] + list(Vv_b)
        R3_old = qTt_b + gt_b + mTt_b + PT_b + ao_b + mTa_b + [junkB_b, fwb_b]

    for tok in y_b.w.values():
        sp.wait(tok)

    with nc.Block() as block:
        @block.tensor
        def _(e):
            for f in pe.prog:
                f(e)

        @block.scalar
        def _(e):
            for f in act.prog:
                f(e)

        @block.vector
        def _(e):
            for f in dve.prog:
                f(e)

        @block.gpsimd
        def _(e):
            for f in pool.prog:
                f(e)

        @block.sync
        def _(e):
            for f in sp.prog:
                f(e)
    return nc, es


def rope_tables(S):
    GRID_W = 64
    rows = S // GRID_W
    row = np.repeat(np.arange(rows, dtype=np.float32), GRID_W)
    col = np.tile(np.arange(GRID_W, dtype=np.float32), rows)
    inv_freq = (10000.0 ** (-np.arange(0, 64, 2, dtype=np.float32) / 64)).astype(np.float32)
    ang = np.stack([row, col], axis=-1)[:, :, None] * inv_freq
    ang = np.broadcast_to(ang[:, :, None, :], (S, 2, 2, 32)).reshape(S, 128)
    cos = np.cos(ang).astype(np.float32)
    sin = np.sin(ang).astype(np.float32).reshape(S, 2, 2, 32).copy()
    sin[:, :, 0, :] *= -1.0
    return cos, sin.reshape(S, 128)


def swap_halves(w, DEPTH):
    w = np.asarray(w, dtype=np.float32).reshape(DEPTH, 2, 2, 32)
    return np.ascontiguousarray(w[:, :, ::-1, :]).reshape(DEPTH, 1, 128)


def host_consts():
    ident = np.eye(128, dtype=np.float32)
    shf = np.zeros((128, 4, 128), np.float32)
    for t in range(128):
        if t - 1 >= 0:
            shf[t - 1, 0, t] = 1.0
        if t + 1 < 128:
            shf[t + 1, 1, t] = 1.0
    shf[127, 2, 0] = 1.0
    shf[0, 3, 127] = 1.0
    e2 = np.zeros((2, 2, 128), np.float32)
    e2[0, 0, 0] = 1.0
    e2[1, 1, 127] = 1.0
    return ident, shf, e2


_CACHE = {}


def run(inputs, TOK, DEPTH=2):
    x = np.ascontiguousarray(inputs["x"], dtype=np.float32)
    Bsz, S, _ = x.shape
    assert Bsz == 2 and S == NR * TOK
    key = (TOK, DEPTH)
    if key not in _CACHE:
        _CACHE[key] = build_program(TOK, DEPTH)
    nc, es = _CACHE[key]
    f32 = lambda a: np.ascontiguousarray(np.asarray(a, dtype=np.float32))
    cos, sin = rope_tables(S)
    ident, shf, e2 = host_consts()
    pp_layout = lambda w: f32(np.asarray(w).reshape(DEPTH, 16, 128).transpose(0, 2, 1))
    common = {
        "w_in": f32(inputs["w_in"]), "w_out": f32(inputs["w_out"]),
        "nw_pp": pp_layout(inputs["norm_w"]), "bw_pp": pp_layout(inputs["branch_norm_w"]),
        "qw": f32(inputs["q_norm_w"]).reshape(DEPTH, 1, 128), "kw": f32(inputs["k_norm_w"]).reshape(DEPTH, 1, 128),
        "qws": swap_halves(inputs["q_norm_w"], DEPTH), "kws": swap_halves(inputs["k_norm_w"], DEPTH),
        "cw": f32(np.asarray(inputs["conv_w"]).transpose(0, 2, 1)),
        "snw": f32(inputs["sgu_norm_w"]).reshape(DEPTH, 1, 512),
        "sb_pp": f32(np.asarray(inputs["sgu_b"]).transpose(0, 2, 1)),
        "wsT": f32(np.asarray(inputs["sgu_w"]).transpose(0, 3, 1, 2)),
        "fw": f32(inputs["final_norm_w"]).reshape(1, D),
        "ident": ident,
    }
    in_maps = []
    for c in range(NCORES):
        b, r = c // NR, c % NR
        sel = np.zeros((8, 2), np.float32)
        if r > 0:
            sel[(r - 1) * 2 + 1, 0] = 1.0
        if r < NR - 1:
            sel[(r + 1) * 2 + 0, 1] = 1.0
        m = dict(common)
        m["x"] = np.ascontiguousarray(x[b, r * TOK:(r + 1) * TOK, :])
        m["cos_t"] = np.ascontiguousarray(cos[r * TOK:(r + 1) * TOK])
        m["sin_t"] = np.ascontiguousarray(sin[r * TOK:(r + 1) * TOK])
        m["sel"] = sel
        in_maps.append(m)
    res = run_bass_kernel_spmd(nc, in_maps, core_ids=list(range(NCORES)))
    out = np.empty((Bsz, S, D), np.float32)
    for c in range(NCORES):
        b, r = c // NR, c % NR
        out[b, r * TOK:(r + 1) * TOK, :] = np.asarray(res.results[c]["y"], dtype=np.float32)
    return out


def kernel(**inputs):
    S = np.asarray(inputs["x"]).shape[1]
    return run(inputs, S // NR, DEPTH=2)
```

```python
import re
import numpy as np
from contextlib import ExitStack
import concourse.bass as bass
import concourse.mybir as mybir
from concourse.bass_utils import run_bass_kernel_spmd

F32 = mybir.dt.float32
BF16 = mybir.dt.bfloat16
AF = mybir.ActivationFunctionType
ALU = mybir.AluOpType
AX = mybir.AxisListType

D = 2048
INW = 6144
HD = 128
EPS = 1e-6
NCORES = 8
NR = 4


class Tok:
    __slots__ = ("sem", "val", "key")

    def __init__(self, sem, val, key):
        self.sem, self.val, self.key = sem, val, key


def _add(d, tok):
    o = d.get(tok.key)
    if o is None or o.val < tok.val:
        d[tok.key] = tok


class Buf:
    def __init__(self, name, sem_key=None):
        self.name = name
        self.w = {}
        self.r = {}
        self.sem_key = sem_key or name


def alias(name, olds):
    b = Buf(name, sem_key=re.sub(r"^([A-Za-z]+)\d+", r"\1", name))
    for o in olds:
        for t in list(o.w.values()) + list(o.r.values()):
            _add(b.r, t)
    return b


class Eng:
    def __init__(self, K, name, is_pe=False):
        self.K = K
        self.name = name
        self.key = "e_" + name
        self.sem = K.newsem("p_" + name)
        self.cnt = 0
        self.waited = {}
        self.is_pe = is_pe
        self.prog = []

    def wait(self, tok):
        if self.waited.get(tok.key, 0) >= tok.val:
            return
        self.waited[tok.key] = tok.val
        sem, val = tok.sem, tok.val
        self.prog.append(lambda e: e.wait_ge(sem, val))

    def deps(self, reads, writes, extra, is_dma=False):
        for b in reads:
            for tok in b.w.values():
                if tok.key == self.key and self.is_pe:
                    continue
                self.wait(tok)
        for b in writes:
            for tok in b.w.values():
                if tok.key == self.key:
                    continue
                if is_dma and tok.key.startswith("d_"):
                    continue
                self.wait(tok)
            for tok in b.r.values():
                if tok.key == self.key:
                    continue
                self.wait(tok)
        for tok in extra:
            self.wait(tok)

    def op(self, fn, reads=(), writes=(), extra=()):
        self.deps(reads, writes, extra)
        self.cnt += 1
        sem = self.sem
        self.prog.append(lambda e: fn(e).then_inc(sem, 1))
        tok = Tok(sem, self.cnt, self.key)
        for b in writes:
            b.w = {tok.key: tok}
            b.r = {}
        for b in reads:
            if b not in writes:
                _add(b.r, tok)
        return tok

    def dma(self, out, in_, sem_buf, reads=(), writes=(), extra=()):
        self.deps(reads, writes, extra, is_dma=True)
        state = self.K.dsems.setdefault(sem_buf.sem_key, [None, 0])
        if state[0] is None:
            state[0] = self.K.newsem("d_" + sem_buf.sem_key)
        state[1] += 16
        sem = state[0]
        self.prog.append(lambda e: e.dma_start(out=out, in_=in_).then_inc(sem, 16))
        tok = Tok(sem, state[1], "d_" + sem_buf.sem_key)
        for b in writes:
            if b.r:
                b.w = {}
                b.r = {}
            _add(b.w, tok)
        for b in reads:
            _add(b.r, tok)
        return tok

    def collective(self, groups, src, dst, reads, writes, name):
        self.deps(reads, writes, ())
        sem = self.K.newsem("cc_" + name)
        self.prog.append(
            lambda e: e.collective_compute("AllGather", ALU.bypass, replica_groups=groups,
                                           ins=[src], outs=[dst], dma_qos="P3").then_inc(sem))
        tok = Tok(sem, 1, "cc_" + name)
        for b in writes:
            b.w = {tok.key: tok}
            b.r = {}
        for b in reads:
            _add(b.r, tok)
        return tok


class Kern:
    def __init__(self, nc, es):
        self.nc = nc
        self.es = es
        self.nsem = 0
        self.dsems = {}
        self.pe = Eng(self, "pe", is_pe=True)
        self.act = Eng(self, "act")
        self.dve = Eng(self, "dve")
        self.pool = Eng(self, "pool")
        self.sp = Eng(self, "sp")

    def newsem(self, name):
        self.nsem += 1
        return self.es.enter_context(self.nc.semaphore(f"{name}_{self.nsem}"))

    def sb(self, name, shape, dt):
        return self.es.enter_context(self.nc.sbuf_tensor(name, list(shape), dt))

    def ps(self, name, shape, dt):
        return self.es.enter_context(self.nc.psum_tensor(name, list(shape), dt))


def build_program(TOK, DEPTH=2, dbg=False):
    NT = TOK // 128
    S = NR * TOK
    NKT = S // 128
    KCH = 8 if NKT >= 8 else NKT
    NCH = NKT // KCH
    DV = 258

    nc = bass.Bass("TRN2", target_bir_lowering=False, dynamic_dma_scratch_size=8192)
    es = ExitStack()
    K = Kern(nc, es)
    pe, act, dve, pool, sp = K.pe, K.act, K.dve, K.pool, K.sp

    def din(name, shape, dt=F32):
        return nc.dram_tensor(name, list(shape), dt, kind="ExternalInput").ap()

    def dint(name, shape, dt):
        return nc.dram_tensor(name, list(shape), dt, kind="Internal").ap()

    x_in = din("x", [TOK, D])
    w_in = din("w_in", [DEPTH, D, INW])
    w_out = din("w_out", [DEPTH, D, D])
    nw_pp = din("nw_pp", [DEPTH, 128, 16])
    bw_pp = din("bw_pp", [DEPTH, 128, 16])
    qw = din("qw", [DEPTH, 1, 128])
    kw = din("kw", [DEPTH, 1, 128])
    qws = din("qws", [DEPTH, 1, 128])
    kws = din("kws", [DEPTH, 1, 128])
    cw = din("cw", [DEPTH, 3, 512])
    snw = din("snw", [DEPTH, 1, 512])
    sb_pp = din("sb_pp", [DEPTH, 128, 4])
    wsT = din("wsT", [DEPTH, 128, 4, 128])
    fw = din("fw", [1, D])
    cos_t = din("cos_t", [TOK, 128])
    sin_t = din("sin_t", [TOK, 128])
    sel = din("sel", [8, 2])
    ident_d = din("ident", [128, 128])
    y_out = nc.dram_tensor("y", [TOK, D], F32, kind="ExternalOutput").ap()

    x1 = dint("x1", [TOK, D], F32)
    qT_s = [dint(f"qT_s{l}", [NT, 128, 8, 128], BF16) for l in range(DEPTH)]
    g_s = [dint(f"g_s{l}", [TOK, 1024], BF16) for l in range(DEPTH)]
    mT_s = [dint(f"mT_s{l}", [NT, 128, 8, 128], BF16) for l in range(DEPTH)]
    kt_src = [dint(f"kt_src{l}", [256, TOK], BF16) for l in range(DEPTH)]
    kt_dst = [dint(f"kt_dst{l}", [NR * 256, TOK], BF16) for l in range(DEPTH)]
    NH = NT // 2
    v_src = [[dint(f"v_src{l}_{i}", [128, NH * DV], BF16) for i in range(2)] for l in range(DEPTH)]
    v_dst = [[dint(f"v_dst{l}_{i}", [NR * 128, NH * DV], BF16) for i in range(2)] for l in range(DEPTH)]
    xe_src = [dint(f"xe_src{l}", [2, D], F32) for l in range(DEPTH)]
    xe_dst = [dint(f"xe_dst{l}", [NR * 2, D], F32) for l in range(DEPTH)]
    groups = [[0, 1, 2, 3], [4, 5, 6, 7]]

    x_src_b = [[Buf(f"xs{l}_{t}") for t in range(NT)] for l in range(DEPTH + 1)]
    qT_b = [[Buf(f"qTb{l}_{t}") for t in range(NT)] for l in range(DEPTH)]
    g_b = [[Buf(f"gb{l}_{t}") for t in range(NT)] for l in range(DEPTH)]
    mT_b = [[Buf(f"mTb{l}_{t}") for t in range(NT)] for l in range(DEPTH)]
    ktsrc_b = [Buf(f"ktsrc{l}") for l in range(DEPTH)]
    ktdst_b = [Buf(f"ktdst{l}") for l in range(DEPTH)]
    vsrc_b = [[Buf(f"vsrc{l}_{i}") for i in range(2)] for l in range(DEPTH)]
    vdst_b = [[Buf(f"vdst{l}_{i}") for i in range(2)] for l in range(DEPTH)]
    xesrc_b = [Buf(f"xesrc{l}") for l in range(DEPTH)]
    xedst_b = [Buf(f"xedst{l}") for l in range(DEPTH)]
    y_b = Buf("y")

    R1 = K.sb("R1", [128, 16, 2048], BF16)
    R2N = max(32768 + 0, 2 * S + NKT * DV)
    R2N = max(R2N, 16384 + 8192 + 2 * TOK + NT * DV)
    R2 = K.sb("R2", [128, R2N], BF16)
    R3N = 27 * 1024
    R3 = K.sb("R3", [128, R3N], BF16)
    xt = [K.sb(f"xt{i}", [128, D], F32) for i in range(2)]
    cs = [K.sb(f"cs{i}", [128, 2, 128], F32) for i in range(3)]
    ident = K.sb("ident_sb", [128, 128], F32)
    selt = K.sb("selt", [8, 2], F32)
    xg = xt[1][0:8, :]
    xh = xt[0][0:2, :]
    hTh = K.sb("hTh", [128, 16, 2], BF16)
    hh = K.sb("hh", [2, 512], F32)
    hh_ci = hh
    nwt = K.sb("nwt", [128, 16], F32)
    bwt = K.sb("bwt", [128, 16], F32)
    qwb = K.sb("qwb", [128, 128], F32)
    kwb = K.sb("kwb", [128, 128], F32)
    qwsb = K.sb("qwsb", [128, 128], F32)
    kwsb = K.sb("kwsb", [128, 128], F32)
    cwb = K.sb("cwb", [128, 3, 512], F32)
    snwb = K.sb("snwb", [128, 512], F32)
    sbt = K.sb("sbt", [128, 4], F32)
    wst = K.sb("wst", [128, 4, 128], BF16)
    neghalf = K.sb("neghalf", [128, 4], F32)
    ss = K.sb("ss", [128, 44], F32)
    rs = K.sb("rs", [128, 44], F32)
    rx = K.sb("rx", [128, 16], F32)
    e2x = K.sb("e2x", [128, 16], F32)
    negc = K.sb("negc", [128, 4], F32)
    stat_i = [0]

    def r3(off, n):
        return R3[:, off:off + n]
    wring = [r3(i * 8192, 8192).rearrange("p (k n) -> p k n", k=16) for i in range(2)]
    wring += [R2[:, i * 8192:(i + 1) * 8192].rearrange("p (k n) -> p k n", k=16) for i in range(2)]
    tA = [[r3(16384 + (s * 4 + i) * 1024, 1024).bitcast(F32) for i in range(4)] for s in range(2)]
    tA += [[xt[s][:, i * 512:(i + 1) * 512] for i in range(4)] for s in range(2)]
    gst = [r3(24576 + i * 512, 512) for i in range(2)]
    qst = [r3(25600 + i * 512, 512).rearrange("p (h t) -> p h t", h=4) for i in range(2)]
    mst = [r3(26624 + i * 512, 512).rearrange("p (h t) -> p h t", h=4) for i in range(2)]
    qTt = [r3(i * 1024, 1024).rearrange("p (h t) -> p h t", h=8) for i in range(2)]
    gt = [r3(2048 + i * 1024, 1024) for i in range(2)]
    mTt = [r3(4096 + i * 1024, 1024).rearrange("p (h t) -> p h t", h=8) for i in range(2)]
    PT = [r3(6144 + i * 1024, 1024) for i in range(3)]
    ao = [r3(9216 + i * 2048, 2048).bitcast(F32) for i in range(2)]
    mTa = [r3(13312 + i * 1024, 1024).rearrange("p (h t) -> p h t", h=8) for i in range(2)]
    junkB = r3(15360, 2048)
    fwb = r3(17408, 4096).bitcast(F32)
    ia = R2[:, 0:16384].bitcast(F32).rearrange("p (t c) -> p t c", c=512)
    ia16 = R2[:, 0:8192].rearrange("p (t c) -> p t c", c=512)
    ib16 = R2[:, 16384:24576].rearrange("p (t c) -> p t c", c=512)
    junkA = R2[:, 16384:16384 + 2048]
    KTst = R2[:, 24576:24576 + 2 * TOK].rearrange("p (h t) -> p h t", h=2)
    Vst = R2[:, 24576 + 2 * TOK:24576 + 2 * TOK + NT * DV].rearrange("p (t c) -> p t c", c=DV)
    KT = R2[:, 0:2 * S].rearrange("p (h s) -> p h s", h=2)
    Vaug = R2[:, 2 * S:2 * S + NKT * DV].rearrange("p (k d) -> p k d", d=DV)

    pp = [K.ps(f"pp{i}", [128, 1024], F32) for i in range(4)]
    bankb = [Buf(f"bank{i}") for i in range(8)]

    def bk(i):
        return pp[i // 2][:, (i % 2) * 512:(i % 2 + 1) * 512]

    B = {}

    def bf(name):
        if name not in B:
            B[name] = Buf(name)
        return B[name]

    def newstat():
        i = stat_i[0] % 10
        stat_i[0] += 1
        return i

    def pbc(ap):
        return ap.partition_broadcast(128).rearrange("p o n -> p (o n)")

    def rstd_from_ss(ss_ap, rs_ap, inv_n, ssb, rsb, eps_ap=None, eps_b=None):
        n = ss_ap.shape[1]
        epsv = EPS if eps_ap is None else eps_ap
        pool.op(lambda e: e.tensor_scalar(out=rs_ap, in0=ss_ap, scalar1=inv_n, scalar2=epsv,
                                          op0=ALU.mult, op1=ALU.add), reads=[ssb] + ([eps_b] if eps_b else []), writes=[rsb])
        pool.op(lambda e: e.tensor_tensor(out=rs_ap, in0=rs_ap, in1=neghalf[:, 0:n], op=ALU.pow),
                reads=[rsb, bf("neghalf")], writes=[rsb])

    sp.dma(ident[:], ident_d[:, :], bf("ident"), writes=[bf("ident")])
    sp.dma(selt[:], sel[:, :], bf("selt"), writes=[bf("selt")])
    dve.op(lambda e: e.memset(neghalf[:], -0.5), writes=[bf("neghalf")])

    wring_b = [Buf("wring0"), Buf("wring1"), None, None]
    R1_old = []
    R2_old = []
    R3_old = []

    GROUPS = [("KV", 1024), ("QA", 0), ("QB", 512), ("GA0", 1536), ("GA1", 2048),
              ("CI", 2560), ("CC", 3584), ("CG", 4096), ("CB", 3072),
              ("SV", 5120), ("SG", 5632), ("SU", 4608)]

    def slot_of(gi):
        return gi if gi < 4 else gi % 2

    def load_wgroup(l, gi):
        slot = slot_of(gi)
        c0 = GROUPS[gi][1]
        for half in range(2):
            src = w_in[l, half * 1024:(half + 1) * 1024, c0:c0 + 512].rearrange("(k p) n -> p k n", p=128)
            pool.dma(wring[slot][:, half * 8:(half + 1) * 8, :], src, wring_b[slot], writes=[wring_b[slot]])

    def run_pipeline(ntiles, make_stages):
        pl = []
        it = 0
        while True:
            if it < ntiles:
                pl.append(make_stages(it))
            live = False
            maxk = max(len(x) for x in pl)
            for k in range(maxk - 1, -1, -1):
                tt = it - k
                if 0 <= tt < len(pl) and k < len(pl[tt]):
                    pl[tt][k]()
            if it >= ntiles - 1 and all(it - tt >= len(pl[tt]) - 1 for tt in range(len(pl))):
                break
            it += 1


    for l in range(DEPTH):
        xs_ap = x_in if l == 0 else x1
        xsb = x_src_b[l]
        for (t_sb, name, src) in [(nwt, "nwt", nw_pp[l]), (bwt, "bwt", bw_pp[l]), (sbt, "sbt", sb_pp[l])]:
            sp.dma(t_sb[:], src, bf(name), writes=[bf(name)])
        sp.dma(qwb[:], pbc(qw[l]), bf("qwb"), writes=[bf("qwb")])
        sp.dma(kwb[:], pbc(kw[l]), bf("kwb"), writes=[bf("kwb")])
        sp.dma(qwsb[:], pbc(qws[l]), bf("qwsb"), writes=[bf("qwsb")])
        sp.dma(kwsb[:], pbc(kws[l]), bf("kwsb"), writes=[bf("kwsb")])
        sp.dma(snwb[:], pbc(snw[l]), bf("snwb"), writes=[bf("snwb")])
        for j in range(3):
            sp.dma(cwb[:, j, :], pbc(cw[l, j:j + 1, :]), bf("cwb"), writes=[bf("cwb")])
        pool.dma(wst[:], wsT[l], bf("wst"), writes=[bf("wst")])
        dve.op(lambda e: e.tensor_reduce(out=negc[:, 1:2], in_=qwb[:], axis=AX.X, op=ALU.max, apply_absolute_value=True),
               reads=[bf("qwb")], writes=[bf("negc")])
        dve.op(lambda e: e.tensor_reduce(out=negc[:, 2:3], in_=kwb[:], axis=AX.X, op=ALU.max, apply_absolute_value=True),
               reads=[bf("kwb")], writes=[bf("negc")])
        dve.op(lambda e: e.tensor_scalar(out=negc[:, 0:1], in0=negc[:, 1:2], scalar1=-float(HD) ** 0.5, scalar2=negc[:, 2:3],
                                         op0=ALU.mult, op1=ALU.mult), reads=[bf("negc")], writes=[bf("negc")])
        dve.op(lambda e: e.tensor_scalar(out=negc[:, 0:1], in0=negc[:, 0:1], scalar1=80.0, scalar2=0.0,
                                         op0=ALU.add, op1=ALU.min), reads=[bf("negc")], writes=[bf("negc")])

        hT_b = [alias(f"hT{l}_{t}", R1_old) for t in range(NT)]
        KTst_b = alias(f"KTst{l}", R2_old)
        Vst_b = alias(f"Vst{l}", R2_old)
        dve.op(lambda e: e.memset(Vst[:, :, 128:129], 1.0), writes=[Vst_b])
        ia_b = [None] * NT
        ib_b = [alias(f"ib{l}_{t}", R2_old) for t in range(NT)]
        ia16_b = [alias(f"iah{l}_{t}", R2_old) for t in range(NT)]
        junkA_b = ib_b[0]
        regionsA = lambda: ([(t * 2048, (t + 1) * 2048, ia_b[t]) for t in range(NT)]
                    + [(t * 1024, (t + 1) * 1024, ia16_b[t]) for t in range(NT)]
                    + [(32768 + t * 1024, 32768 + (t + 1) * 1024, ib_b[t]) for t in range(NT)]
                    + [(49152, 49152 + 4 * TOK, KTst_b), (49152 + 4 * TOK, 49152 + 4 * TOK + NT * DV * 2, Vst_b)])
        if l > 0:
            wring_b = [alias(f"wring{l}_0", R3_old), alias(f"wring{l}_1", R3_old), None, None]
        wring_b[2] = alias(f"wringx{l}_2", R2_old)
        wring_b[3] = alias(f"wringx{l}_3", R2_old)
        tA_b = [[alias(f"tA{l}_{s}_{i}", R3_old) for i in range(4)] for s in range(2)]
        gst_b = [alias(f"gst{l}_{i}", R3_old) for i in range(2)]
        qst_b = [alias(f"qst{l}_{i}", R3_old) for i in range(2)]
        mst_b = [alias(f"mst{l}_{i}", R3_old) for i in range(2)]

        load_wgroup(l, 0)
        load_wgroup(l, 1)

        sp.dma(xe_src[l][0:1, :], xs_ap[0:1, :], xesrc_b[l], reads=[xsb[0]], writes=[xesrc_b[l]])
        sp.dma(xe_src[l][1:2, :], xs_ap[TOK - 1:TOK, :], xesrc_b[l], reads=[xsb[NT - 1]], writes=[xesrc_b[l]])
        pool.collective(groups, xe_src[l].opt(), xe_dst[l].opt(), [xesrc_b[l]], [xedst_b[l]], f"xe{l}")

        rx_b = [Buf(f"rx{l}_{t}") for t in range(NT)]

        def a0_stages(t):
            sl = t % 2
            xb = bf(f"xt{sl}")

            def s0():
                sp.dma(xt[sl][:], xs_ap[t * 128:(t + 1) * 128, :], xb, reads=[xsb[t]], writes=[xb])
                si = newstat()
                ssb = bf(f"ss{si}")
                act.op(lambda e: e.activation(out=junkA, in_=xt[sl][:], func=AF.Square, accum_out=ss[:, si * 4:si * 4 + 1]),
                       reads=[xb], writes=[junkA_b, ssb])
                pool.op(lambda e: e.tensor_scalar(out=rx[:, t:t + 1], in0=ss[:, si * 4:si * 4 + 1], scalar1=1.0 / D,
                                                  scalar2=EPS, op0=ALU.mult, op1=ALU.add), reads=[ssb], writes=[rx_b[t]])
                pool.op(lambda e: e.tensor_scalar(out=e2x[:, t:t + 1], in0=rx[:, t:t + 1], scalar1=EPS, scalar2=None,
                                                  op0=ALU.mult), reads=[rx_b[t]], writes=[rx_b[t]])
                pool.op(lambda e: e.tensor_tensor(out=rx[:, t:t + 1], in0=rx[:, t:t + 1], in1=neghalf[:, 0:1], op=ALU.pow),
                        reads=[rx_b[t], bf("neghalf")], writes=[rx_b[t]])

            def s2():
                for j in range(4):
                    pb = 4 + (j % 2)
                    for i in range(4):
                        k = 4 * j + i
                        pe.op(lambda e, k=k, i=i, pb=pb: e.transpose(
                            out=bk(pb)[:, i * 128:(i + 1) * 128], in_=xt[sl][:, k * 128:(k + 1) * 128], identity=ident[:]),
                            reads=[xb, bf("ident")], writes=[bankb[pb]])
                    dve.op(lambda e, j=j, pb=pb: e.tensor_tensor(
                        out=R1[:, 4 * j:4 * j + 4, t * 128:(t + 1) * 128],
                        in0=bk(pb).rearrange("p (i t) -> p i t", i=4),
                        in1=nwt[:, 4 * j:4 * j + 4].unsqueeze(2).broadcast_to([128, 4, 128]), op=ALU.mult),
                        reads=[bankb[pb], bf("nwt")], writes=[hT_b[t]])
            return [s0, s2]

        run_pipeline(NT, a0_stages)

        xeb = bf("xt0")
        sp.dma(xg, xe_dst[l][:, :], bf("xt1"), reads=[xedst_b[l]], writes=[bf("xt1")])
        for n in range(4):
            pe.op(lambda e, n=n: e.matmul(bk(4)[0:2, :],
                                          lhsT=selt[:, :], rhs=xg[:, n * 512:(n + 1) * 512], start=True, stop=True),
                  reads=[bf("selt"), bf("xt1")], writes=[bankb[4]])
            dve.op(lambda e, n=n: e.tensor_copy(out=xh[:, n * 512:(n + 1) * 512], in_=bk(4)[0:2, :]),
                   reads=[bankb[4]], writes=[xeb])
        sh_b, rh_b = bf("ssh"), bf("rsh")
        dve.op(lambda e: e.scalar_tensor_tensor(out=junkA[0:2, :], in0=xh, scalar=1.0, in1=xh,
                                                op0=ALU.mult, op1=ALU.mult, accum_out=ss[0:2, 40:41]),
               reads=[xeb], writes=[junkA_b, sh_b])
        pool.op(lambda e: e.tensor_scalar(out=rs[0:2, 40:41], in0=ss[0:2, 40:41], scalar1=1.0 / D, scalar2=EPS,
                                          op0=ALU.mult, op1=ALU.add), reads=[sh_b], writes=[rh_b])
        pool.op(lambda e: e.tensor_tensor(out=rs[0:2, 40:41], in0=rs[0:2, 40:41], in1=neghalf[0:2, 0:1], op=ALU.pow),
                reads=[rh_b, bf("neghalf")], writes=[rh_b])
        act.op(lambda e: e.activation(out=xh, in_=xh, func=AF.Copy, scale=rs[0:2, 40:41]),
               reads=[xeb, rh_b], writes=[xeb])
        for j in range(4):
            for i in range(4):
                k = 4 * j + i
                pe.op(lambda e, k=k, i=i: e.transpose(out=bk(5)[:, i * 2:i * 2 + 2], in_=xh[:, k * 128:(k + 1) * 128],
                                                      identity=ident[0:2, 0:2]),
                      reads=[xeb, bf("ident")], writes=[bankb[5]])
            dve.op(lambda e, j=j: e.tensor_tensor(
                out=hTh[:, 4 * j:4 * j + 4, :],
                in0=bk(5)[:, 0:8].rearrange("p (i t) -> p i t", i=4),
                in1=nwt[:, 4 * j:4 * j + 4].unsqueeze(2).broadcast_to([128, 4, 2]), op=ALU.mult),
                reads=[bankb[5], bf("nwt")], writes=[bf("hTh")])


        tA_b += [[alias(f"tAx{l}_{s}_{i}", [bf(f"xt{s}")]) for i in range(4)] for s in range(2)]
        bank_ctr = [0]
        slot_ctr = [0]
        cs_ctr = [0]

        def proj_mm(gi, t):
            b = bank_ctr[0] % 4
            bank_ctr[0] += 1
            ws = slot_of(gi)
            for k in range(16):
                pe.op(lambda e, k=k, b=b, ws=ws, t=t: e.matmul(
                    bk(b), lhsT=R1[:, k, t * 128:(t + 1) * 128], rhs=wring[ws][:, k, :],
                    start=(k == 0), stop=(k == 15)),
                    reads=[hT_b[t], wring_b[ws]], writes=[bankb[b]])
            return b

        def transposes_out(src_ap, src_b, nblk, dst_sb, dst_b, scale_tab, evac_eng, dram_ap, dram_b, pbank):
            for i in range(nblk):
                pe.op(lambda e, i=i: e.transpose(out=bk(pbank)[:, i * 128:(i + 1) * 128],
                                                 in_=src_ap[:, i * 128:(i + 1) * 128], identity=ident[:]),
                      reads=[src_b, bf("ident")], writes=[bankb[pbank]])
            pin = bk(pbank)[:, 0:nblk * 128].rearrange("p (i t) -> p i t", i=nblk)
            if scale_tab is None:
                act.op(lambda e: e.activation(out=dst_sb, in_=pin, func=AF.Copy),
                       reads=[bankb[pbank]], writes=[dst_b])
            else:
                for i in range(nblk):
                    act.op(lambda e, i=i: e.activation(out=dst_sb[:, i, :], in_=pin[:, i, :], func=AF.Copy,
                                                       scale=scale_tab[:, i:i + 1]),
                           reads=[bankb[pbank], bf("bwt")], writes=[dst_b])
            if dram_ap is not None:
                sp.dma(dram_ap, dst_sb, dst_b, reads=[dst_b], writes=[dram_b])

        def load_cs(t, wtab, wtab_b, wstab, wstab_b):
            csl = cs_ctr[0] % 3
            cs_ctr[0] += 1
            csb = bf(f"cs{csl}")
            sp.dma(cs[csl][:, 0, :], cos_t[t * 128:(t + 1) * 128, :], csb, writes=[csb])
            sp.dma(cs[csl][:, 1, :], sin_t[t * 128:(t + 1) * 128, :], csb, writes=[csb])
            pool.op(lambda e: e.tensor_tensor(out=cs[csl][:, 0, :], in0=cs[csl][:, 0, :], in1=wtab[:], op=ALU.mult),
                    reads=[csb, wtab_b], writes=[csb])
            pool.op(lambda e: e.tensor_tensor(out=cs[csl][:, 1, :], in0=cs[csl][:, 1, :], in1=wstab[:], op=ALU.mult),
                    reads=[csb, wstab_b], writes=[csb])
            return csl

        def group_pre(gi, gname):
            if gi == 0:
                load_wgroup(l, 2)
                load_wgroup(l, 3)
            if gi >= 3 and gi + 1 < len(GROUPS):
                load_wgroup(l, gi + 1)
            if gname == "CI":
                for t_ in range(NT):
                    ia_b[t_] = alias(f"ia{l}_{t_}", [wring_b[2] if (t_ + 1) * 2048 <= 16384 else wring_b[3]])
            if gname in ("CI", "CC"):
                hb = 4
                for k in range(16):
                    pe.op(lambda e, k=k, ws=slot_of(gi): e.matmul(bk(hb)[0:2, :], lhsT=hTh[:, k, :], rhs=wring[ws][:, k, :],
                                                             start=(k == 0), stop=(k == 15)),
                          reads=[bf("hTh"), wring_b[slot_of(gi)]], writes=[bankb[hb]])
                if gname == "CI":
                    act.op(lambda e: e.activation(out=hh_ci[:], in_=bk(hb)[0:2, :], func=AF.Copy),
                           reads=[bankb[hb]], writes=[bf("hh")])
                else:
                    dve.op(lambda e: e.tensor_tensor(out=hh[:], in0=bk(hb)[0:2, :], in1=hh_ci[:], op=ALU.mult),
                           reads=[bankb[hb], bf("hh")], writes=[bf("hh")])

        def make_stages(t, gi, gname):
            s4 = slot_ctr[0] % 4
            s = slot_ctr[0] % 2
            slot_ctr[0] += 1
            tb, tbb = tA[s4], tA_b[s4]
            t0, t1, t2, t3 = tb
            st = {}
            v3 = lambda a: a.rearrange("p (h d) -> p h d", d=128)

            def rope_s0(ncol, wt, wtb, wst_, wstb):
                b = st["b"]
                csl = load_cs(t, wt, wtb, wst_, wstb)
                si = newstat()
                st["csl"], st["si"] = csl, si
                act.op(lambda e: e.activation(out=t0[:, 0:ncol], in_=bk(b)[:, 0:ncol], func=AF.Square),
                       reads=[bankb[b]], writes=[tbb[0]])

            def rope_s0b(ncol):
                nh = ncol // 128
                si = st["si"]
                ssb, rsb = bf(f"ss{si}"), bf(f"rs{si}")
                dve.op(lambda e: e.tensor_reduce(out=ss[:, si * 4:si * 4 + nh], in_=v3(t0[:, 0:ncol]),
                                                 axis=AX.X, op=ALU.add), reads=[tbb[0]], writes=[ssb])
                rstd_from_ss(ss[:, si * 4:si * 4 + nh], rs[:, si * 4:si * 4 + nh], 1.0 / 128, ssb, rsb,
                             eps_ap=e2x[:, t:t + 1], eps_b=rx_b[t])

            def rope_s1a(ncol):
                nh = ncol // 128
                b, si = st["b"], st["si"]
                rsb = bf(f"rs{si}")
                for h in range(nh):
                    act.op(lambda e, h=h: e.activation(out=t1[:, h * 128:(h + 1) * 128], in_=bk(b)[:, h * 128:(h + 1) * 128],
                                                       func=AF.Copy, scale=rs[:, si * 4 + h:si * 4 + h + 1]),
                           reads=[bankb[b], rsb], writes=[tbb[1]])

            def rope_s1(ncol):
                nh = ncol // 128
                b, csl, si = st["b"], st["csl"], st["si"]
                csb, rsb = bf(f"cs{csl}"), bf(f"rs{si}")
                dve.op(lambda e: e.tensor_tensor(out=v3(t2[:, 0:ncol]), in0=v3(t1[:, 0:ncol]),
                                                 in1=cs[csl][:, 0, :].unsqueeze(1).broadcast_to([128, nh, 128]),
                                                 op=ALU.mult), reads=[tbb[1], csb], writes=[tbb[2]])
                for hf in range(2):
                    o_v = t0[:, 0:ncol].rearrange("p (h a two f) -> p h a two f", a=2, two=2, f=32)[:, :, :, hf, :]
                    i_v = t1[:, 0:ncol].rearrange("p (h a two f) -> p h a two f", a=2, two=2, f=32)[:, :, :, 1 - hf, :]
                    s_v = cs[csl][:, 1, :].rearrange("p (a two f) -> p a two f", a=2, two=2)[:, :, hf, :]
                    dve.op(lambda e, o_v=o_v, i_v=i_v, s_v=s_v: e.tensor_tensor(
                        out=o_v, in0=i_v, in1=s_v.unsqueeze(1).broadcast_to([128, nh, 2, 32]), op=ALU.mult),
                        reads=[tbb[1], csb], writes=[tbb[0]])
                dve.op(lambda e: e.tensor_tensor(out=t3[:, 0:ncol], in0=t2[:, 0:ncol], in1=t0[:, 0:ncol], op=ALU.add),
                       reads=[tbb[2], tbb[0]], writes=[tbb[3]])

            def mm():
                st["b"] = proj_mm(gi, t)

            if gname == "KV":
                def s0():
                    mm()
                    b = st["b"]
                    act.op(lambda e: e.activation(
                        out=Vst[:, t, 0:258].rearrange("p (h d) -> p h d", h=2)[:, :, 0:128],
                        in_=bk(b)[:, 256:512].rearrange("p (h d) -> p h d", h=2), func=AF.Copy, scale=rx[:, t:t + 1]),
                        reads=[bankb[b], rx_b[t]], writes=[Vst_b])
                    rope_s0(256, kwb, bf("kwb"), kwsb, bf("kwsb"))
                return [s0, lambda: rope_s0b(256), lambda: rope_s1a(256), lambda: rope_s1(256),
                        lambda: transposes_out(t3[:, 0:256], tbb[3], 2, KTst[:, :, t * 128:(t + 1) * 128], KTst_b,
                                               None, act, None, None, 4 + (t % 2))]
            if gname in ("QA", "QB"):
                h0 = 0 if gname == "QA" else 4

                def s0():
                    mm()
                    rope_s0(512, qwb, bf("qwb"), qwsb, bf("qwsb"))
                return [s0, lambda: rope_s0b(512), lambda: rope_s1a(512), lambda: rope_s1(512),
                        lambda: transposes_out(t3, tbb[3], 4, qst[s][:], qst_b[s], None, act,
                                               qT_s[l][t, :, h0:h0 + 4, :], qT_b[l][t], 4 + (t % 2))]
            if gname in ("GA0", "GA1"):
                cc = 0 if gname == "GA0" else 512

                def s0():
                    mm()
                    b = st["b"]
                    act.op(lambda e: e.activation(out=gst[s], in_=bk(b), func=AF.Silu, scale=rx[:, t:t + 1]),
                           reads=[bankb[b], rx_b[t]], writes=[gst_b[s]])
                    sp.dma(g_s[l][t * 128:(t + 1) * 128, cc:cc + 512], gst[s], gst_b[s],
                           reads=[gst_b[s]], writes=[g_b[l][t]])
                return [s0]
            if gname == "CI":
                def s0():
                    mm()
                    b = st["b"]
                    act.op(lambda e: e.activation(out=ia[:, t, :], in_=bk(b), func=AF.Copy, scale=rx[:, t:t + 1]),
                           reads=[bankb[b], rx_b[t]], writes=[ia_b[t]])
                return [s0]
            if gname == "CC":
                def s0():
                    mm()
                    b = st["b"]
                    dve.op(lambda e: e.scalar_tensor_tensor(out=ia[:, t, :], in0=bk(b), scalar=rx[:, t:t + 1],
                                                            in1=ia[:, t, :], op0=ALU.mult, op1=ALU.mult),
                           reads=[bankb[b], ia_b[t], rx_b[t]], writes=[ia_b[t]])
                return [s0]
            if gname == "CG":
                def s0():
                    mm()
                    b = st["b"]
                    act.op(lambda e: e.activation(out=ib16[:, t, :], in_=bk(b), func=AF.Silu, scale=rx[:, t:t + 1]),
                           reads=[bankb[b], rx_b[t]], writes=[ib_b[t]])
                return [s0]
            if gname == "SG":
                def s0():
                    mm()
                    b = st["b"]
                    act.op(lambda e: e.activation(out=ia16[:, t, :], in_=bk(b), func=AF.Silu, scale=rx[:, t:t + 1]),
                           reads=[bankb[b], rx_b[t]], writes=[ia16_b[t], ia_b[t // 2]])
                return [s0]
            if gname == "CB":
                def s0():
                    sp.dma(t0[1:127, :], ia[0:126, t, :], tbb[0], reads=[ia_b[t]], writes=[tbb[0]])
                    sp.dma(t0[127:128, :], ia[126:127, t, :], tbb[0], reads=[ia_b[t]], writes=[tbb[0]])
                    if t > 0:
                        sp.dma(t0[0:1, :], ia[127:128, t - 1, :], tbb[0], reads=[ia_b[t - 1]], writes=[tbb[0]])
                    else:
                        sp.dma(t0[0:1, :], hh[0:1, :], tbb[0], reads=[bf("hh")], writes=[tbb[0]])
                    sp.dma(t1[1:127, :], ia[2:128, t, :], tbb[1], reads=[ia_b[t]], writes=[tbb[1]])
                    sp.dma(t1[0:1, :], ia[1:2, t, :], tbb[1], reads=[ia_b[t]], writes=[tbb[1]])
                    if t < NT - 1:
                        sp.dma(t1[127:128, :], ia[0:1, t + 1, :], tbb[1], reads=[ia_b[t + 1]], writes=[tbb[1]])
                    else:
                        sp.dma(t1[127:128, :], hh[1:2, :], tbb[1], reads=[bf("hh")], writes=[tbb[1]])
                    mm()
                    pool.op(lambda e: e.tensor_tensor(out=t2, in0=ia[:, t, :], in1=cwb[:, 1, :], op=ALU.mult),
                            reads=[ia_b[t], bf("cwb")], writes=[tbb[2]])

                def s1():
                    b = st["b"]
                    dve.op(lambda e: e.tensor_tensor(out=t0, in0=t0, in1=cwb[:, 0, :], op=ALU.mult),
                           reads=[tbb[0], bf("cwb")], writes=[tbb[0]])
                    dve.op(lambda e: e.tensor_tensor(out=t1, in0=t1, in1=cwb[:, 2, :], op=ALU.mult),
                           reads=[tbb[1], bf("cwb")], writes=[tbb[1]])
                    dve.op(lambda e: e.tensor_tensor(out=t1, in0=t1, in1=t2, op=ALU.add),
                           reads=[tbb[1], tbb[2]], writes=[tbb[1]])
                    dve.op(lambda e: e.tensor_tensor(out=t0, in0=t0, in1=t1, op=ALU.add),
                           reads=[tbb[0], tbb[1]], writes=[tbb[0]])
                    dve.op(lambda e: e.scalar_tensor_tensor(out=t0, in0=bk(b), scalar=rx[:, t:t + 1], in1=t0,
                                                            op0=ALU.mult, op1=ALU.mult),
                           reads=[bankb[b], tbb[0], rx_b[t]], writes=[tbb[0]])

                def s1b():
                    si = newstat()
                    st["si"] = si
                    ssb, rsb = bf(f"ss{si}"), bf(f"rs{si}")
                    act.op(lambda e: e.activation(out=t1, in_=t0, func=AF.Square, accum_out=ss[:, si * 4:si * 4 + 1]),
                           reads=[tbb[0]], writes=[tbb[1], ssb])
                    rstd_from_ss(ss[:, si * 4:si * 4 + 1], rs[:, si * 4:si * 4 + 1], 1.0 / 512, ssb, rsb)

                def s2():
                    si = st["si"]
                    dve.op(lambda e: e.scalar_tensor_tensor(
                        out=t3, in0=t0, scalar=rs[:, si * 4:si * 4 + 1], in1=ib16[:, t, :],
                        op0=ALU.mult, op1=ALU.mult), reads=[tbb[0], bf(f"rs{si}"), ib_b[t]], writes=[tbb[3]])
                return [s0, s1, s1b, s2,
                        lambda: transposes_out(t3, tbb[3], 4, mst[s][:], mst_b[s], bwt[:, 8:12], dve,
                                               mT_s[l][t, :, 0:4, :], mT_b[l][t], 4 + (t % 2))]
            if gname == "SV":
                def s0():
                    mm()
                    b = st["b"]
                    act.op(lambda e: e.activation(out=t0, in_=bk(b), func=AF.Gelu, scale=rx[:, t:t + 1]),
                           reads=[bankb[b], rx_b[t]], writes=[tbb[0]])
                    act.op(lambda e: e.activation(out=t1, in_=t0, func=AF.Square),
                           reads=[tbb[0]], writes=[tbb[1]])

                def s0b():
                    si = newstat()
                    st["si"] = si
                    ssb, rsb = bf(f"ss{si}"), bf(f"rs{si}")
                    dve.op(lambda e: e.tensor_reduce(out=ss[:, si * 4:si * 4 + 4], in_=v3(t1), axis=AX.X, op=ALU.add),
                           reads=[tbb[1]], writes=[ssb])
                    rstd_from_ss(ss[:, si * 4:si * 4 + 4], rs[:, si * 4:si * 4 + 4], 1.0 / 128, ssb, rsb)

                def s1():
                    si = st["si"]
                    dve.op(lambda e: e.tensor_tensor(
                        out=v3(t2), in0=v3(t0),
                        in1=rs[:, si * 4:si * 4 + 4].unsqueeze(2).broadcast_to([128, 4, 128]), op=ALU.mult),
                        reads=[tbb[0], bf(f"rs{si}")], writes=[tbb[2]])

                def s2():
                    dve.op(lambda e: e.tensor_tensor(out=ib16[:, t, :], in0=t2, in1=snwb[:], op=ALU.mult),
                           reads=[tbb[2], bf("snwb")], writes=[ib_b[t]])
                return [s0, s0b, s1, s2]
            if gname == "SU":
                sbk = 6 + (t % 2)

                def s0():
                    for g in range(4):
                        pe.op(lambda e, g=g: e.matmul(bk(sbk)[:, g * 128:(g + 1) * 128], lhsT=wst[:, g, :],
                                                      rhs=ib16[:, t, g * 128:(g + 1) * 128], start=True, stop=True),
                              reads=[bf("wst"), ib_b[t]], writes=[bankb[sbk]])
                    mm()
                    b = st["b"]
                    act.op(lambda e: e.activation(out=t0, in_=bk(b), func=AF.Gelu, scale=rx[:, t:t + 1]),
                           reads=[bankb[b], rx_b[t]], writes=[tbb[0]])

                def s1():
                    for g in range(4):
                        dve.op(lambda e, g=g: e.scalar_tensor_tensor(
                            out=t1[:, g * 128:(g + 1) * 128], in0=bk(sbk)[:, g * 128:(g + 1) * 128],
                            scalar=sbt[:, g:g + 1], in1=t0[:, g * 128:(g + 1) * 128], op0=ALU.add, op1=ALU.mult),
                            reads=[bankb[sbk], bf("sbt"), tbb[0]], writes=[tbb[1]])

                def s1b():
                    si = newstat()
                    st["si"] = si
                    ssb, rsb = bf(f"ss{si}"), bf(f"rs{si}")
                    act.op(lambda e: e.activation(out=t2, in_=t1, func=AF.Square, accum_out=ss[:, si * 4:si * 4 + 1]),
                           reads=[tbb[1]], writes=[tbb[2], ssb])
                    rstd_from_ss(ss[:, si * 4:si * 4 + 1], rs[:, si * 4:si * 4 + 1], 1.0 / 512, ssb, rsb)

                def s2():
                    si = st["si"]
                    dve.op(lambda e: e.scalar_tensor_tensor(
                        out=t3, in0=t1, scalar=rs[:, si * 4:si * 4 + 1], in1=ia16[:, t, :],
                        op0=ALU.mult, op1=ALU.mult), reads=[tbb[1], bf(f"rs{si}"), ia16_b[t]], writes=[tbb[3]])
                return [s0, s1, s1b, s2,
                        lambda: transposes_out(t3, tbb[3], 4, mst[s][:], mst_b[s], bwt[:, 12:16], dve,
                                               mT_s[l][t, :, 4:8, :], mT_b[l][t], 4 + (t % 2))]
            raise AssertionError(gname)

        def olds_for(lo, hi):
            return [b_ for (a_, z_, b_) in regionsA() if a_ < hi and lo < z_]

        pv = lambda r_: (r_ + 2) % NR
        KTv_b = [[None] * NR for _ in range(2)]
        Vv_b = [None] * NR

        def load_kt(h, r_):
            lo = (h * S + r_ * TOK) * 2
            KTv_b[h][r_] = alias(f"KT{l}_{h}_{r_}", olds_for(lo, lo + 2 * TOK))
            sp.dma(KT[:, h, r_ * TOK:(r_ + 1) * TOK], kt_dst[l][(r_ * 2 + h) * 128:(r_ * 2 + h + 1) * 128, :],
                   KTv_b[h][r_], reads=[ktdst_b[l]], writes=[KTv_b[h][r_]])

        def load_v(r_):
            p_ = pv(r_)
            lo = (2 * S + p_ * NT * DV) * 2
            Vv_b[p_] = alias(f"Vv{l}_{p_}", olds_for(lo, lo + NT * DV * 2))
            for i_ in range(2):
                o_ = 2 * S + p_ * NT * DV + i_ * NH * DV
                sp.dma(R2[:, o_:o_ + NH * DV], v_dst[l][i_][r_ * 128:(r_ + 1) * 128, :],
                       Vv_b[p_], reads=[vdst_b[l][i_]], writes=[Vv_b[p_]])

        def group_post(gi, gname):
            if gname == "CB":
                for r_ in range(NR):
                    load_kt(1, r_)
                load_v(0)
                load_v(1)
            if gname == "KV":
                sp.dma(kt_src[l].rearrange("(h d) t -> d h t", h=2), KTst, KTst_b, reads=[KTst_b], writes=[ktsrc_b[l]])
                for i_ in range(2):
                    o_ = 24576 + 2 * TOK + i_ * NH * DV
                    sp.dma(v_src[l][i_][:, :], R2[:, o_:o_ + NH * DV], Vst_b, reads=[Vst_b], writes=[vsrc_b[l][i_]])
            if gname == "KV":
                pool.collective(groups, kt_src[l].opt(), kt_dst[l].opt(), [ktsrc_b[l]], [ktdst_b[l]], f"kt{l}")
                for i_ in range(2):
                    pool.collective(groups, v_src[l][i_].opt(), v_dst[l][i_].opt(), [vsrc_b[l][i_]], [vdst_b[l][i_]], f"v{l}_{i_}")

        items = [(gi, gname, t) for gi, (gname, c0) in enumerate(GROUPS) for t in range(NT)]

        def make_item(idx):
            gi, gname, t = items[idx]
            stages = list(make_stages(t, gi, gname))
            if t == 0:
                f0 = stages[0]
                stages[0] = (lambda f0=f0, gi=gi, gname=gname: (group_pre(gi, gname), f0()))
            if t == NT - 1:
                fl = stages[-1]
                stages[-1] = (lambda fl=fl, gi=gi, gname=gname: (fl(), group_post(gi, gname)))
            return stages

        run_pipeline(len(items), make_item)

        for s_ in range(2):
            B[f"xt{s_}"] = alias(f"xtB{l}_{s_}", tA_b[2 + s_])
        R1_old = hT_b
        R2_old = ia_b + ib_b + ia16_b + [KTst_b, Vst_b]
        R3_old = wring_b + [x for s_ in tA_b[0:2] for x in s_] + gst_b + qst_b + mst_b
        wout_b = [alias(f"wout{l}_{q}", R1_old) for q in range(4)]
        qTt_b = [alias(f"qTt{l}_{i}", R3_old) for i in range(2)]
        gt_b = [alias(f"gt{l}_{i}", R3_old) for i in range(2)]
        mTt_b = [alias(f"mTt{l}_{i}", R3_old) for i in range(2)]
        PT_b = [alias(f"PT{l}_{i}", R3_old) for i in range(3)]
        ao_b = [alias(f"ao{l}_{i}", R3_old) for i in range(2)]
        mTa_b = [alias(f"mTa{l}_{i}", R3_old) for i in range(2)]
        junkB_b = alias(f"junkB{l}", R3_old)
        fwb_b = alias(f"fwb{l}", R3_old)

        def load_wout(half):
            for n in (2 * half, 2 * half + 1):
                for hf in range(2):
                    src = w_out[l, hf * 1024:(hf + 1) * 1024, n * 512:(n + 1) * 512].rearrange("(k p) c -> p k c", p=128)
                    pool.dma(R1[:, hf * 8:(hf + 1) * 8, n * 512:(n + 1) * 512], src, wout_b[n], writes=[wout_b[n]])

        KV_REST = True
        last = (l == DEPTH - 1)
        if last:
            sp.dma(fwb, pbc(fw), fwb_b, writes=[fwb_b])

        def loads_B(t):
            sl = t % 2
            sp.dma(qTt[sl][:], qT_s[l][t], qTt_b[sl], reads=[qT_b[l][t]], writes=[qTt_b[sl]])
            sp.dma(gt[sl], g_s[l][t * 128:(t + 1) * 128, :], gt_b[sl], reads=[g_b[l][t]], writes=[gt_b[sl]])
            sp.dma(mTt[sl][:], mT_s[l][t], mTt_b[sl], reads=[mT_b[l][t]], writes=[mTt_b[sl]])
            xb = bf(f"xt{sl}")
            sp.dma(xt[sl][:], xs_ap[t * 128:(t + 1) * 128, :], xb, reads=[xsb[t]], writes=[xb])

        def post1(t):
            sl = t % 2
            a = ao[sl]
            si = newstat()
            ssb, rsb = bf(f"ss{si}"), bf(f"rs{si}")
            dve.op(lambda e: e.scalar_tensor_tensor(out=junkB[:, 0:1024], in0=a, scalar=1.0, in1=a,
                                                    op0=ALU.mult, op1=ALU.mult, accum_out=ss[:, si * 4:si * 4 + 1]),
                   reads=[ao_b[sl]], writes=[junkB_b, ssb])
            rstd_from_ss(ss[:, si * 4:si * 4 + 1], rs[:, si * 4:si * 4 + 1], 1.0 / 1024, ssb, rsb)
            dve.op(lambda e: e.scalar_tensor_tensor(out=a, in0=a, scalar=rs[:, si * 4:si * 4 + 1], in1=gt[sl],
                                                    op0=ALU.mult, op1=ALU.mult),
                   reads=[ao_b[sl], rsb, gt_b[sl]], writes=[ao_b[sl]])
            for j in range(2):
                pb = 6 + j
                for i in range(4):
                    k = 4 * j + i
                    pe.op(lambda e, k=k, i=i, pb=pb: e.transpose(out=bk(pb)[:, i * 128:(i + 1) * 128],
                                                                 in_=a[:, k * 128:(k + 1) * 128], identity=ident[:]),
                          reads=[ao_b[sl], bf("ident")], writes=[bankb[pb]])
                dve.op(lambda e, j=j, pb=pb: e.tensor_tensor(
                    out=mTa[sl][:, 4 * j:4 * j + 4, :], in0=bk(pb).rearrange("p (i t) -> p i t", i=4),
                    in1=bwt[:, 4 * j:4 * j + 4].unsqueeze(2).broadcast_to([128, 4, 128]), op=ALU.mult),
                    reads=[bankb[pb], bf("bwt")], writes=[mTa_b[sl]])

        def post2(t):
            sl = t % 2
            xb = bf(f"xt{sl}")
            for n in range(4):
                pb = 6 + (n % 2)
                for k in range(16):
                    lhsT = mTa[sl][:, k, :] if k < 8 else mTt[sl][:, k - 8, :]
                    lb = mTa_b[sl] if k < 8 else mTt_b[sl]
                    pe.op(lambda e, k=k, n=n, pb=pb, lhsT=lhsT: e.matmul(
                        bk(pb), lhsT=lhsT, rhs=R1[:, k, n * 512:(n + 1) * 512], start=(k == 0), stop=(k == 15)),
                        reads=[lb, wout_b[n]], writes=[bankb[pb]])
                dve.op(lambda e, n=n, pb=pb: e.tensor_tensor(out=xt[sl][:, n * 512:(n + 1) * 512], in0=bk(pb),
                                                             in1=xt[sl][:, n * 512:(n + 1) * 512], op=ALU.add),
                       reads=[bankb[pb], xb], writes=[xb])
            if not last:
                sp.dma(x1[t * 128:(t + 1) * 128, :], xt[sl][:], xb, reads=[xb], writes=[x_src_b[l + 1][t]])
            else:
                si = newstat()
                ssb, rsb = bf(f"ss{si}"), bf(f"rs{si}")
                dve.op(lambda e: e.scalar_tensor_tensor(out=junkB, in0=xt[sl][:], scalar=1.0, in1=xt[sl][:],
                                                        op0=ALU.mult, op1=ALU.mult, accum_out=ss[:, si * 4:si * 4 + 1]),
                       reads=[xb], writes=[junkB_b, ssb])
                rstd_from_ss(ss[:, si * 4:si * 4 + 1], rs[:, si * 4:si * 4 + 1], 1.0 / D, ssb, rsb)
                dve.op(lambda e: e.scalar_tensor_tensor(out=xt[sl][:], in0=xt[sl][:], scalar=rs[:, si * 4:si * 4 + 1],
                                                        in1=fwb, op0=ALU.mult, op1=ALU.mult),
                       reads=[xb, rsb, fwb_b], writes=[xb])
                sp.dma(y_out[t * 128:(t + 1) * 128, :], xt[sl][:], xb, reads=[xb], writes=[y_b])

        NC2 = NKT // 2
        steps = [(t, g, c) for t in range(NT) for g in (1, 0) for c in range(NC2)]
        scale = float(HD) ** -0.5

        def emit_qk(i):
            t, g, c = steps[i]
            sl = t % 2
            sb_ = i % 2
            for j in range(2):
                kt_i = c * 2 + j
                pe.op(lambda e, j=j, kt_i=kt_i, g=g, sl=sl, sb_=sb_: e.matmul(
                    pp[sb_][:, j * 512:(j + 1) * 512], lhsT=KT[:, g, kt_i * 128:(kt_i + 1) * 128],
                    rhs=qTt[sl][:, 4 * g:4 * g + 4, :].rearrange("p h t -> p (h t)"), start=True, stop=True),
                    reads=[KTv_b[g][kt_i // NT], qTt_b[sl]], writes=[bankb[2 * sb_ + j]])

        def emit_exp(i):
            sb_ = i % 2
            p = i % 3
            act.op(lambda e: e.activation(out=PT[p], in_=pp[sb_][:, :], func=AF.Exp, scale=scale, bias=negc[:, 0:1]),
                   reads=[bankb[2 * sb_], bankb[2 * sb_ + 1], bf("negc")], writes=[PT_b[p]])

        def emit_pv(i):
            t, g, c = steps[i]
            p = i % 3
            for j in range(2):
                kt_i = c * 2 + j
                for h4 in range(4):
                    ob = 4 + h4 // 2
                    c0_ = (h4 % 2) * 256
                    pe.op(lambda e, j=j, kt_i=kt_i, h4=h4, ob=ob, c0_=c0_: e.matmul(
                        bk(ob)[:, c0_:c0_ + 129], lhsT=PT[p][:, j * 512 + h4 * 128:j * 512 + (h4 + 1) * 128],
                        rhs=Vaug[:, pv(kt_i // NT) * NT + kt_i % NT, 128 * g:128 * g + 129],
                        start=(c == 0 and j == 0 and h4 % 2 == 0), stop=(c == NC2 - 1 and j == 1),
                        skip_group_check=True),
                        reads=[PT_b[p], Vv_b[pv(kt_i // NT)]], writes=[bankb[ob]])
            if c == NC2 - 1:
                sl = t % 2
                rb = bf("rden")
                for h4 in range(4):
                    ob = 4 + h4 // 2
                    c0_ = (h4 % 2) * 256
                    h = 4 * g + h4
                    dcol = c0_ + (128 if g == 0 else 0)
                    vcol = c0_ + (0 if g == 0 else 1)
                    dve.op(lambda e, ob=ob, dcol=dcol, h4=h4: e.reciprocal(out=rs[:, 41 + (h4 % 2):42 + (h4 % 2)],
                                                                         in_=bk(ob)[:, dcol:dcol + 1]),
                           reads=[bankb[ob]], writes=[rb])
                    dve.op(lambda e, ob=ob, vcol=vcol, h=h, h4=h4: e.tensor_scalar(
                        out=ao[sl][:, h * 128:(h + 1) * 128], in0=bk(ob)[:, vcol:vcol + 128],
                        scalar1=rs[:, 41 + (h4 % 2):42 + (h4 % 2)], scalar2=None,
                        op0=ALU.mult), reads=[bankb[ob], rb], writes=[ao_b[sl]])

        loads_B(0)
        load_v(2)
        load_v(3)
        for r_ in range(NR):
            load_kt(0, r_)
        emit_qk(0)
        if len(steps) > 1:
            emit_qk(1)
        for i, (t, g, c) in enumerate(steps):
            emit_exp(i)
            if i + 2 < len(steps):
                emit_qk(i + 2)
            emit_pv(i)
            st_ = i % (2 * NC2)
            if t == 0 and st_ == min(8, 2 * NC2 - 4):
                load_wout(0)
            if t == 0 and st_ == min(24, 2 * NC2 - 3):
                load_wout(1)
            if st_ == 1 and t > 0:
                post1(t - 1)
            if st_ == 3 and t > 0:
                post2(t - 1)
            if st_ == 4 and t + 1 < NT:
                loads_B(t + 1)
        post1(NT - 1)
        post2(NT - 1)

        R1_old = wout_b
        R2_old = [b_ for row in KTv_b for b_ in row] + list(Vv_b)
        R3_old = qTt_b + gt_b + mTt_b + PT_b + ao_b + mTa_b + [junkB_b, fwb_b]

    for tok in y_b.w.values():
        sp.wait(tok)

    with nc.Block() as block:
        @block.tensor
        def _(e):
            for f in pe.prog:
                f(e)

        @block.scalar
        def _(e):
            for f in act.prog:
                f(e)

        @block.vector
        def _(e):
            for f in dve.prog:
                f(e)

        @block.gpsimd
        def _(e):
            for f in pool.prog:
                f(e)

        @block.sync
        def _(e):
            for f in sp.prog:
                f(e)
    return nc, es


def rope_tables(S):
    GRID_W = 64
    rows = S // GRID_W
    row = np.repeat(np.arange(rows, dtype=np.float32), GRID_W)
    col = np.tile(np.arange(GRID_W, dtype=np.float32), rows)
    inv_freq = (10000.0 ** (-np.arange(0, 64, 2, dtype=np.float32) / 64)).astype(np.float32)
    ang = np.stack([row, col], axis=-1)[:, :, None] * inv_freq
    ang = np.broadcast_to(ang[:, :, None, :], (S, 2, 2, 32)).reshape(S, 128)
    cos = np.cos(ang).astype(np.float32)
    sin = np.sin(ang).astype(np.float32).reshape(S, 2, 2, 32).copy()
    sin[:, :, 0, :] *= -1.0
    return cos, sin.reshape(S, 128)


def swap_halves(w, DEPTH):
    w = np.asarray(w, dtype=np.float32).reshape(DEPTH, 2, 2, 32)
    return np.ascontiguousarray(w[:, :, ::-1, :]).reshape(DEPTH, 1, 128)


def host_consts():
    ident = np.eye(128, dtype=np.float32)
    shf = np.zeros((128, 4, 128), np.float32)
    for t in range(128):
        if t - 1 >= 0:
            shf[t - 1, 0, t] = 1.0
        if t + 1 < 128:
            shf[t + 1, 1, t] = 1.0
    shf[127, 2, 0] = 1.0
    shf[0, 3, 127] = 1.0
    e2 = np.zeros((2, 2, 128), np.float32)
    e2[0, 0, 0] = 1.0
    e2[1, 1, 127] = 1.0
    return ident, shf, e2


_CACHE = {}


def run(inputs, TOK, DEPTH=2):
    x = np.ascontiguousarray(inputs["x"], dtype=np.float32)
    Bsz, S, _ = x.shape
    assert Bsz == 2 and S == NR * TOK
    key = (TOK, DEPTH)
    if key not in _CACHE:
        _CACHE[key] = build_program(TOK, DEPTH)
    nc, es = _CACHE[key]
    f32 = lambda a: np.ascontiguousarray(np.asarray(a, dtype=np.float32))
    cos, sin = rope_tables(S)
    ident, shf, e2 = host_consts()
    pp_layout = lambda w: f32(np.asarray(w).reshape(DEPTH, 16, 128).transpose(0, 2, 1))
    common = {
        "w_in": f32(inputs["w_in"]), "w_out": f32(inputs["w_out"]),
        "nw_pp": pp_layout(inputs["norm_w"]), "bw_pp": pp_layout(inputs["branch_norm_w"]),
        "qw": f32(inputs["q_norm_w"]).reshape(DEPTH, 1, 128), "kw": f32(inputs["k_norm_w"]).reshape(DEPTH, 1, 128),
        "qws": swap_halves(inputs["q_norm_w"], DEPTH), "kws": swap_halves(inputs["k_norm_w"], DEPTH),
        "cw": f32(np.asarray(inputs["conv_w"]).transpose(0, 2, 1)),
        "snw": f32(inputs["sgu_norm_w"]).reshape(DEPTH, 1, 512),
        "sb_pp": f32(np.asarray(inputs["sgu_b"]).transpose(0, 2, 1)),
        "wsT": f32(np.asarray(inputs["sgu_w"]).transpose(0, 3, 1, 2)),
        "fw": f32(inputs["final_norm_w"]).reshape(1, D),
        "ident": ident,
    }
    in_maps = []
    for c in range(NCORES):
        b, r = c // NR, c % NR
        sel = np.zeros((8, 2), np.float32)
        if r > 0:
            sel[(r - 1) * 2 + 1, 0] = 1.0
        if r < NR - 1:
            sel[(r + 1) * 2 + 0, 1] = 1.0
        m = dict(common)
        m["x"] = np.ascontiguousarray(x[b, r * TOK:(r + 1) * TOK, :])
        m["cos_t"] = np.ascontiguousarray(cos[r * TOK:(r + 1) * TOK])
        m["sin_t"] = np.ascontiguousarray(sin[r * TOK:(r + 1) * TOK])
        m["sel"] = sel
        in_maps.append(m)
    res = run_bass_kernel_spmd(nc, in_maps, core_ids=list(range(NCORES)))
    out = np.empty((Bsz, S, D), np.float32)
    for c in range(NCORES):
        b, r = c // NR, c % NR
        out[b, r * TOK:(r + 1) * TOK, :] = np.asarray(res.results[c]["y"], dtype=np.float32)
    return out


def kernel(**inputs):
    S = np.asarray(inputs["x"]).shape[1]
    return run(inputs, S // NR, DEPTH=2)
```

```python
import re
import numpy as np
from contextlib import ExitStack
import concourse.bass as bass
import concourse.mybir as mybir
from concourse.bass_utils import run_bass_kernel_spmd

F32 = mybir.dt.float32
BF16 = mybir.dt.bfloat16
AF = mybir.ActivationFunctionType
ALU = mybir.AluOpType
AX = mybir.AxisListType

D = 2048
INW = 6144
HD = 128
EPS = 1e-6
NCORES = 8
NR = 4


class Tok:
    __slots__ = ("sem", "val", "key")

    def __init__(self, sem, val, key):
        self.sem, self.val, self.key = sem, val, key


def _add(d, tok):
    o = d.get(tok.key)
    if o is None or o.val < tok.val:
        d[tok.key] = tok


class Buf:
    def __init__(self, name, sem_key=None):
        self.name = name
        self.w = {}
        self.r = {}
        self.sem_key = sem_key or name


def alias(name, olds):
    b = Buf(name, sem_key=re.sub(r"^([A-Za-z]+)\d+", r"\1", name))
    for o in olds:
        for t in list(o.w.values()) + list(o.r.values()):
            _add(b.r, t)
    return b


class Eng:
    def __init__(self, K, name, is_pe=False):
        self.K = K
        self.name = name
        self.key = "e_" + name
        self.sem = K.newsem("p_" + name)
        self.cnt = 0
        self.waited = {}
        self.is_pe = is_pe
        self.prog = []

    def wait(self, tok):
        if self.waited.get(tok.key, 0) >= tok.val:
            return
        self.waited[tok.key] = tok.val
        sem, val = tok.sem, tok.val
        self.prog.append(lambda e: e.wait_ge(sem, val))

    def deps(self, reads, writes, extra, is_dma=False):
        for b in reads:
            for tok in b.w.values():
                if tok.key == self.key and self.is_pe:
                    continue
                self.wait(tok)
        for b in writes:
            for tok in b.w.values():
                if tok.key == self.key:
                    continue
                if is_dma and tok.key.startswith("d_"):
                    continue
                self.wait(tok)
            for tok in b.r.values():
                if tok.key == self.key:
                    continue
                self.wait(tok)
        for tok in extra:
            self.wait(tok)

    def op(self, fn, reads=(), writes=(), extra=()):
        self.deps(reads, writes, extra)
        self.cnt += 1
        sem = self.sem
        self.prog.append(lambda e: fn(e).then_inc(sem, 1))
        tok = Tok(sem, self.cnt, self.key)
        for b in writes:
            b.w = {tok.key: tok}
            b.r = {}
        for b in reads:
            if b not in writes:
                _add(b.r, tok)
        return tok

    def dma(self, out, in_, sem_buf, reads=(), writes=(), extra=()):
        self.deps(reads, writes, extra, is_dma=True)
        state = self.K.dsems.setdefault(sem_buf.sem_key, [None, 0])
        if state[0] is None:
            state[0] = self.K.newsem("d_" + sem_buf.sem_key)
        state[1] += 16
        sem = state[0]
        self.prog.append(lambda e: e.dma_start(out=out, in_=in_).then_inc(sem, 16))
        tok = Tok(sem, state[1], "d_" + sem_buf.sem_key)
        for b in writes:
            if b.r:
                b.w = {}
                b.r = {}
            _add(b.w, tok)
        for b in reads:
            _add(b.r, tok)
        return tok

    def collective(self, groups, src, dst, reads, writes, name):
        self.deps(reads, writes, ())
        sem = self.K.newsem("cc_" + name)
        self.prog.append(
            lambda e: e.collective_compute("AllGather", ALU.bypass, replica_groups=groups,
                                           ins=[src], outs=[dst], dma_qos="P3").then_inc(sem))
        tok = Tok(sem, 1, "cc_" + name)
        for b in writes:
            b.w = {tok.key: tok}
            b.r = {}
        for b in reads:
            _add(b.r, tok)
        return tok


class Kern:
    def __init__(self, nc, es):
        self.nc = nc
        self.es = es
        self.nsem = 0
        self.dsems = {}
        self.pe = Eng(self, "pe", is_pe=True)
        self.act = Eng(self, "act")
        self.dve = Eng(self, "dve")
        self.pool = Eng(self, "pool")
        self.sp = Eng(self, "sp")

    def newsem(self, name):
        self.nsem += 1
        return self.es.enter_context(self.nc.semaphore(f"{name}_{self.nsem}"))

    def sb(self, name, shape, dt):
        return self.es.enter_context(self.nc.sbuf_tensor(name, list(shape), dt))

    def ps(self, name, shape, dt):
        return self.es.enter_context(self.nc.psum_tensor(name, list(shape), dt))


def build_program(TOK, DEPTH=2, dbg=False):
    NT = TOK // 128
    S = NR * TOK
    NKT = S // 128
    KCH = 8 if NKT >= 8 else NKT
    NCH = NKT // KCH
    DV = 258

    nc = bass.Bass("TRN2", target_bir_lowering=False, dynamic_dma_scratch_size=8192)
    es = ExitStack()
    K = Kern(nc, es)
    pe, act, dve, pool, sp = K.pe, K.act, K.dve, K.pool, K.sp

    def din(name, shape, dt=F32):
        return nc.dram_tensor(name, list(shape), dt, kind="ExternalInput").ap()

    def dint(name, shape, dt):
        return nc.dram_tensor(name, list(shape), dt, kind="Internal").ap()

    x_in = din("x", [TOK, D])
    w_in = din("w_in", [DEPTH, D, INW])
    w_out = din("w_out", [DEPTH, D, D])
    nw_pp = din("nw_pp", [DEPTH, 128, 16])
    bw_pp = din("bw_pp", [DEPTH, 128, 16])
    qw = din("qw", [DEPTH, 1, 128])
    kw = din("kw", [DEPTH, 1, 128])
    qws = din("qws", [DEPTH, 1, 128])
    kws = din("kws", [DEPTH, 1, 128])
    cw = din("cw", [DEPTH, 3, 512])
    snw = din("snw", [DEPTH, 1, 512])
    sb_pp = din("sb_pp", [DEPTH, 128, 4])
    wsT = din("wsT", [DEPTH, 128, 4, 128])
    fw = din("fw", [1, D])
    cos_t = din("cos_t", [TOK, 128])
    sin_t = din("sin_t", [TOK, 128])
    sel = din("sel", [8, 2])
    ident_d = din("ident", [128, 128])
    y_out = nc.dram_tensor("y", [TOK, D], F32, kind="ExternalOutput").ap()

    x1 = dint("x1", [TOK, D], F32)
    qT_s = [dint(f"qT_s{l}", [NT, 128, 8, 128], BF16) for l in range(DEPTH)]
    g_s = [dint(f"g_s{l}", [TOK, 1024], BF16) for l in range(DEPTH)]
    mT_s = [dint(f"mT_s{l}", [NT, 128, 8, 128], BF16) for l in range(DEPTH)]
    kt_src = [dint(f"kt_src{l}", [256, TOK], BF16) for l in range(DEPTH)]
    kt_dst = [dint(f"kt_dst{l}", [NR * 256, TOK], BF16) for l in range(DEPTH)]
    NH = NT // 2
    v_src = [[dint(f"v_src{l}_{i}", [128, NH * DV], BF16) for i in range(2)] for l in range(DEPTH)]
    v_dst = [[dint(f"v_dst{l}_{i}", [NR * 128, NH * DV], BF16) for i in range(2)] for l in range(DEPTH)]
    xe_src = [dint(f"xe_src{l}", [2, D], F32) for l in range(DEPTH)]
    xe_dst = [dint(f"xe_dst{l}", [NR * 2, D], F32) for l in range(DEPTH)]
    groups = [[0, 1, 2, 3], [4, 5, 6, 7]]

    x_src_b = [[Buf(f"xs{l}_{t}") for t in range(NT)] for l in range(DEPTH + 1)]
    qT_b = [[Buf(f"qTb{l}_{t}") for t in range(NT)] for l in range(DEPTH)]
    g_b = [[Buf(f"gb{l}_{t}") for t in range(NT)] for l in range(DEPTH)]
    mT_b = [[Buf(f"mTb{l}_{t}") for t in range(NT)] for l in range(DEPTH)]
    ktsrc_b = [Buf(f"ktsrc{l}") for l in range(DEPTH)]
    ktdst_b = [Buf(f"ktdst{l}") for l in range(DEPTH)]
    vsrc_b = [[Buf(f"vsrc{l}_{i}") for i in range(2)] for l in range(DEPTH)]
    vdst_b = [[Buf(f"vdst{l}_{i}") for i in range(2)] for l in range(DEPTH)]
    xesrc_b = [Buf(f"xesrc{l}") for l in range(DEPTH)]
    xedst_b = [Buf(f"xedst{l}") for l in range(DEPTH)]
    y_b = Buf("y")

    R1 = K.sb("R1", [128, 16, 2048], BF16)
    R2N = max(32768 + 0, 2 * S + NKT * DV)
    R2N = max(R2N, 16384 + 8192 + 2 * TOK + NT * DV)
    R2 = K.sb("R2", [128, R2N], BF16)
    R3N = 27 * 1024
    R3 = K.sb("R3", [128, R3N], BF16)
    xt = [K.sb(f"xt{i}", [128, D], F32) for i in range(2)]
    cs = [K.sb(f"cs{i}", [128, 2, 128], F32) for i in range(3)]
    ident = K.sb("ident_sb", [128, 128], F32)
    selt = K.sb("selt", [8, 2], F32)
    xg = xt[1][0:8, :]
    xh = xt[0][0:2, :]
    hTh = K.sb("hTh", [128, 16, 2], BF16)
    hh = K.sb("hh", [2, 512], F32)
    hh_ci = hh
    nwt = K.sb("nwt", [128, 16], F32)
    bwt = K.sb("bwt", [128, 16], F32)
    qwb = K.sb("qwb", [128, 128], F32)
    kwb = K.sb("kwb", [128, 128], F32)
    qwsb = K.sb("qwsb", [128, 128], F32)
    kwsb = K.sb("kwsb", [128, 128], F32)
    cwb = K.sb("cwb", [128, 3, 512], F32)
    snwb = K.sb("snwb", [128, 512], F32)
    sbt = K.sb("sbt", [128, 4], F32)
    wst = K.sb("wst", [128, 4, 128], BF16)
    neghalf = K.sb("neghalf", [128, 4], F32)
    ss = K.sb("ss", [128, 44], F32)
    rs = K.sb("rs", [128, 44], F32)
    rx = K.sb("rx", [128, 16], F32)
    e2x = K.sb("e2x", [128, 16], F32)
    negc = K.sb("negc", [128, 4], F32)
    stat_i = [0]

    def r3(off, n):
        return R3[:, off:off + n]
    wring = [r3(i * 8192, 8192).rearrange("p (k n) -> p k n", k=16) for i in range(2)]
    wring += [R2[:, i * 8192:(i + 1) * 8192].rearrange("p (k n) -> p k n", k=16) for i in range(2)]
    tA = [[r3(16384 + (s * 4 + i) * 1024, 1024).bitcast(F32) for i in range(4)] for s in range(2)]
    tA += [[xt[s][:, i * 512:(i + 1) * 512] for i in range(4)] for s in range(2)]
    gst = [r3(24576 + i * 512, 512) for i in range(2)]
    qst = [r3(25600 + i * 512, 512).rearrange("p (h t) -> p h t", h=4) for i in range(2)]
    mst = [r3(26624 + i * 512, 512).rearrange("p (h t) -> p h t", h=4) for i in range(2)]
    qTt = [r3(i * 1024, 1024).rearrange("p (h t) -> p h t", h=8) for i in range(2)]
    gt = [r3(2048 + i * 1024, 1024) for i in range(2)]
    mTt = [r3(4096 + i * 1024, 1024).rearrange("p (h t) -> p h t", h=8) for i in range(2)]
    PT = [r3(6144 + i * 1024, 1024) for i in range(3)]
    ao = [r3(9216 + i * 2048, 2048).bitcast(F32) for i in range(2)]
    mTa = [r3(13312 + i * 1024, 1024).rearrange("p (h t) -> p h t", h=8) for i in range(2)]
    junkB = r3(15360, 2048)
    fwb = r3(17408, 4096).bitcast(F32)
    ia = R2[:, 0:16384].bitcast(F32).rearrange("p (t c) -> p t c", c=512)
    ia16 = R2[:, 0:8192].rearrange("p (t c) -> p t c", c=512)
    ib16 = R2[:, 16384:24576].rearrange("p (t c) -> p t c", c=512)
    junkA = R2[:, 16384:16384 + 2048]
    KTst = R2[:, 24576:24576 + 2 * TOK].rearrange("p (h t) -> p h t", h=2)
    Vst = R2[:, 24576 + 2 * TOK:24576 + 2 * TOK + NT * DV].rearrange("p (t c) -> p t c", c=DV)
    KT = R2[:, 0:2 * S].rearrange("p (h s) -> p h s", h=2)
    Vaug = R2[:, 2 * S:2 * S + NKT * DV].rearrange("p (k d) -> p k d", d=DV)

    pp = [K.ps(f"pp{i}", [128, 1024], F32) for i in range(4)]
    bankb = [Buf(f"bank{i}") for i in range(8)]

    def bk(i):
        return pp[i // 2][:, (i % 2) * 512:(i % 2 + 1) * 512]

    B = {}

    def bf(name):
        if name not in B:
            B[name] = Buf(name)
        return B[name]

    def newstat():
        i = stat_i[0] % 10
        stat_i[0] += 1
        return i

    def pbc(ap):
        return ap.partition_broadcast(128).rearrange("p o n -> p (o n)")

    def rstd_from_ss(ss_ap, rs_ap, inv_n, ssb, rsb, eps_ap=None, eps_b=None):
        n = ss_ap.shape[1]
        epsv = EPS if eps_ap is None else eps_ap
        pool.op(lambda e: e.tensor_scalar(out=rs_ap, in0=ss_ap, scalar1=inv_n, scalar2=epsv,
                                          op0=ALU.mult, op1=ALU.add), reads=[ssb] + ([eps_b] if eps_b else []), writes=[rsb])
        pool.op(lambda e: e.tensor_tensor(out=rs_ap, in0=rs_ap, in1=neghalf[:, 0:n], op=ALU.pow),
                reads=[rsb, bf("neghalf")], writes=[rsb])

    sp.dma(ident[:], ident_d[:, :], bf("ident"), writes=[bf("ident")])
    sp.dma(selt[:], sel[:, :], bf("selt"), writes=[bf("selt")])
    dve.op(lambda e: e.memset(neghalf[:], -0.5), writes=[bf("neghalf")])

    wring_b = [Buf("wring0"), Buf("wring1"), None, None]
    R1_old = []
    R2_old = []
    R3_old = []

    GROUPS = [("KV", 1024), ("GA0", 1536), ("GA1", 2048), ("QA", 0), ("QB", 512),
              ("CI", 2560), ("CC", 3584), ("CG", 4096), ("CB", 3072),
              ("SV", 5120), ("SG", 5632), ("SU", 4608)]

    def slot_of(gi):
        return gi if gi < 4 else gi % 2

    def load_wgroup(l, gi):
        slot = slot_of(gi)
        c0 = GROUPS[gi][1]
        for half in range(2):
            src = w_in[l, half * 1024:(half + 1) * 1024, c0:c0 + 512].rearrange("(k p) n -> p k n", p=128)
            pool.dma(wring[slot][:, half * 8:(half + 1) * 8, :], src, wring_b[slot], writes=[wring_b[slot]])

    def run_pipeline(ntiles, make_stages):
        pl = []
        it = 0
        while True:
            if it < ntiles:
                pl.append(make_stages(it))
            live = False
            maxk = max(len(x) for x in pl)
            for k in range(maxk - 1, -1, -1):
                tt = it - k
                if 0 <= tt < len(pl) and k < len(pl[tt]):
                    pl[tt][k]()
            if it >= ntiles - 1 and all(it - tt >= len(pl[tt]) - 1 for tt in range(len(pl))):
                break
            it += 1


    for l in range(DEPTH):
        xs_ap = x_in if l == 0 else x1
        xsb = x_src_b[l]
        for (t_sb, name, src) in [(nwt, "nwt", nw_pp[l]), (bwt, "bwt", bw_pp[l]), (sbt, "sbt", sb_pp[l])]:
            sp.dma(t_sb[:], src, bf(name), writes=[bf(name)])
        sp.dma(qwb[:], pbc(qw[l]), bf("qwb"), writes=[bf("qwb")])
        sp.dma(kwb[:], pbc(kw[l]), bf("kwb"), writes=[bf("kwb")])
        sp.dma(qwsb[:], pbc(qws[l]), bf("qwsb"), writes=[bf("qwsb")])
        sp.dma(kwsb[:], pbc(kws[l]), bf("kwsb"), writes=[bf("kwsb")])
        sp.dma(snwb[:], pbc(snw[l]), bf("snwb"), writes=[bf("snwb")])
        for j in range(3):
            sp.dma(cwb[:, j, :], pbc(cw[l, j:j + 1, :]), bf("cwb"), writes=[bf("cwb")])
        pool.dma(wst[:], wsT[l], bf("wst"), writes=[bf("wst")])
        dve.op(lambda e: e.tensor_reduce(out=negc[:, 1:2], in_=qwb[:], axis=AX.X, op=ALU.max, apply_absolute_value=True),
               reads=[bf("qwb")], writes=[bf("negc")])
        dve.op(lambda e: e.tensor_reduce(out=negc[:, 2:3], in_=kwb[:], axis=AX.X, op=ALU.max, apply_absolute_value=True),
               reads=[bf("kwb")], writes=[bf("negc")])
        dve.op(lambda e: e.tensor_scalar(out=negc[:, 0:1], in0=negc[:, 1:2], scalar1=-float(HD) ** 0.5, scalar2=negc[:, 2:3],
                                         op0=ALU.mult, op1=ALU.mult), reads=[bf("negc")], writes=[bf("negc")])
        dve.op(lambda e: e.tensor_scalar(out=negc[:, 0:1], in0=negc[:, 0:1], scalar1=80.0, scalar2=0.0,
                                         op0=ALU.add, op1=ALU.min), reads=[bf("negc")], writes=[bf("negc")])

        hT_b = [alias(f"hT{l}_{t}", R1_old) for t in range(NT)]
        KTst_b = alias(f"KTst{l}", R2_old)
        Vst_b = alias(f"Vst{l}", R2_old)
        dve.op(lambda e: e.memset(Vst[:, :, 128:129], 1.0), writes=[Vst_b])
        ia_b = [None] * NT
        ib_b = [alias(f"ib{l}_{t}", R2_old) for t in range(NT)]
        ia16_b = [alias(f"iah{l}_{t}", R2_old) for t in range(NT)]
        junkA_b = ib_b[0]
        regionsA = lambda: ([(t * 2048, (t + 1) * 2048, ia_b[t]) for t in range(NT)]
                    + [(t * 1024, (t + 1) * 1024, ia16_b[t]) for t in range(NT)]
                    + [(32768 + t * 1024, 32768 + (t + 1) * 1024, ib_b[t]) for t in range(NT)]
                    + [(49152, 49152 + 4 * TOK, KTst_b), (49152 + 4 * TOK, 49152 + 4 * TOK + NT * DV * 2, Vst_b)])
        if l > 0:
            wring_b = [alias(f"wring{l}_0", R3_old), alias(f"wring{l}_1", R3_old), None, None]
        wring_b[2] = alias(f"wringx{l}_2", R2_old)
        wring_b[3] = alias(f"wringx{l}_3", R2_old)
        tA_b = [[alias(f"tA{l}_{s}_{i}", R3_old) for i in range(4)] for s in range(2)]
        gst_b = [alias(f"gst{l}_{i}", R3_old) for i in range(2)]
        qst_b = [alias(f"qst{l}_{i}", R3_old) for i in range(2)]
        mst_b = [alias(f"mst{l}_{i}", R3_old) for i in range(2)]

        load_wgroup(l, 0)
        load_wgroup(l, 1)

        sp.dma(xe_src[l][0:1, :], xs_ap[0:1, :], xesrc_b[l], reads=[xsb[0]], writes=[xesrc_b[l]])
        sp.dma(xe_src[l][1:2, :], xs_ap[TOK - 1:TOK, :], xesrc_b[l], reads=[xsb[NT - 1]], writes=[xesrc_b[l]])
        pool.collective(groups, xe_src[l].opt(), xe_dst[l].opt(), [xesrc_b[l]], [xedst_b[l]], f"xe{l}")

        rx_b = [Buf(f"rx{l}_{t}") for t in range(NT)]

        def a0_stages(t):
            sl = t % 2
            xb = bf(f"xt{sl}")

            def s0():
                sp.dma(xt[sl][:], xs_ap[t * 128:(t + 1) * 128, :], xb, reads=[xsb[t]], writes=[xb])
                si = newstat()
                ssb = bf(f"ss{si}")
                act.op(lambda e: e.activation(out=junkA, in_=xt[sl][:], func=AF.Square, accum_out=ss[:, si * 4:si * 4 + 1]),
                       reads=[xb], writes=[junkA_b, ssb])
                pool.op(lambda e: e.tensor_scalar(out=rx[:, t:t + 1], in0=ss[:, si * 4:si * 4 + 1], scalar1=1.0 / D,
                                                  scalar2=EPS, op0=ALU.mult, op1=ALU.add), reads=[ssb], writes=[rx_b[t]])
                pool.op(lambda e: e.tensor_scalar(out=e2x[:, t:t + 1], in0=rx[:, t:t + 1], scalar1=EPS, scalar2=None,
                                                  op0=ALU.mult), reads=[rx_b[t]], writes=[rx_b[t]])
                pool.op(lambda e: e.tensor_tensor(out=rx[:, t:t + 1], in0=rx[:, t:t + 1], in1=neghalf[:, 0:1], op=ALU.pow),
                        reads=[rx_b[t], bf("neghalf")], writes=[rx_b[t]])

            def s2():
                for j in range(4):
                    pb = 4 + (j % 2)
                    for i in range(4):
                        k = 4 * j + i
                        pe.op(lambda e, k=k, i=i, pb=pb: e.transpose(
                            out=bk(pb)[:, i * 128:(i + 1) * 128], in_=xt[sl][:, k * 128:(k + 1) * 128], identity=ident[:]),
                            reads=[xb, bf("ident")], writes=[bankb[pb]])
                    dve.op(lambda e, j=j, pb=pb: e.tensor_tensor(
                        out=R1[:, 4 * j:4 * j + 4, t * 128:(t + 1) * 128],
                        in0=bk(pb).rearrange("p (i t) -> p i t", i=4),
                        in1=nwt[:, 4 * j:4 * j + 4].unsqueeze(2).broadcast_to([128, 4, 128]), op=ALU.mult),
                        reads=[bankb[pb], bf("nwt")], writes=[hT_b[t]])
            return [s0, s2]

        run_pipeline(NT, a0_stages)

        xeb = bf("xt0")
        sp.dma(xg, xe_dst[l][:, :], bf("xt1"), reads=[xedst_b[l]], writes=[bf("xt1")])
        for n in range(4):
            pe.op(lambda e, n=n: e.matmul(bk(4)[0:2, :],
                                          lhsT=selt[:, :], rhs=xg[:, n * 512:(n + 1) * 512], start=True, stop=True),
                  reads=[bf("selt"), bf("xt1")], writes=[bankb[4]])
            dve.op(lambda e, n=n: e.tensor_copy(out=xh[:, n * 512:(n + 1) * 512], in_=bk(4)[0:2, :]),
                   reads=[bankb[4]], writes=[xeb])
        sh_b, rh_b = bf("ssh"), bf("rsh")
        dve.op(lambda e: e.scalar_tensor_tensor(out=junkA[0:2, :], in0=xh, scalar=1.0, in1=xh,
                                                op0=ALU.mult, op1=ALU.mult, accum_out=ss[0:2, 40:41]),
               reads=[xeb], writes=[junkA_b, sh_b])
        pool.op(lambda e: e.tensor_scalar(out=rs[0:2, 40:41], in0=ss[0:2, 40:41], scalar1=1.0 / D, scalar2=EPS,
                                          op0=ALU.mult, op1=ALU.add), reads=[sh_b], writes=[rh_b])
        pool.op(lambda e: e.tensor_tensor(out=rs[0:2, 40:41], in0=rs[0:2, 40:41], in1=neghalf[0:2, 0:1], op=ALU.pow),
                reads=[rh_b, bf("neghalf")], writes=[rh_b])
        act.op(lambda e: e.activation(out=xh, in_=xh, func=AF.Copy, scale=rs[0:2, 40:41]),
               reads=[xeb, rh_b], writes=[xeb])
        for j in range(4):
            for i in range(4):
                k = 4 * j + i
                pe.op(lambda e, k=k, i=i: e.transpose(out=bk(5)[:, i * 2:i * 2 + 2], in_=xh[:, k * 128:(k + 1) * 128],
                                                      identity=ident[0:2, 0:2]),
                      reads=[xeb, bf("ident")], writes=[bankb[5]])
            dve.op(lambda e, j=j: e.tensor_tensor(
                out=hTh[:, 4 * j:4 * j + 4, :],
                in0=bk(5)[:, 0:8].rearrange("p (i t) -> p i t", i=4),
                in1=nwt[:, 4 * j:4 * j + 4].unsqueeze(2).broadcast_to([128, 4, 2]), op=ALU.mult),
                reads=[bankb[5], bf("nwt")], writes=[bf("hTh")])


        tA_b += [[alias(f"tAx{l}_{s}_{i}", [bf(f"xt{s}")]) for i in range(4)] for s in range(2)]
        bank_ctr = [0]
        slot_ctr = [0]
        cs_ctr = [0]

        def proj_mm(gi, t):
            b = bank_ctr[0] % 4
            bank_ctr[0] += 1
            ws = slot_of(gi)
            for k in range(16):
                pe.op(lambda e, k=k, b=b, ws=ws, t=t: e.matmul(
                    bk(b), lhsT=R1[:, k, t * 128:(t + 1) * 128], rhs=wring[ws][:, k, :],
                    start=(k == 0), stop=(k == 15)),
                    reads=[hT_b[t], wring_b[ws]], writes=[bankb[b]])
            return b

        def transposes_out(src_ap, src_b, nblk, dst_sb, dst_b, scale_tab, evac_eng, dram_ap, dram_b, pbank):
            for i in range(nblk):
                pe.op(lambda e, i=i: e.transpose(out=bk(pbank)[:, i * 128:(i + 1) * 128],
                                                 in_=src_ap[:, i * 128:(i + 1) * 128], identity=ident[:]),
                      reads=[src_b, bf("ident")], writes=[bankb[pbank]])
            pin = bk(pbank)[:, 0:nblk * 128].rearrange("p (i t) -> p i t", i=nblk)
            if scale_tab is None:
                act.op(lambda e: e.activation(out=dst_sb, in_=pin, func=AF.Copy),
                       reads=[bankb[pbank]], writes=[dst_b])
            else:
                for i in range(nblk):
                    act.op(lambda e, i=i: e.activation(out=dst_sb[:, i, :], in_=pin[:, i, :], func=AF.Copy,
                                                       scale=scale_tab[:, i:i + 1]),
                           reads=[bankb[pbank], bf("bwt")], writes=[dst_b])
            if dram_ap is not None:
                sp.dma(dram_ap, dst_sb, dst_b, reads=[dst_b], writes=[dram_b])

        def load_cs(t, wtab, wtab_b, wstab, wstab_b):
            csl = cs_ctr[0] % 3
            cs_ctr[0] += 1
            csb = bf(f"cs{csl}")
            sp.dma(cs[csl][:, 0, :], cos_t[t * 128:(t + 1) * 128, :], csb, writes=[csb])
            sp.dma(cs[csl][:, 1, :], sin_t[t * 128:(t + 1) * 128, :], csb, writes=[csb])
            pool.op(lambda e: e.tensor_tensor(out=cs[csl][:, 0, :], in0=cs[csl][:, 0, :], in1=wtab[:], op=ALU.mult),
                    reads=[csb, wtab_b], writes=[csb])
            pool.op(lambda e: e.tensor_tensor(out=cs[csl][:, 1, :], in0=cs[csl][:, 1, :], in1=wstab[:], op=ALU.mult),
                    reads=[csb, wstab_b], writes=[csb])
            return csl

        def group_pre(gi, gname):
            if gi == 0:
                load_wgroup(l, 2)
                load_wgroup(l, 3)
            if gi >= 3 and gi + 1 < len(GROUPS):
                load_wgroup(l, gi + 1)
            if gname == "CI":
                for t_ in range(NT):
                    ia_b[t_] = alias(f"ia{l}_{t_}", [wring_b[2] if (t_ + 1) * 2048 <= 16384 else wring_b[3]])
            if gname in ("CI", "CC"):
                hb = 4
                for k in range(16):
                    pe.op(lambda e, k=k, ws=slot_of(gi): e.matmul(bk(hb)[0:2, :], lhsT=hTh[:, k, :], rhs=wring[ws][:, k, :],
                                                             start=(k == 0), stop=(k == 15)),
                          reads=[bf("hTh"), wring_b[slot_of(gi)]], writes=[bankb[hb]])
                if gname == "CI":
                    act.op(lambda e: e.activation(out=hh_ci[:], in_=bk(hb)[0:2, :], func=AF.Copy),
                           reads=[bankb[hb]], writes=[bf("hh")])
                else:
                    dve.op(lambda e: e.tensor_tensor(out=hh[:], in0=bk(hb)[0:2, :], in1=hh_ci[:], op=ALU.mult),
                           reads=[bankb[hb], bf("hh")], writes=[bf("hh")])

        def make_stages(t, gi, gname):
            s4 = slot_ctr[0] % 4
            s = slot_ctr[0] % 2
            slot_ctr[0] += 1
            tb, tbb = tA[s4], tA_b[s4]
            t0, t1, t2, t3 = tb
            st = {}
            v3 = lambda a: a.rearrange("p (h d) -> p h d", d=128)

            def rope_s0(ncol, wt, wtb, wst_, wstb):
                b = st["b"]
                csl = load_cs(t, wt, wtb, wst_, wstb)
                si = newstat()
                st["csl"], st["si"] = csl, si
                act.op(lambda e: e.activation(out=t0[:, 0:ncol], in_=bk(b)[:, 0:ncol], func=AF.Square),
                       reads=[bankb[b]], writes=[tbb[0]])

            def rope_s0b(ncol):
                nh = ncol // 128
                si = st["si"]
                ssb, rsb = bf(f"ss{si}"), bf(f"rs{si}")
                dve.op(lambda e: e.tensor_reduce(out=ss[:, si * 4:si * 4 + nh], in_=v3(t0[:, 0:ncol]),
                                                 axis=AX.X, op=ALU.add), reads=[tbb[0]], writes=[ssb])
                rstd_from_ss(ss[:, si * 4:si * 4 + nh], rs[:, si * 4:si * 4 + nh], 1.0 / 128, ssb, rsb,
                             eps_ap=e2x[:, t:t + 1], eps_b=rx_b[t])

            def rope_s1a(ncol):
                nh = ncol // 128
                b, si = st["b"], st["si"]
                rsb = bf(f"rs{si}")
                for h in range(nh):
                    act.op(lambda e, h=h: e.activation(out=t1[:, h * 128:(h + 1) * 128], in_=bk(b)[:, h * 128:(h + 1) * 128],
                                                       func=AF.Copy, scale=rs[:, si * 4 + h:si * 4 + h + 1]),
                           reads=[bankb[b], rsb], writes=[tbb[1]])

            def rope_s1(ncol):
                nh = ncol // 128
                b, csl, si = st["b"], st["csl"], st["si"]
                csb, rsb = bf(f"cs{csl}"), bf(f"rs{si}")
                dve.op(lambda e: e.tensor_tensor(out=v3(t2[:, 0:ncol]), in0=v3(t1[:, 0:ncol]),
                                                 in1=cs[csl][:, 0, :].unsqueeze(1).broadcast_to([128, nh, 128]),
                                                 op=ALU.mult), reads=[tbb[1], csb], writes=[tbb[2]])
                for hf in range(2):
                    o_v = t0[:, 0:ncol].rearrange("p (h a two f) -> p h a two f", a=2, two=2, f=32)[:, :, :, hf, :]
                    i_v = t1[:, 0:ncol].rearrange("p (h a two f) -> p h a two f", a=2, two=2, f=32)[:, :, :, 1 - hf, :]
                    s_v = cs[csl][:, 1, :].rearrange("p (a two f) -> p a two f", a=2, two=2)[:, :, hf, :]
                    dve.op(lambda e, o_v=o_v, i_v=i_v, s_v=s_v: e.tensor_tensor(
                        out=o_v, in0=i_v, in1=s_v.unsqueeze(1).broadcast_to([128, nh, 2, 32]), op=ALU.mult),
                        reads=[tbb[1], csb], writes=[tbb[0]])
                dve.op(lambda e: e.tensor_tensor(out=t3[:, 0:ncol], in0=t2[:, 0:ncol], in1=t0[:, 0:ncol], op=ALU.add),
                       reads=[tbb[2], tbb[0]], writes=[tbb[3]])

            def mm():
                st["b"] = proj_mm(gi, t)

            if gname == "KV":
                def s0():
                    mm()
                    b = st["b"]
                    act.op(lambda e: e.activation(
                        out=Vst[:, t, 0:258].rearrange("p (h d) -> p h d", h=2)[:, :, 0:128],
                        in_=bk(b)[:, 256:512].rearrange("p (h d) -> p h d", h=2), func=AF.Copy, scale=rx[:, t:t + 1]),
                        reads=[bankb[b], rx_b[t]], writes=[Vst_b])
                    rope_s0(256, kwb, bf("kwb"), kwsb, bf("kwsb"))
                return [s0, lambda: rope_s0b(256), lambda: rope_s1a(256), lambda: rope_s1(256),
                        lambda: transposes_out(t3[:, 0:256], tbb[3], 2, KTst[:, :, t * 128:(t + 1) * 128], KTst_b,
                                               None, act, None, None, 4 + (t % 2))]
            if gname in ("QA", "QB"):
                h0 = 0 if gname == "QA" else 4

                def s0():
                    mm()
                    rope_s0(512, qwb, bf("qwb"), qwsb, bf("qwsb"))
                return [s0, lambda: rope_s0b(512), lambda: rope_s1a(512), lambda: rope_s1(512),
                        lambda: transposes_out(t3, tbb[3], 4, qst[s][:], qst_b[s], None, act,
                                               qT_s[l][t, :, h0:h0 + 4, :], qT_b[l][t], 4 + (t % 2))]
            if gname in ("GA0", "GA1"):
                cc = 0 if gname == "GA0" else 512

                def s0():
                    mm()
                    b = st["b"]
                    act.op(lambda e: e.activation(out=gst[s], in_=bk(b), func=AF.Silu, scale=rx[:, t:t + 1]),
                           reads=[bankb[b], rx_b[t]], writes=[gst_b[s]])
                    sp.dma(g_s[l][t * 128:(t + 1) * 128, cc:cc + 512], gst[s], gst_b[s],
                           reads=[gst_b[s]], writes=[g_b[l][t]])
                return [s0]
            if gname == "CI":
                def s0():
                    mm()
                    b = st["b"]
                    act.op(lambda e: e.activation(out=ia[:, t, :], in_=bk(b), func=AF.Copy, scale=rx[:, t:t + 1]),
                           reads=[bankb[b], rx_b[t]], writes=[ia_b[t]])
                return [s0]
            if gname == "CC":
                def s0():
                    mm()
                    b = st["b"]
                    dve.op(lambda e: e.scalar_tensor_tensor(out=ia[:, t, :], in0=bk(b), scalar=rx[:, t:t + 1],
                                                            in1=ia[:, t, :], op0=ALU.mult, op1=ALU.mult),
                           reads=[bankb[b], ia_b[t], rx_b[t]], writes=[ia_b[t]])
                return [s0]
            if gname == "CG":
                def s0():
                    mm()
                    b = st["b"]
                    act.op(lambda e: e.activation(out=ib16[:, t, :], in_=bk(b), func=AF.Silu, scale=rx[:, t:t + 1]),
                           reads=[bankb[b], rx_b[t]], writes=[ib_b[t]])
                return [s0]
            if gname == "SG":
                def s0():
                    mm()
                    b = st["b"]
                    act.op(lambda e: e.activation(out=ia16[:, t, :], in_=bk(b), func=AF.Silu, scale=rx[:, t:t + 1]),
                           reads=[bankb[b], rx_b[t]], writes=[ia16_b[t], ia_b[t // 2]])
                return [s0]
            if gname == "CB":
                def s0():
                    sp.dma(t0[1:127, :], ia[0:126, t, :], tbb[0], reads=[ia_b[t]], writes=[tbb[0]])
                    sp.dma(t0[127:128, :], ia[126:127, t, :], tbb[0], reads=[ia_b[t]], writes=[tbb[0]])
                    if t > 0:
                        sp.dma(t0[0:1, :], ia[127:128, t - 1, :], tbb[0], reads=[ia_b[t - 1]], writes=[tbb[0]])
                    else:
                        sp.dma(t0[0:1, :], hh[0:1, :], tbb[0], reads=[bf("hh")], writes=[tbb[0]])
                    sp.dma(t1[1:127, :], ia[2:128, t, :], tbb[1], reads=[ia_b[t]], writes=[tbb[1]])
                    sp.dma(t1[0:1, :], ia[1:2, t, :], tbb[1], reads=[ia_b[t]], writes=[tbb[1]])
                    if t < NT - 1:
                        sp.dma(t1[127:128, :], ia[0:1, t + 1, :], tbb[1], reads=[ia_b[t + 1]], writes=[tbb[1]])
                    else:
                        sp.dma(t1[127:128, :], hh[1:2, :], tbb[1], reads=[bf("hh")], writes=[tbb[1]])
                    mm()
                    pool.op(lambda e: e.tensor_tensor(out=t2, in0=ia[:, t, :], in1=cwb[:, 1, :], op=ALU.mult),
                            reads=[ia_b[t], bf("cwb")], writes=[tbb[2]])

                def s1():
                    b = st["b"]
                    dve.op(lambda e: e.tensor_tensor(out=t0, in0=t0, in1=cwb[:, 0, :], op=ALU.mult),
                           reads=[tbb[0], bf("cwb")], writes=[tbb[0]])
                    dve.op(lambda e: e.tensor_tensor(out=t1, in0=t1, in1=cwb[:, 2, :], op=ALU.mult),
                           reads=[tbb[1], bf("cwb")], writes=[tbb[1]])
                    dve.op(lambda e: e.tensor_tensor(out=t1, in0=t1, in1=t2, op=ALU.add),
                           reads=[tbb[1], tbb[2]], writes=[tbb[1]])
                    dve.op(lambda e: e.tensor_tensor(out=t0, in0=t0, in1=t1, op=ALU.add),
                           reads=[tbb[0], tbb[1]], writes=[tbb[0]])
                    dve.op(lambda e: e.scalar_tensor_tensor(out=t0, in0=bk(b), scalar=rx[:, t:t + 1], in1=t0,
                                                            op0=ALU.mult, op1=ALU.mult),
                           reads=[bankb[b], tbb[0], rx_b[t]], writes=[tbb[0]])

                def s1b():
                    si = newstat()
                    st["si"] = si
                    ssb, rsb = bf(f"ss{si}"), bf(f"rs{si}")
                    act.op(lambda e: e.activation(out=t1, in_=t0, func=AF.Square, accum_out=ss[:, si * 4:si * 4 + 1]),
                           reads=[tbb[0]], writes=[tbb[1], ssb])
                    rstd_from_ss(ss[:, si * 4:si * 4 + 1], rs[:, si * 4:si * 4 + 1], 1.0 / 512, ssb, rsb)

                def s2():
                    si = st["si"]
                    dve.op(lambda e: e.scalar_tensor_tensor(
                        out=t3, in0=t0, scalar=rs[:, si * 4:si * 4 + 1], in1=ib16[:, t, :],
                        op0=ALU.mult, op1=ALU.mult), reads=[tbb[0], bf(f"rs{si}"), ib_b[t]], writes=[tbb[3]])
                return [s0, s1, s1b, s2,
                        lambda: transposes_out(t3, tbb[3], 4, mst[s][:], mst_b[s], bwt[:, 8:12], dve,
                                               mT_s[l][t, :, 0:4, :], mT_b[l][t], 4 + (t % 2))]
            if gname == "SV":
                def s0():
                    mm()
                    b = st["b"]
                    act.op(lambda e: e.activation(out=t0, in_=bk(b), func=AF.Gelu, scale=rx[:, t:t + 1]),
                           reads=[bankb[b], rx_b[t]], writes=[tbb[0]])
                    act.op(lambda e: e.activation(out=t1, in_=t0, func=AF.Square),
                           reads=[tbb[0]], writes=[tbb[1]])

                def s0b():
                    si = newstat()
                    st["si"] = si
                    ssb, rsb = bf(f"ss{si}"), bf(f"rs{si}")
                    dve.op(lambda e: e.tensor_reduce(out=ss[:, si * 4:si * 4 + 4], in_=v3(t1), axis=AX.X, op=ALU.add),
                           reads=[tbb[1]], writes=[ssb])
                    rstd_from_ss(ss[:, si * 4:si * 4 + 4], rs[:, si * 4:si * 4 + 4], 1.0 / 128, ssb, rsb)

                def s1():
                    si = st["si"]
                    dve.op(lambda e: e.tensor_tensor(
                        out=v3(t2), in0=v3(t0),
                        in1=rs[:, si * 4:si * 4 + 4].unsqueeze(2).broadcast_to([128, 4, 128]), op=ALU.mult),
                        reads=[tbb[0], bf(f"rs{si}")], writes=[tbb[2]])

                def s2():
                    dve.op(lambda e: e.tensor_tensor(out=ib16[:, t, :], in0=t2, in1=snwb[:], op=ALU.mult),
                           reads=[tbb[2], bf("snwb")], writes=[ib_b[t]])
                return [s0, s0b, s1, s2]
            if gname == "SU":
                sbk = 6 + (t % 2)

                def s0():
                    for g in range(4):
                        pe.op(lambda e, g=g: e.matmul(bk(sbk)[:, g * 128:(g + 1) * 128], lhsT=wst[:, g, :],
                                                      rhs=ib16[:, t, g * 128:(g + 1) * 128], start=True, stop=True),
                              reads=[bf("wst"), ib_b[t]], writes=[bankb[sbk]])
                    mm()
                    b = st["b"]
                    act.op(lambda e: e.activation(out=t0, in_=bk(b), func=AF.Gelu, scale=rx[:, t:t + 1]),
                           reads=[bankb[b], rx_b[t]], writes=[tbb[0]])

                def s1():
                    for g in range(4):
                        dve.op(lambda e, g=g: e.scalar_tensor_tensor(
                            out=t1[:, g * 128:(g + 1) * 128], in0=bk(sbk)[:, g * 128:(g + 1) * 128],
                            scalar=sbt[:, g:g + 1], in1=t0[:, g * 128:(g + 1) * 128], op0=ALU.add, op1=ALU.mult),
                            reads=[bankb[sbk], bf("sbt"), tbb[0]], writes=[tbb[1]])

                def s1b():
                    si = newstat()
                    st["si"] = si
                    ssb, rsb = bf(f"ss{si}"), bf(f"rs{si}")
                    act.op(lambda e: e.activation(out=t2, in_=t1, func=AF.Square, accum_out=ss[:, si * 4:si * 4 + 1]),
                           reads=[tbb[1]], writes=[tbb[2], ssb])
                    rstd_from_ss(ss[:, si * 4:si * 4 + 1], rs[:, si * 4:si * 4 + 1], 1.0 / 512, ssb, rsb)

                def s2():
                    si = st["si"]
                    dve.op(lambda e: e.scalar_tensor_tensor(
                        out=t3, in0=t1, scalar=rs[:, si * 4:si * 4 + 1], in1=ia16[:, t, :],
                        op0=ALU.mult, op1=ALU.mult), reads=[tbb[1], bf(f"rs{si}"), ia16_b[t]], writes=[tbb[3]])
                return [s0, s1, s1b, s2,
                        lambda: transposes_out(t3, tbb[3], 4, mst[s][:], mst_b[s], bwt[:, 12:16], dve,
                                               mT_s[l][t, :, 4:8, :], mT_b[l][t], 4 + (t % 2))]
            raise AssertionError(gname)

        def olds_for(lo, hi):
            return [b_ for (a_, z_, b_) in regionsA() if a_ < hi and lo < z_]

        pv = lambda r_: (r_ + 2) % NR
        KTv_b = [[None] * NR for _ in range(2)]
        Vv_b = [None] * NR

        def load_kt(h, r_):
            lo = (h * S + r_ * TOK) * 2
            KTv_b[h][r_] = alias(f"KT{l}_{h}_{r_}", olds_for(lo, lo + 2 * TOK))
            sp.dma(KT[:, h, r_ * TOK:(r_ + 1) * TOK], kt_dst[l][(r_ * 2 + h) * 128:(r_ * 2 + h + 1) * 128, :],
                   KTv_b[h][r_], reads=[ktdst_b[l]], writes=[KTv_b[h][r_]])

        def load_v(r_):
            p_ = pv(r_)
            lo = (2 * S + p_ * NT * DV) * 2
            Vv_b[p_] = alias(f"Vv{l}_{p_}", olds_for(lo, lo + NT * DV * 2))
            for i_ in range(2):
                o_ = 2 * S + p_ * NT * DV + i_ * NH * DV
                sp.dma(R2[:, o_:o_ + NH * DV], v_dst[l][i_][r_ * 128:(r_ + 1) * 128, :],
                       Vv_b[p_], reads=[vdst_b[l][i_]], writes=[Vv_b[p_]])

        def group_post(gi, gname):
            if gname == "CB":
                for r_ in range(NR):
                    load_kt(1, r_)
                load_v(0)
                load_v(1)
            if gname == "KV":
                sp.dma(kt_src[l].rearrange("(h d) t -> d h t", h=2), KTst, KTst_b, reads=[KTst_b], writes=[ktsrc_b[l]])
                for i_ in range(2):
                    o_ = 24576 + 2 * TOK + i_ * NH * DV
                    sp.dma(v_src[l][i_][:, :], R2[:, o_:o_ + NH * DV], Vst_b, reads=[Vst_b], writes=[vsrc_b[l][i_]])
            if gname == "KV":
                pool.collective(groups, kt_src[l].opt(), kt_dst[l].opt(), [ktsrc_b[l]], [ktdst_b[l]], f"kt{l}")
                for i_ in range(2):
                    pool.collective(groups, v_src[l][i_].opt(), v_dst[l][i_].opt(), [vsrc_b[l][i_]], [vdst_b[l][i_]], f"v{l}_{i_}")

        items = [(gi, gname, t) for gi, (gname, c0) in enumerate(GROUPS) for t in range(NT)]

        def make_item(idx):
            gi, gname, t = items[idx]
            stages = list(make_stages(t, gi, gname))
            if t == 0:
                f0 = stages[0]
                stages[0] = (lambda f0=f0, gi=gi, gname=gname: (group_pre(gi, gname), f0()))
            if t == NT - 1:
                fl = stages[-1]
                stages[-1] = (lambda fl=fl, gi=gi, gname=gname: (fl(), group_post(gi, gname)))
            return stages

        run_pipeline(len(items), make_item)

        for s_ in range(2):
            B[f"xt{s_}"] = alias(f"xtB{l}_{s_}", tA_b[2 + s_])
        R1_old = hT_b
        R2_old = ia_b + ib_b + ia16_b + [KTst_b, Vst_b]
        R3_old = wring_b + [x for s_ in tA_b[0:2] for x in s_] + gst_b + qst_b + mst_b
        wout_b = [alias(f"wout{l}_{q}", R1_old) for q in range(4)]
        qTt_b = [alias(f"qTt{l}_{i}", R3_old) for i in range(2)]
        gt_b = [alias(f"gt{l}_{i}", R3_old) for i in range(2)]
        mTt_b = [alias(f"mTt{l}_{i}", R3_old) for i in range(2)]
        PT_b = [alias(f"PT{l}_{i}", R3_old) for i in range(3)]
        ao_b = [alias(f"ao{l}_{i}", R3_old) for i in range(2)]
        mTa_b = [alias(f"mTa{l}_{i}", R3_old) for i in range(2)]
        junkB_b = alias(f"junkB{l}", R3_old)
        fwb_b = alias(f"fwb{l}", R3_old)

        def load_wout(half):
            for n in (2 * half, 2 * half + 1):
                for hf in range(2):
                    src = w_out[l, hf * 1024:(hf + 1) * 1024, n * 512:(n + 1) * 512].rearrange("(k p) c -> p k c", p=128)
                    pool.dma(R1[:, hf * 8:(hf + 1) * 8, n * 512:(n + 1) * 512], src, wout_b[n], writes=[wout_b[n]])

        KV_REST = True
        last = (l == DEPTH - 1)
        if last:
            sp.dma(fwb, pbc(fw), fwb_b, writes=[fwb_b])

        def loads_B(t):
            sl = t % 2
            sp.dma(qTt[sl][:], qT_s[l][t], qTt_b[sl], reads=[qT_b[l][t]], writes=[qTt_b[sl]])
            sp.dma(gt[sl], g_s[l][t * 128:(t + 1) * 128, :], gt_b[sl], reads=[g_b[l][t]], writes=[gt_b[sl]])
            sp.dma(mTt[sl][:], mT_s[l][t], mTt_b[sl], reads=[mT_b[l][t]], writes=[mTt_b[sl]])
            xb = bf(f"xt{sl}")
            sp.dma(xt[sl][:], xs_ap[t * 128:(t + 1) * 128, :], xb, reads=[xsb[t]], writes=[xb])

        def post1(t):
            sl = t % 2
            a = ao[sl]
            si = newstat()
            ssb, rsb = bf(f"ss{si}"), bf(f"rs{si}")
            dve.op(lambda e: e.scalar_tensor_tensor(out=junkB[:, 0:1024], in0=a, scalar=1.0, in1=a,
                                                    op0=ALU.mult, op1=ALU.mult, accum_out=ss[:, si * 4:si * 4 + 1]),
                   reads=[ao_b[sl]], writes=[junkB_b, ssb])
            rstd_from_ss(ss[:, si * 4:si * 4 + 1], rs[:, si * 4:si * 4 + 1], 1.0 / 1024, ssb, rsb)
            dve.op(lambda e: e.scalar_tensor_tensor(out=a, in0=a, scalar=rs[:, si * 4:si * 4 + 1], in1=gt[sl],
                                                    op0=ALU.mult, op1=ALU.mult),
                   reads=[ao_b[sl], rsb, gt_b[sl]], writes=[ao_b[sl]])
            for j in range(2):
                pb = 6 + j
                for i in range(4):
                    k = 4 * j + i
                    pe.op(lambda e, k=k, i=i, pb=pb: e.transpose(out=bk(pb)[:, i * 128:(i + 1) * 128],
                                                                 in_=a[:, k * 128:(k + 1) * 128], identity=ident[:]),
                          reads=[ao_b[sl], bf("ident")], writes=[bankb[pb]])
                dve.op(lambda e, j=j, pb=pb: e.tensor_tensor(
                    out=mTa[sl][:, 4 * j:4 * j + 4, :], in0=bk(pb).rearrange("p (i t) -> p i t", i=4),
                    in1=bwt[:, 4 * j:4 * j + 4].unsqueeze(2).broadcast_to([128, 4, 128]), op=ALU.mult),
                    reads=[bankb[pb], bf("bwt")], writes=[mTa_b[sl]])

        def post2(t):
            sl = t % 2
            xb = bf(f"xt{sl}")
            for n in range(4):
                pb = 6 + (n % 2)
                for k in range(16):
                    lhsT = mTa[sl][:, k, :] if k < 8 else mTt[sl][:, k - 8, :]
                    lb = mTa_b[sl] if k < 8 else mTt_b[sl]
                    pe.op(lambda e, k=k, n=n, pb=pb, lhsT=lhsT: e.matmul(
                        bk(pb), lhsT=lhsT, rhs=R1[:, k, n * 512:(n + 1) * 512], start=(k == 0), stop=(k == 15)),
                        reads=[lb, wout_b[n]], writes=[bankb[pb]])
                dve.op(lambda e, n=n, pb=pb: e.tensor_tensor(out=xt[sl][:, n * 512:(n + 1) * 512], in0=bk(pb),
                                                             in1=xt[sl][:, n * 512:(n + 1) * 512], op=ALU.add),
                       reads=[bankb[pb], xb], writes=[xb])
            if not last:
                sp.dma(x1[t * 128:(t + 1) * 128, :], xt[sl][:], xb, reads=[xb], writes=[x_src_b[l + 1][t]])
            else:
                si = newstat()
                ssb, rsb = bf(f"ss{si}"), bf(f"rs{si}")
                dve.op(lambda e: e.scalar_tensor_tensor(out=junkB, in0=xt[sl][:], scalar=1.0, in1=xt[sl][:],
                                                        op0=ALU.mult, op1=ALU.mult, accum_out=ss[:, si * 4:si * 4 + 1]),
                       reads=[xb], writes=[junkB_b, ssb])
                rstd_from_ss(ss[:, si * 4:si * 4 + 1], rs[:, si * 4:si * 4 + 1], 1.0 / D, ssb, rsb)
                dve.op(lambda e: e.scalar_tensor_tensor(out=xt[sl][:], in0=xt[sl][:], scalar=rs[:, si * 4:si * 4 + 1],
                                                        in1=fwb, op0=ALU.mult, op1=ALU.mult),
                       reads=[xb, rsb, fwb_b], writes=[xb])
                sp.dma(y_out[t * 128:(t + 1) * 128, :], xt[sl][:], xb, reads=[xb], writes=[y_b])

        NC2 = NKT // 2
        steps = [(t, g, c) for t in range(NT) for g in (1, 0) for c in range(NC2)]
        scale = float(HD) ** -0.5

        def emit_qk(i):
            t, g, c = steps[i]
            sl = t % 2
            sb_ = i % 2
            for j in range(2):
                kt_i = c * 2 + j
                pe.op(lambda e, j=j, kt_i=kt_i, g=g, sl=sl, sb_=sb_: e.matmul(
                    pp[sb_][:, j * 512:(j + 1) * 512], lhsT=KT[:, g, kt_i * 128:(kt_i + 1) * 128],
                    rhs=qTt[sl][:, 4 * g:4 * g + 4, :].rearrange("p h t -> p (h t)"), start=True, stop=True),
                    reads=[KTv_b[g][kt_i // NT], qTt_b[sl]], writes=[bankb[2 * sb_ + j]])

        def emit_exp(i):
            sb_ = i % 2
            p = i % 3
            act.op(lambda e: e.activation(out=PT[p], in_=pp[sb_][:, :], func=AF.Exp, scale=scale, bias=negc[:, 0:1]),
                   reads=[bankb[2 * sb_], bankb[2 * sb_ + 1], bf("negc")], writes=[PT_b[p]])

        def emit_pv(i):
            t, g, c = steps[i]
            p = i % 3
            for j in range(2):
                kt_i = c * 2 + j
                for h4 in range(4):
                    ob = 4 + h4 // 2
                    c0_ = (h4 % 2) * 256
                    pe.op(lambda e, j=j, kt_i=kt_i, h4=h4, ob=ob, c0_=c0_: e.matmul(
                        bk(ob)[:, c0_:c0_ + 129], lhsT=PT[p][:, j * 512 + h4 * 128:j * 512 + (h4 + 1) * 128],
                        rhs=Vaug[:, pv(kt_i // NT) * NT + kt_i % NT, 128 * g:128 * g + 129],
                        start=(c == 0 and j == 0 and h4 % 2 == 0), stop=(c == NC2 - 1 and j == 1),
                        skip_group_check=True),
                        reads=[PT_b[p], Vv_b[pv(kt_i // NT)]], writes=[bankb[ob]])
            if c == NC2 - 1:
                sl = t % 2
                rb = bf("rden")
                for h4 in range(4):
                    ob = 4 + h4 // 2
                    c0_ = (h4 % 2) * 256
                    h = 4 * g + h4
                    dcol = c0_ + (128 if g == 0 else 0)
                    vcol = c0_ + (0 if g == 0 else 1)
                    dve.op(lambda e, ob=ob, dcol=dcol, h4=h4: e.reciprocal(out=rs[:, 41 + (h4 % 2):42 + (h4 % 2)],
                                                                         in_=bk(ob)[:, dcol:dcol + 1]),
                           reads=[bankb[ob]], writes=[rb])
                    dve.op(lambda e, ob=ob, vcol=vcol, h=h, h4=h4: e.tensor_scalar(
                        out=ao[sl][:, h * 128:(h + 1) * 128], in0=bk(ob)[:, vcol:vcol + 128],
                        scalar1=rs[:, 41 + (h4 % 2):42 + (h4 % 2)], scalar2=None,
                        op0=ALU.mult), reads=[bankb[ob], rb], writes=[ao_b[sl]])

        loads_B(0)
        load_v(2)
        load_v(3)
        for r_ in range(NR):
            load_kt(0, r_)
        emit_qk(0)
        if len(steps) > 1:
            emit_qk(1)
        for i, (t, g, c) in enumerate(steps):
            emit_exp(i)
            if i + 2 < len(steps):
                emit_qk(i + 2)
            emit_pv(i)
            st_ = i % (2 * NC2)
            if t == 0 and st_ == min(8, 2 * NC2 - 4):
                load_wout(0)
            if t == 0 and st_ == min(24, 2 * NC2 - 3):
                load_wout(1)
            if st_ == 1 and t > 0:
                post1(t - 1)
            if st_ == 3 and t > 0:
                post2(t - 1)
            if st_ == 4 and t + 1 < NT:
                loads_B(t + 1)
        post1(NT - 1)
        post2(NT - 1)

        R1_old = wout_b
        R2_old = [b_ for row in KTv_b for b_ in row] + list(Vv_b)
        R3_old = qTt_b + gt_b + mTt_b + PT_b + ao_b + mTa_b + [junkB_b, fwb_b]

    for tok in y_b.w.values():
        sp.wait(tok)

    with nc.Block() as block:
        @block.tensor
        def _(e):
            for f in pe.prog:
                f(e)

        @block.scalar
        def _(e):
            for f in act.prog:
                f(e)

        @block.vector
        def _(e):
            for f in dve.prog:
                f(e)

        @block.gpsimd
        def _(e):
            for f in pool.prog:
                f(e)

        @block.sync
        def _(e):
            for f in sp.prog:
                f(e)
    return nc, es


def rope_tables(S):
    GRID_W = 64
    rows = S // GRID_W
    row = np.repeat(np.arange(rows, dtype=np.float32), GRID_W)
    col = np.tile(np.arange(GRID_W, dtype=np.float32), rows)
    inv_freq = (10000.0 ** (-np.arange(0, 64, 2, dtype=np.float32) / 64)).astype(np.float32)
    ang = np.stack([row, col], axis=-1)[:, :, None] * inv_freq
    ang = np.broadcast_to(ang[:, :, None, :], (S, 2, 2, 32)).reshape(S, 128)
    cos = np.cos(ang).astype(np.float32)
    sin = np.sin(ang).astype(np.float32).reshape(S, 2, 2, 32).copy()
    sin[:, :, 0, :] *= -1.0
    return cos, sin.reshape(S, 128)


def swap_halves(w, DEPTH):
    w = np.asarray(w, dtype=np.float32).reshape(DEPTH, 2, 2, 32)
    return np.ascontiguousarray(w[:, :, ::-1, :]).reshape(DEPTH, 1, 128)


def host_consts():
    ident = np.eye(128, dtype=np.float32)
    shf = np.zeros((128, 4, 128), np.float32)
    for t in range(128):
        if t - 1 >= 0:
            shf[t - 1, 0, t] = 1.0
        if t + 1 < 128:
            shf[t + 1, 1, t] = 1.0
    shf[127, 2, 0] = 1.0
    shf[0, 3, 127] = 1.0
    e2 = np.zeros((2, 2, 128), np.float32)
    e2[0, 0, 0] = 1.0
    e2[1, 1, 127] = 1.0
    return ident, shf, e2


_CACHE = {}


def run(inputs, TOK, DEPTH=2):
    x = np.ascontiguousarray(inputs["x"], dtype=np.float32)
    Bsz, S, _ = x.shape
    assert Bsz == 2 and S == NR * TOK
    key = (TOK, DEPTH)
    if key not in _CACHE:
        _CACHE[key] = build_program(TOK, DEPTH)
    nc, es = _CACHE[key]
    f32 = lambda a: np.ascontiguousarray(np.asarray(a, dtype=np.float32))
    cos, sin = rope_tables(S)
    ident, shf, e2 = host_consts()
    pp_layout = lambda w: f32(np.asarray(w).reshape(DEPTH, 16, 128).transpose(0, 2, 1))
    common = {
        "w_in": f32(inputs["w_in"]), "w_out": f32(inputs["w_out"]),
        "nw_pp": pp_layout(inputs["norm_w"]), "bw_pp": pp_layout(inputs["branch_norm_w"]),
        "qw": f32(inputs["q_norm_w"]).reshape(DEPTH, 1, 128), "kw": f32(inputs["k_norm_w"]).reshape(DEPTH, 1, 128),
        "qws": swap_halves(inputs["q_norm_w"], DEPTH), "kws": swap_halves(inputs["k_norm_w"], DEPTH),
        "cw": f32(np.asarray(inputs["conv_w"]).transpose(0, 2, 1)),
        "snw": f32(inputs["sgu_norm_w"]).reshape(DEPTH, 1, 512),
        "sb_pp": f32(np.asarray(inputs["sgu_b"]).transpose(0, 2, 1)),
        "wsT": f32(np.asarray(inputs["sgu_w"]).transpose(0, 3, 1, 2)),
        "fw": f32(inputs["final_norm_w"]).reshape(1, D),
        "ident": ident,
    }
    in_maps = []
    for c in range(NCORES):
        b, r = c // NR, c % NR
        sel = np.zeros((8, 2), np.float32)
        if r > 0:
            sel[(r - 1) * 2 + 1, 0] = 1.0
        if r < NR - 1:
            sel[(r + 1) * 2 + 0, 1] = 1.0
        m = dict(common)
        m["x"] = np.ascontiguousarray(x[b, r * TOK:(r + 1) * TOK, :])
        m["cos_t"] = np.ascontiguousarray(cos[r * TOK:(r + 1) * TOK])
        m["sin_t"] = np.ascontiguousarray(sin[r * TOK:(r + 1) * TOK])
        m["sel"] = sel
        in_maps.append(m)
    res = run_bass_kernel_spmd(nc, in_maps, core_ids=list(range(NCORES)))
    out = np.empty((Bsz, S, D), np.float32)
    for c in range(NCORES):
        b, r = c // NR, c % NR
        out[b, r * TOK:(r + 1) * TOK, :] = np.asarray(res.results[c]["y"], dtype=np.float32)
    return out


def kernel(**inputs):
    S = np.asarray(inputs["x"]).shape[1]
    return run(inputs, S // NR, DEPTH=2)
```

```python
import re
import numpy as np
from contextlib import ExitStack
import concourse.bass as bass
import concourse.mybir as mybir
from concourse.bass_utils import run_bass_kernel_spmd

F32 = mybir.dt.float32
BF16 = mybir.dt.bfloat16
AF = mybir.ActivationFunctionType
ALU = mybir.AluOpType
AX = mybir.AxisListType

D = 2048
INW = 6144
HD = 128
EPS = 1e-6
NCORES = 8
NR = 4


class Tok:
    __slots__ = ("sem", "val", "key")

    def __init__(self, sem, val, key):
        self.sem, self.val, self.key = sem, val, key


def _add(d, tok):
    o = d.get(tok.key)
    if o is None or o.val < tok.val:
        d[tok.key] = tok


class Buf:
    def __init__(self, name, sem_key=None):
        self.name = name
        self.w = {}
        self.r = {}
        self.sem_key = sem_key or name


def alias(name, olds):
    b = Buf(name, sem_key=re.sub(r"^([A-Za-z]+)\d+", r"\1", name))
    for o in olds:
        for t in list(o.w.values()) + list(o.r.values()):
            _add(b.r, t)
    return b


class Eng:
    def __init__(self, K, name, is_pe=False):
        self.K = K
        self.name = name
        self.key = "e_" + name
        self.sem = K.newsem("p_" + name)
        self.cnt = 0
        self.waited = {}
        self.is_pe = is_pe
        self.prog = []

    def wait(self, tok):
        if self.waited.get(tok.key, 0) >= tok.val:
            return
        self.waited[tok.key] = tok.val
        sem, val = tok.sem, tok.val
        self.prog.append(lambda e: e.wait_ge(sem, val))

    def deps(self, reads, writes, extra, is_dma=False):
        for b in reads:
            for tok in b.w.values():
                if tok.key == self.key and self.is_pe:
                    continue
                self.wait(tok)
        for b in writes:
            for tok in b.w.values():
                if tok.key == self.key:
                    continue
                if is_dma and tok.key.startswith("d_"):
                    continue
                self.wait(tok)
            for tok in b.r.values():
                if tok.key == self.key:
                    continue
                self.wait(tok)
        for tok in extra:
            self.wait(tok)

    def op(self, fn, reads=(), writes=(), extra=()):
        self.deps(reads, writes, extra)
        self.cnt += 1
        sem = self.sem
        self.prog.append(lambda e: fn(e).then_inc(sem, 1))
        tok = Tok(sem, self.cnt, self.key)
        for b in writes:
            b.w = {tok.key: tok}
            b.r = {}
        for b in reads:
            if b not in writes:
                _add(b.r, tok)
        return tok

    def dma(self, out, in_, sem_buf, reads=(), writes=(), extra=()):
        self.deps(reads, writes, extra, is_dma=True)
        state = self.K.dsems.setdefault(sem_buf.sem_key, [None, 0])
        if state[0] is None:
            state[0] = self.K.newsem("d_" + sem_buf.sem_key)
        state[1] += 16
        sem = state[0]
        self.prog.append(lambda e: e.dma_start(out=out, in_=in_).then_inc(sem, 16))
        tok = Tok(sem, state[1], "d_" + sem_buf.sem_key)
        for b in writes:
            if b.r:
                b.w = {}
                b.r = {}
            _add(b.w, tok)
        for b in reads:
            _add(b.r, tok)
        return tok

    def collective(self, groups, src, dst, reads, writes, name):
        self.deps(reads, writes, ())
        sem = self.K.newsem("cc_" + name)
        self.prog.append(
            lambda e: e.collective_compute("AllGather", ALU.bypass, replica_groups=groups,
                                           ins=[src], outs=[dst], dma_qos="P3").then_inc(sem))
        tok = Tok(sem, 1, "cc_" + name)
        for b in writes:
            b.w = {tok.key: tok}
            b.r = {}
        for b in reads:
            _add(b.r, tok)
        return tok


class Kern:
    def __init__(self, nc, es):
        self.nc = nc
        self.es = es
        self.nsem = 0
        self.dsems = {}
        self.pe = Eng(self, "pe", is_pe=True)
        self.act = Eng(self, "act")
        self.dve = Eng(self, "dve")
        self.pool = Eng(self, "pool")
        self.sp = Eng(self, "sp")

    def newsem(self, name):
        self.nsem += 1
        return self.es.enter_context(self.nc.semaphore(f"{name}_{self.nsem}"))

    def sb(self, name, shape, dt):
        return self.es.enter_context(self.nc.sbuf_tensor(name, list(shape), dt))

    def ps(self, name, shape, dt):
        return self.es.enter_context(self.nc.psum_tensor(name, list(shape), dt))


def build_program(TOK, DEPTH=2, dbg=False):
    NT = TOK // 128
    S = NR * TOK
    NKT = S // 128
    KCH = 8 if NKT >= 8 else NKT
    NCH = NKT // KCH
    DV = 258

    nc = bass.Bass("TRN2", target_bir_lowering=False, dynamic_dma_scratch_size=8192)
    es = ExitStack()
    K = Kern(nc, es)
    pe, act, dve, pool, sp = K.pe, K.act, K.dve, K.pool, K.sp

    def din(name, shape, dt=F32):
        return nc.dram_tensor(name, list(shape), dt, kind="ExternalInput").ap()

    def dint(name, shape, dt):
        return nc.dram_tensor(name, list(shape), dt, kind="Internal").ap()

    x_in = din("x", [TOK, D])
    w_in = din("w_in", [DEPTH, D, INW])
    w_out = din("w_out", [DEPTH, D, D])
    nw_pp = din("nw_pp", [DEPTH, 128, 16])
    bw_pp = din("bw_pp", [DEPTH, 128, 16])
    qw = din("qw", [DEPTH, 1, 128])
    kw = din("kw", [DEPTH, 1, 128])
    qws = din("qws", [DEPTH, 1, 128])
    kws = din("kws", [DEPTH, 1, 128])
    cw = din("cw", [DEPTH, 3, 512])
    snw = din("snw", [DEPTH, 1, 512])
    sb_pp = din("sb_pp", [DEPTH, 128, 4])
    wsT = din("wsT", [DEPTH, 128, 4, 128])
    fw = din("fw", [1, D])
    cs_t = din("cs_t", [TOK, 2, 128])
    sel = din("sel", [8, 2])
    ident_d = din("ident", [128, 128])
    y_out = nc.dram_tensor("y", [TOK, D], F32, kind="ExternalOutput").ap()

    x1 = dint("x1", [TOK, D], F32)
    qT_s = [dint(f"qT_s{l}", [NT, 128, 8, 128], BF16) for l in range(DEPTH)]
    g_s = [dint(f"g_s{l}", [TOK, 1024], BF16) for l in range(DEPTH)]
    mT_s = [dint(f"mT_s{l}", [NT, 128, 8, 128], BF16) for l in range(DEPTH)]
    kt_src = [dint(f"kt_src{l}", [256, TOK], BF16) for l in range(DEPTH)]
    kt_dst = [dint(f"kt_dst{l}", [NR * 256, TOK], BF16) for l in range(DEPTH)]
    NH = NT // 2
    v_src = [[dint(f"v_src{l}_{i}", [128, NH * DV], BF16) for i in range(2)] for l in range(DEPTH)]
    v_dst = [[dint(f"v_dst{l}_{i}", [NR * 128, NH * DV], BF16) for i in range(2)] for l in range(DEPTH)]
    xe_src = [dint(f"xe_src{l}", [2, D], F32) for l in range(DEPTH)]
    xe_dst = [dint(f"xe_dst{l}", [NR * 2, D], F32) for l in range(DEPTH)]
    groups = [[0, 1, 2, 3], [4, 5, 6, 7]]

    x_src_b = [[Buf(f"xs{l}_{t}") for t in range(NT)] for l in range(DEPTH + 1)]
    qT_b = [[Buf(f"qTb{l}_{t}") for t in range(NT)] for l in range(DEPTH)]
    g_b = [[Buf(f"gb{l}_{t}") for t in range(NT)] for l in range(DEPTH)]
    mT_b = [[Buf(f"mTb{l}_{t}") for t in range(NT)] for l in range(DEPTH)]
    ktsrc_b = [Buf(f"ktsrc{l}") for l in range(DEPTH)]
    ktdst_b = [Buf(f"ktdst{l}") for l in range(DEPTH)]
    vsrc_b = [[Buf(f"vsrc{l}_{i}") for i in range(2)] for l in range(DEPTH)]
    vdst_b = [[Buf(f"vdst{l}_{i}") for i in range(2)] for l in range(DEPTH)]
    xesrc_b = [Buf(f"xesrc{l}") for l in range(DEPTH)]
    xedst_b = [Buf(f"xedst{l}") for l in range(DEPTH)]
    y_b = Buf("y")

    R1 = K.sb("R1", [128, 16, 2048], BF16)
    R2N = max(32768 + 0, 2 * S + NKT * DV)
    R2N = max(R2N, 16384 + 8192 + 2 * TOK + NT * DV)
    R2 = K.sb("R2", [128, R2N], BF16)
    R3N = 27 * 1024
    R3 = K.sb("R3", [128, R3N], BF16)
    xt = [K.sb(f"xt{i}", [128, D], F32) for i in range(2)]
    cs = [K.sb(f"cs{i}", [128, 2, 128], F32) for i in range(3)]
    ident = K.sb("ident_sb", [128, 128], F32)
    selt = K.sb("selt", [8, 2], F32)
    xg = xt[1][0:8, :]
    xh = xt[0][0:2, :]
    hTh = K.sb("hTh", [128, 16, 2], BF16)
    hh = K.sb("hh", [2, 512], F32)
    hh_ci = hh
    nwt = K.sb("nwt", [128, 16], F32)
    bwt = K.sb("bwt", [128, 16], F32)
    qwb = K.sb("qwb", [128, 128], F32)
    kwb = K.sb("kwb", [128, 128], F32)
    qwsb = K.sb("qwsb", [128, 128], F32)
    kwsb = K.sb("kwsb", [128, 128], F32)
    cwb = K.sb("cwb", [128, 3, 512], F32)
    snwb = K.sb("snwb", [128, 512], F32)
    sbt = K.sb("sbt", [128, 4], F32)
    wst = K.sb("wst", [128, 4, 128], BF16)
    neghalf = K.sb("neghalf", [128, 4], F32)
    ss = K.sb("ss", [128, 44], F32)
    rs = K.sb("rs", [128, 44], F32)
    rx = K.sb("rx", [128, 16], F32)
    e2x = K.sb("e2x", [128, 16], F32)
    negc = K.sb("negc", [128, 4], F32)
    stat_i = [0]

    def r3(off, n):
        return R3[:, off:off + n]
    wring = [r3(i * 8192, 8192).rearrange("p (k n) -> p k n", k=16) for i in range(2)]
    wring += [R2[:, i * 8192:(i + 1) * 8192].rearrange("p (k n) -> p k n", k=16) for i in range(2)]
    tA = [[r3(16384 + (s * 4 + i) * 1024, 1024).bitcast(F32) for i in range(4)] for s in range(2)]
    tA += [[xt[s][:, i * 512:(i + 1) * 512] for i in range(4)] for s in range(2)]
    gst = [r3(24576 + i * 512, 512) for i in range(2)]
    qst = [r3(25600 + i * 512, 512).rearrange("p (h t) -> p h t", h=4) for i in range(2)]
    mst = [r3(26624 + i * 512, 512).rearrange("p (h t) -> p h t", h=4) for i in range(2)]
    qTt = [r3(i * 1024, 1024).rearrange("p (h t) -> p h t", h=8) for i in range(2)]
    gt = [r3(2048 + i * 1024, 1024) for i in range(2)]
    mTt = [r3(4096 + i * 1024, 1024).rearrange("p (h t) -> p h t", h=8) for i in range(2)]
    PT = [r3(6144 + i * 1024, 1024) for i in range(3)]
    ao = [r3(9216 + i * 2048, 2048).bitcast(F32) for i in range(2)]
    mTa = [r3(13312 + i * 1024, 1024).rearrange("p (h t) -> p h t", h=8) for i in range(2)]
    junkB = r3(15360, 2048)
    fwb = r3(17408, 4096).bitcast(F32)
    ia = R2[:, 0:16384].bitcast(F32).rearrange("p (t c) -> p t c", c=512)
    ia16 = R2[:, 0:8192].rearrange("p (t c) -> p t c", c=512)
    ib16 = R2[:, 16384:24576].rearrange("p (t c) -> p t c", c=512)
    junkA = R2[:, 16384:16384 + 2048]
    KTst = R2[:, 24576:24576 + 2 * TOK].rearrange("p (h t) -> p h t", h=2)
    Vst = R2[:, 24576 + 2 * TOK:24576 + 2 * TOK + NT * DV].rearrange("p (t c) -> p t c", c=DV)
    KT = R2[:, 0:2 * S].rearrange("p (h s) -> p h s", h=2)
    Vaug = R2[:, 2 * S:2 * S + NKT * DV].rearrange("p (k d) -> p k d", d=DV)

    pp = [K.ps(f"pp{i}", [128, 1024], F32) for i in range(4)]
    bankb = [Buf(f"bank{i}") for i in range(8)]

    def bk(i):
        return pp[i // 2][:, (i % 2) * 512:(i % 2 + 1) * 512]

    B = {}

    def bf(name):
        if name not in B:
            B[name] = Buf(name)
        return B[name]

    def newstat():
        i = stat_i[0] % 10
        stat_i[0] += 1
        return i

    def pbc(ap):
        return ap.partition_broadcast(128).rearrange("p o n -> p (o n)")

    def rstd_from_ss(ss_ap, rs_ap, inv_n, ssb, rsb, eps_ap=None, eps_b=None):
        n = ss_ap.shape[1]
        epsv = EPS if eps_ap is None else eps_ap
        pool.op(lambda e: e.tensor_scalar(out=rs_ap, in0=ss_ap, scalar1=inv_n, scalar2=epsv,
                                          op0=ALU.mult, op1=ALU.add), reads=[ssb] + ([eps_b] if eps_b else []), writes=[rsb])
        pool.op(lambda e: e.tensor_tensor(out=rs_ap, in0=rs_ap, in1=neghalf[:, 0:n], op=ALU.pow),
                reads=[rsb, bf("neghalf")], writes=[rsb])

    sp.dma(ident[:], ident_d[:, :], bf("ident"), writes=[bf("ident")])
    sp.dma(selt[:], sel[:, :], bf("selt"), writes=[bf("selt")])
    dve.op(lambda e: e.memset(neghalf[:], -0.5), writes=[bf("neghalf")])

    wring_b = [Buf("wring0"), Buf("wring1"), None, None]
    R1_old = []
    R2_old = []
    R3_old = []

    GROUPS = [("KV", 1024), ("QA", 0), ("QB", 512), ("GA0", 1536), ("GA1", 2048),
              ("CI", 2560), ("CC", 3584), ("CG", 4096), ("CB", 3072),
              ("SV", 5120), ("SG", 5632), ("SU", 4608)]

    def slot_of(gi):
        return gi if gi < 4 else gi % 2

    def load_wgroup(l, gi):
        slot = slot_of(gi)
        c0 = GROUPS[gi][1]
        for half in range(2):
            src = w_in[l, half * 1024:(half + 1) * 1024, c0:c0 + 512].rearrange("(k p) n -> p k n", p=128)
            pool.dma(wring[slot][:, half * 8:(half + 1) * 8, :], src, wring_b[slot], writes=[wring_b[slot]])

    def run_pipeline(ntiles, make_stages):
        pl = []
        it = 0
        while True:
            if it < ntiles:
                pl.append(make_stages(it))
            live = False
            maxk = max(len(x) for x in pl)
            for k in range(maxk - 1, -1, -1):
                tt = it - k
                if 0 <= tt < len(pl) and k < len(pl[tt]):
                    pl[tt][k]()
            if it >= ntiles - 1 and all(it - tt >= len(pl[tt]) - 1 for tt in range(len(pl))):
                break
            it += 1


    for l in range(DEPTH):
        xs_ap = x_in if l == 0 else x1
        xsb = x_src_b[l]
        for (t_sb, name, src) in [(nwt, "nwt", nw_pp[l]), (bwt, "bwt", bw_pp[l]), (sbt, "sbt", sb_pp[l])]:
            sp.dma(t_sb[:], src, bf(name), writes=[bf(name)])
        sp.dma(qwb[:], pbc(qw[l]), bf("qwb"), writes=[bf("qwb")])
        sp.dma(kwb[:], pbc(kw[l]), bf("kwb"), writes=[bf("kwb")])
        sp.dma(qwsb[:], pbc(qws[l]), bf("qwsb"), writes=[bf("qwsb")])
        sp.dma(kwsb[:], pbc(kws[l]), bf("kwsb"), writes=[bf("kwsb")])
        sp.dma(snwb[:], pbc(snw[l]), bf("snwb"), writes=[bf("snwb")])
        for j in range(3):
            sp.dma(cwb[:, j, :], pbc(cw[l, j:j + 1, :]), bf("cwb"), writes=[bf("cwb")])
        pool.dma(wst[:], wsT[l], bf("wst"), writes=[bf("wst")])
        dve.op(lambda e: e.tensor_reduce(out=negc[:, 1:2], in_=qwb[:], axis=AX.X, op=ALU.max, apply_absolute_value=True),
               reads=[bf("qwb")], writes=[bf("negc")])
        dve.op(lambda e: e.tensor_reduce(out=negc[:, 2:3], in_=kwb[:], axis=AX.X, op=ALU.max, apply_absolute_value=True),
               reads=[bf("kwb")], writes=[bf("negc")])
        dve.op(lambda e: e.tensor_scalar(out=negc[:, 0:1], in0=negc[:, 1:2], scalar1=-float(HD) ** 0.5, scalar2=negc[:, 2:3],
                                         op0=ALU.mult, op1=ALU.mult), reads=[bf("negc")], writes=[bf("negc")])
        dve.op(lambda e: e.tensor_scalar(out=negc[:, 0:1], in0=negc[:, 0:1], scalar1=80.0, scalar2=0.0,
                                         op0=ALU.add, op1=ALU.min), reads=[bf("negc")], writes=[bf("negc")])

        hT_b = [alias(f"hT{l}_{t}", R1_old) for t in range(NT)]
        KTst_b = alias(f"KTst{l}", R2_old)
        Vst_b = alias(f"Vst{l}", R2_old)
        dve.op(lambda e: e.memset(Vst[:, :, 128:129], 1.0), writes=[Vst_b])
        ia_b = [None] * NT
        ib_b = [alias(f"ib{l}_{t}", R2_old) for t in range(NT)]
        ia16_b = [alias(f"iah{l}_{t}", R2_old) for t in range(NT)]
        junkA_b = ib_b[0]
        regionsA = lambda: ([(t * 2048, (t + 1) * 2048, ia_b[t]) for t in range(NT)]
                    + [(t * 1024, (t + 1) * 1024, ia16_b[t]) for t in range(NT)]
                    + [(32768 + t * 1024, 32768 + (t + 1) * 1024, ib_b[t]) for t in range(NT)]
                    + [(49152, 49152 + 4 * TOK, KTst_b), (49152 + 4 * TOK, 49152 + 4 * TOK + NT * DV * 2, Vst_b)])
        if l > 0:
            wring_b = [alias(f"wring{l}_0", R3_old), alias(f"wring{l}_1", R3_old), None, None]
        wring_b[2] = alias(f"wringx{l}_2", R2_old)
        wring_b[3] = alias(f"wringx{l}_3", R2_old)
        tA_b = [[alias(f"tA{l}_{s}_{i}", R3_old) for i in range(4)] for s in range(2)]
        gst_b = [alias(f"gst{l}_{i}", R3_old) for i in range(2)]
        qst_b = [alias(f"qst{l}_{i}", R3_old) for i in range(2)]
        mst_b = [alias(f"mst{l}_{i}", R3_old) for i in range(2)]

        load_wgroup(l, 0)
        load_wgroup(l, 1)

        sp.dma(xe_src[l][0:1, :], xs_ap[0:1, :], xesrc_b[l], reads=[xsb[0]], writes=[xesrc_b[l]])
        sp.dma(xe_src[l][1:2, :], xs_ap[TOK - 1:TOK, :], xesrc_b[l], reads=[xsb[NT - 1]], writes=[xesrc_b[l]])
        pool.collective(groups, xe_src[l].opt(), xe_dst[l].opt(), [xesrc_b[l]], [xedst_b[l]], f"xe{l}")

        rx_b = [Buf(f"rx{l}_{t}") for t in range(NT)]

        def a0_stages(t):
            sl = t % 2
            xb = bf(f"xt{sl}")

            def s0():
                sp.dma(xt[sl][:], xs_ap[t * 128:(t + 1) * 128, :], xb, reads=[xsb[t]], writes=[xb])
                si = newstat()
                ssb = bf(f"ss{si}")
                act.op(lambda e: e.activation(out=junkA, in_=xt[sl][:], func=AF.Square, accum_out=ss[:, si * 4:si * 4 + 1]),
                       reads=[xb], writes=[junkA_b, ssb])
                pool.op(lambda e: e.tensor_scalar(out=rx[:, t:t + 1], in0=ss[:, si * 4:si * 4 + 1], scalar1=1.0 / D,
                                                  scalar2=EPS, op0=ALU.mult, op1=ALU.add), reads=[ssb], writes=[rx_b[t]])
                pool.op(lambda e: e.tensor_scalar(out=e2x[:, t:t + 1], in0=rx[:, t:t + 1], scalar1=EPS, scalar2=None,
                                                  op0=ALU.mult), reads=[rx_b[t]], writes=[rx_b[t]])
                pool.op(lambda e: e.tensor_tensor(out=rx[:, t:t + 1], in0=rx[:, t:t + 1], in1=neghalf[:, 0:1], op=ALU.pow),
                        reads=[rx_b[t], bf("neghalf")], writes=[rx_b[t]])

            def s2():
                for j in range(4):
                    pb = 4 + (j % 2)
                    for i in range(4):
                        k = 4 * j + i
                        pe.op(lambda e, k=k, i=i, pb=pb: e.transpose(
                            out=bk(pb)[:, i * 128:(i + 1) * 128], in_=xt[sl][:, k * 128:(k + 1) * 128], identity=ident[:]),
                            reads=[xb, bf("ident")], writes=[bankb[pb]])
                    dve.op(lambda e, j=j, pb=pb: e.tensor_tensor(
                        out=R1[:, 4 * j:4 * j + 4, t * 128:(t + 1) * 128],
                        in0=bk(pb).rearrange("p (i t) -> p i t", i=4),
                        in1=nwt[:, 4 * j:4 * j + 4].unsqueeze(2).broadcast_to([128, 4, 128]), op=ALU.mult),
                        reads=[bankb[pb], bf("nwt")], writes=[hT_b[t]])
            return [s0, s2]

        run_pipeline(NT, a0_stages)

        xeb = bf("xt0")
        sp.dma(xg, xe_dst[l][:, :], bf("xt1"), reads=[xedst_b[l]], writes=[bf("xt1")])
        for n in range(4):
            pe.op(lambda e, n=n: e.matmul(bk(4)[0:2, :],
                                          lhsT=selt[:, :], rhs=xg[:, n * 512:(n + 1) * 512], start=True, stop=True),
                  reads=[bf("selt"), bf("xt1")], writes=[bankb[4]])
            dve.op(lambda e, n=n: e.tensor_copy(out=xh[:, n * 512:(n + 1) * 512], in_=bk(4)[0:2, :]),
                   reads=[bankb[4]], writes=[xeb])
        sh_b, rh_b = bf("ssh"), bf("rsh")
        dve.op(lambda e: e.scalar_tensor_tensor(out=junkA[0:2, :], in0=xh, scalar=1.0, in1=xh,
                                                op0=ALU.mult, op1=ALU.mult, accum_out=ss[0:2, 40:41]),
               reads=[xeb], writes=[junkA_b, sh_b])
        pool.op(lambda e: e.tensor_scalar(out=rs[0:2, 40:41], in0=ss[0:2, 40:41], scalar1=1.0 / D, scalar2=EPS,
                                          op0=ALU.mult, op1=ALU.add), reads=[sh_b], writes=[rh_b])
        pool.op(lambda e: e.tensor_tensor(out=rs[0:2, 40:41], in0=rs[0:2, 40:41], in1=neghalf[0:2, 0:1], op=ALU.pow),
                reads=[rh_b, bf("neghalf")], writes=[rh_b])
        act.op(lambda e: e.activation(out=xh, in_=xh, func=AF.Copy, scale=rs[0:2, 40:41]),
               reads=[xeb, rh_b], writes=[xeb])
        for j in range(4):
            for i in range(4):
                k = 4 * j + i
                pe.op(lambda e, k=k, i=i: e.transpose(out=bk(5)[:, i * 2:i * 2 + 2], in_=xh[:, k * 128:(k + 1) * 128],
                                                      identity=ident[0:2, 0:2]),
                      reads=[xeb, bf("ident")], writes=[bankb[5]])
            dve.op(lambda e, j=j: e.tensor_tensor(
                out=hTh[:, 4 * j:4 * j + 4, :],
                in0=bk(5)[:, 0:8].rearrange("p (i t) -> p i t", i=4),
                in1=nwt[:, 4 * j:4 * j + 4].unsqueeze(2).broadcast_to([128, 4, 2]), op=ALU.mult),
                reads=[bankb[5], bf("nwt")], writes=[bf("hTh")])


        tA_b += [[alias(f"tAx{l}_{s}_{i}", [bf(f"xt{s}")]) for i in range(4)] for s in range(2)]
        bank_ctr = [0]
        slot_ctr = [0]
        cs_ctr = [0]

        def proj_mm(gi, t):
            b = bank_ctr[0] % 4
            bank_ctr[0] += 1
            ws = slot_of(gi)
            for k in range(16):
                pe.op(lambda e, k=k, b=b, ws=ws, t=t: e.matmul(
                    bk(b), lhsT=R1[:, k, t * 128:(t + 1) * 128], rhs=wring[ws][:, k, :],
                    start=(k == 0), stop=(k == 15)),
                    reads=[hT_b[t], wring_b[ws]], writes=[bankb[b]])
            return b

        def transposes_out(src_ap, src_b, nblk, dst_sb, dst_b, scale_tab, evac_eng, dram_ap, dram_b, pbank):
            for i in range(nblk):
                pe.op(lambda e, i=i: e.transpose(out=bk(pbank)[:, i * 128:(i + 1) * 128],
                                                 in_=src_ap[:, i * 128:(i + 1) * 128], identity=ident[:]),
                      reads=[src_b, bf("ident")], writes=[bankb[pbank]])
            pin = bk(pbank)[:, 0:nblk * 128].rearrange("p (i t) -> p i t", i=nblk)
            if scale_tab is None:
                act.op(lambda e: e.activation(out=dst_sb, in_=pin, func=AF.Copy),
                       reads=[bankb[pbank]], writes=[dst_b])
            else:
                for i in range(nblk):
                    act.op(lambda e, i=i: e.activation(out=dst_sb[:, i, :], in_=pin[:, i, :], func=AF.Copy,
                                                       scale=scale_tab[:, i:i + 1]),
                           reads=[bankb[pbank], bf("bwt")], writes=[dst_b])
            if dram_ap is not None:
                sp.dma(dram_ap, dst_sb, dst_b, reads=[dst_b], writes=[dram_b])

        def load_cs(t, wtab, wtab_b, wstab, wstab_b):
            csl = cs_ctr[0] % 3
            cs_ctr[0] += 1
            csb = bf(f"cs{csl}")
            sp.dma(cs[csl][:], cs_t[t * 128:(t + 1) * 128, :, :], csb, writes=[csb])
            pool.op(lambda e: e.tensor_tensor(out=cs[csl][:, 0, :], in0=cs[csl][:, 0, :], in1=wtab[:], op=ALU.mult),
                    reads=[csb, wtab_b], writes=[csb])
            pool.op(lambda e: e.tensor_tensor(out=cs[csl][:, 1, :], in0=cs[csl][:, 1, :], in1=wstab[:], op=ALU.mult),
                    reads=[csb, wstab_b], writes=[csb])
            return csl

        def group_pre(gi, gname):
            if gi == 0:
                load_wgroup(l, 2)
                load_wgroup(l, 3)
            if gi >= 3 and gi + 1 < len(GROUPS):
                load_wgroup(l, gi + 1)
            if gname == "CI":
                for t_ in range(NT):
                    ia_b[t_] = alias(f"ia{l}_{t_}", [wring_b[2] if (t_ + 1) * 2048 <= 16384 else wring_b[3]])
            if gname in ("CI", "CC"):
                hb = 4
                for k in range(16):
                    pe.op(lambda e, k=k, ws=slot_of(gi): e.matmul(bk(hb)[0:2, :], lhsT=hTh[:, k, :], rhs=wring[ws][:, k, :],
                                                             start=(k == 0), stop=(k == 15)),
                          reads=[bf("hTh"), wring_b[slot_of(gi)]], writes=[bankb[hb]])
                if gname == "CI":
                    act.op(lambda e: e.activation(out=hh_ci[:], in_=bk(hb)[0:2, :], func=AF.Copy),
                           reads=[bankb[hb]], writes=[bf("hh")])
                else:
                    dve.op(lambda e: e.tensor_tensor(out=hh[:], in0=bk(hb)[0:2, :], in1=hh_ci[:], op=ALU.mult),
                           reads=[bankb[hb], bf("hh")], writes=[bf("hh")])

        def make_stages(t, gi, gname):
            s4 = slot_ctr[0] % 4
            s = slot_ctr[0] % 2
            slot_ctr[0] += 1
            tb, tbb = tA[s4], tA_b[s4]
            t0, t1, t2, t3 = tb
            st = {}
            v3 = lambda a: a.rearrange("p (h d) -> p h d", d=128)

            def rope_s0(ncol, wt, wtb, wst_, wstb):
                b = st["b"]
                csl = load_cs(t, wt, wtb, wst_, wstb)
                si = newstat()
                st["csl"], st["si"] = csl, si
                act.op(lambda e: e.activation(out=t0[:, 0:ncol], in_=bk(b)[:, 0:ncol], func=AF.Square),
                       reads=[bankb[b]], writes=[tbb[0]])

            def rope_s0b(ncol):
                nh = ncol // 128
                si = st["si"]
                ssb, rsb = bf(f"ss{si}"), bf(f"rs{si}")
                dve.op(lambda e: e.tensor_reduce(out=ss[:, si * 4:si * 4 + nh], in_=v3(t0[:, 0:ncol]),
                                                 axis=AX.X, op=ALU.add), reads=[tbb[0]], writes=[ssb])
                rstd_from_ss(ss[:, si * 4:si * 4 + nh], rs[:, si * 4:si * 4 + nh], 1.0 / 128, ssb, rsb,
                             eps_ap=e2x[:, t:t + 1], eps_b=rx_b[t])

            def rope_s1a(ncol):
                nh = ncol // 128
                b, si = st["b"], st["si"]
                rsb = bf(f"rs{si}")
                for h in range(nh):
                    act.op(lambda e, h=h: e.activation(out=t1[:, h * 128:(h + 1) * 128], in_=bk(b)[:, h * 128:(h + 1) * 128],
                                                       func=AF.Copy, scale=rs[:, si * 4 + h:si * 4 + h + 1]),
                           reads=[bankb[b], rsb], writes=[tbb[1]])

            def rope_s1(ncol):
                nh = ncol // 128
                b, csl, si = st["b"], st["csl"], st["si"]
                csb, rsb = bf(f"cs{csl}"), bf(f"rs{si}")
                dve.op(lambda e: e.tensor_tensor(out=v3(t2[:, 0:ncol]), in0=v3(t1[:, 0:ncol]),
                                                 in1=cs[csl][:, 0, :].unsqueeze(1).broadcast_to([128, nh, 128]),
                                                 op=ALU.mult), reads=[tbb[1], csb], writes=[tbb[2]])
                for hf in range(2):
                    o_v = t0[:, 0:ncol].rearrange("p (h a two f) -> p h a two f", a=2, two=2, f=32)[:, :, :, hf, :]
                    i_v = t1[:, 0:ncol].rearrange("p (h a two f) -> p h a two f", a=2, two=2, f=32)[:, :, :, 1 - hf, :]
                    s_v = cs[csl][:, 1, :].rearrange("p (a two f) -> p a two f", a=2, two=2)[:, :, hf, :]
                    dve.op(lambda e, o_v=o_v, i_v=i_v, s_v=s_v: e.tensor_tensor(
                        out=o_v, in0=i_v, in1=s_v.unsqueeze(1).broadcast_to([128, nh, 2, 32]), op=ALU.mult),
                        reads=[tbb[1], csb], writes=[tbb[0]])
                dve.op(lambda e: e.tensor_tensor(out=t3[:, 0:ncol], in0=t2[:, 0:ncol], in1=t0[:, 0:ncol], op=ALU.add),
                       reads=[tbb[2], tbb[0]], writes=[tbb[3]])

            def mm():
                st["b"] = proj_mm(gi, t)

            if gname == "KV":
                def s0():
                    mm()
                    b = st["b"]
                    act.op(lambda e: e.activation(
                        out=Vst[:, t, 0:258].rearrange("p (h d) -> p h d", h=2)[:, :, 0:128],
                        in_=bk(b)[:, 256:512].rearrange("p (h d) -> p h d", h=2), func=AF.Copy, scale=rx[:, t:t + 1]),
                        reads=[bankb[b], rx_b[t]], writes=[Vst_b])
                    rope_s0(256, kwb, bf("kwb"), kwsb, bf("kwsb"))
                return [s0, lambda: rope_s0b(256), lambda: rope_s1a(256), lambda: rope_s1(256),
                        lambda: transposes_out(t3[:, 0:256], tbb[3], 2, KTst[:, :, t * 128:(t + 1) * 128], KTst_b,
                                               None, act, None, None, 4 + (t % 2))]
            if gname in ("QA", "QB"):
                h0 = 0 if gname == "QA" else 4

                def s0():
                    mm()
                    rope_s0(512, qwb, bf("qwb"), qwsb, bf("qwsb"))
                return [s0, lambda: rope_s0b(512), lambda: rope_s1a(512), lambda: rope_s1(512),
                        lambda: transposes_out(t3, tbb[3], 4, qst[s][:], qst_b[s], None, act,
                                               qT_s[l][t, :, h0:h0 + 4, :], qT_b[l][t], 4 + (t % 2))]
            if gname in ("GA0", "GA1"):
                cc = 0 if gname == "GA0" else 512

                def s0():
                    mm()
                    b = st["b"]
                    act.op(lambda e: e.activation(out=gst[s], in_=bk(b), func=AF.Silu, scale=rx[:, t:t + 1]),
                           reads=[bankb[b], rx_b[t]], writes=[gst_b[s]])
                    sp.dma(g_s[l][t * 128:(t + 1) * 128, cc:cc + 512], gst[s], gst_b[s],
                           reads=[gst_b[s]], writes=[g_b[l][t]])
                return [s0]
            if gname == "CI":
                def s0():
                    mm()
                    b = st["b"]
                    act.op(lambda e: e.activation(out=ia[:, t, :], in_=bk(b), func=AF.Copy, scale=rx[:, t:t + 1]),
                           reads=[bankb[b], rx_b[t]], writes=[ia_b[t]])
                return [s0]
            if gname == "CC":
                def s0():
                    mm()
                    b = st["b"]
                    dve.op(lambda e: e.scalar_tensor_tensor(out=ia[:, t, :], in0=bk(b), scalar=rx[:, t:t + 1],
                                                            in1=ia[:, t, :], op0=ALU.mult, op1=ALU.mult),
                           reads=[bankb[b], ia_b[t], rx_b[t]], writes=[ia_b[t]])
                return [s0]
            if gname == "CG":
                def s0():
                    mm()
                    b = st["b"]
                    act.op(lambda e: e.activation(out=ib16[:, t, :], in_=bk(b), func=AF.Silu, scale=rx[:, t:t + 1]),
                           reads=[bankb[b], rx_b[t]], writes=[ib_b[t]])
                return [s0]
            if gname == "SG":
                def s0():
                    mm()
                    b = st["b"]
                    act.op(lambda e: e.activation(out=ia16[:, t, :], in_=bk(b), func=AF.Silu, scale=rx[:, t:t + 1]),
                           reads=[bankb[b], rx_b[t]], writes=[ia16_b[t], ia_b[t // 2]])
                return [s0]
            if gname == "CB":
                def s0():
                    sp.dma(t0[1:127, :], ia[0:126, t, :], tbb[0], reads=[ia_b[t]], writes=[tbb[0]])
                    sp.dma(t0[127:128, :], ia[126:127, t, :], tbb[0], reads=[ia_b[t]], writes=[tbb[0]])
                    if t > 0:
                        sp.dma(t0[0:1, :], ia[127:128, t - 1, :], tbb[0], reads=[ia_b[t - 1]], writes=[tbb[0]])
                    else:
                        sp.dma(t0[0:1, :], hh[0:1, :], tbb[0], reads=[bf("hh")], writes=[tbb[0]])
                    sp.dma(t1[1:127, :], ia[2:128, t, :], tbb[1], reads=[ia_b[t]], writes=[tbb[1]])
                    sp.dma(t1[0:1, :], ia[1:2, t, :], tbb[1], reads=[ia_b[t]], writes=[tbb[1]])
                    if t < NT - 1:
                        sp.dma(t1[127:128, :], ia[0:1, t + 1, :], tbb[1], reads=[ia_b[t + 1]], writes=[tbb[1]])
                    else:
                        sp.dma(t1[127:128, :], hh[1:2, :], tbb[1], reads=[bf("hh")], writes=[tbb[1]])
                    mm()
                    pool.op(lambda e: e.tensor_tensor(out=t2, in0=ia[:, t, :], in1=cwb[:, 1, :], op=ALU.mult),
                            reads=[ia_b[t], bf("cwb")], writes=[tbb[2]])

                def s1():
                    b = st["b"]
                    dve.op(lambda e: e.tensor_tensor(out=t0, in0=t0, in1=cwb[:, 0, :], op=ALU.mult),
                           reads=[tbb[0], bf("cwb")], writes=[tbb[0]])
                    dve.op(lambda e: e.tensor_tensor(out=t1, in0=t1, in1=cwb[:, 2, :], op=ALU.mult),
                           reads=[tbb[1], bf("cwb")], writes=[tbb[1]])
                    dve.op(lambda e: e.tensor_tensor(out=t1, in0=t1, in1=t2, op=ALU.add),
                           reads=[tbb[1], tbb[2]], writes=[tbb[1]])
                    dve.op(lambda e: e.tensor_tensor(out=t0, in0=t0, in1=t1, op=ALU.add),
                           reads=[tbb[0], tbb[1]], writes=[tbb[0]])
                    dve.op(lambda e: e.scalar_tensor_tensor(out=t0, in0=bk(b), scalar=rx[:, t:t + 1], in1=t0,
                                                            op0=ALU.mult, op1=ALU.mult),
                           reads=[bankb[b], tbb[0], rx_b[t]], writes=[tbb[0]])

                def s1b():
                    si = newstat()
                    st["si"] = si
                    ssb, rsb = bf(f"ss{si}"), bf(f"rs{si}")
                    act.op(lambda e: e.activation(out=t1, in_=t0, func=AF.Square, accum_out=ss[:, si * 4:si * 4 + 1]),
                           reads=[tbb[0]], writes=[tbb[1], ssb])
                    rstd_from_ss(ss[:, si * 4:si * 4 + 1], rs[:, si * 4:si * 4 + 1], 1.0 / 512, ssb, rsb)

                def s2():
                    si = st["si"]
                    dve.op(lambda e: e.scalar_tensor_tensor(
                        out=t3, in0=t0, scalar=rs[:, si * 4:si * 4 + 1], in1=ib16[:, t, :],
                        op0=ALU.mult, op1=ALU.mult), reads=[tbb[0], bf(f"rs{si}"), ib_b[t]], writes=[tbb[3]])
                return [s0, s1, s1b, s2,
                        lambda: transposes_out(t3, tbb[3], 4, mst[s][:], mst_b[s], bwt[:, 8:12], dve,
                                               mT_s[l][t, :, 0:4, :], mT_b[l][t], 4 + (t % 2))]
            if gname == "SV":
                def s0():
                    mm()
                    b = st["b"]
                    act.op(lambda e: e.activation(out=t0, in_=bk(b), func=AF.Gelu, scale=rx[:, t:t + 1]),
                           reads=[bankb[b], rx_b[t]], writes=[tbb[0]])
                    act.op(lambda e: e.activation(out=t1, in_=t0, func=AF.Square),
                           reads=[tbb[0]], writes=[tbb[1]])

                def s0b():
                    si = newstat()
                    st["si"] = si
                    ssb, rsb = bf(f"ss{si}"), bf(f"rs{si}")
                    dve.op(lambda e: e.tensor_reduce(out=ss[:, si * 4:si * 4 + 4], in_=v3(t1), axis=AX.X, op=ALU.add),
                           reads=[tbb[1]], writes=[ssb])
                    rstd_from_ss(ss[:, si * 4:si * 4 + 4], rs[:, si * 4:si * 4 + 4], 1.0 / 128, ssb, rsb)

                def s1():
                    si = st["si"]
                    dve.op(lambda e: e.tensor_tensor(
                        out=v3(t2), in0=v3(t0),
                        in1=rs[:, si * 4:si * 4 + 4].unsqueeze(2).broadcast_to([128, 4, 128]), op=ALU.mult),
                        reads=[tbb[0], bf(f"rs{si}")], writes=[tbb[2]])

                def s2():
                    dve.op(lambda e: e.tensor_tensor(out=ib16[:, t, :], in0=t2, in1=snwb[:], op=ALU.mult),
                           reads=[tbb[2], bf("snwb")], writes=[ib_b[t]])
                return [s0, s0b, s1, s2]
            if gname == "SU":
                sbk = 6 + (t % 2)

                def s0():
                    for g in range(4):
                        pe.op(lambda e, g=g: e.matmul(bk(sbk)[:, g * 128:(g + 1) * 128], lhsT=wst[:, g, :],
                                                      rhs=ib16[:, t, g * 128:(g + 1) * 128], start=True, stop=True),
                              reads=[bf("wst"), ib_b[t]], writes=[bankb[sbk]])
                    mm()
                    b = st["b"]
                    act.op(lambda e: e.activation(out=t0, in_=bk(b), func=AF.Gelu, scale=rx[:, t:t + 1]),
                           reads=[bankb[b], rx_b[t]], writes=[tbb[0]])

                def s1():
                    for g in range(4):
                        dve.op(lambda e, g=g: e.scalar_tensor_tensor(
                            out=t1[:, g * 128:(g + 1) * 128], in0=bk(sbk)[:, g * 128:(g + 1) * 128],
                            scalar=sbt[:, g:g + 1], in1=t0[:, g * 128:(g + 1) * 128], op0=ALU.add, op1=ALU.mult),
                            reads=[bankb[sbk], bf("sbt"), tbb[0]], writes=[tbb[1]])

                def s1b():
                    si = newstat()
                    st["si"] = si
                    ssb, rsb = bf(f"ss{si}"), bf(f"rs{si}")
                    act.op(lambda e: e.activation(out=t2, in_=t1, func=AF.Square, accum_out=ss[:, si * 4:si * 4 + 1]),
                           reads=[tbb[1]], writes=[tbb[2], ssb])
                    rstd_from_ss(ss[:, si * 4:si * 4 + 1], rs[:, si * 4:si * 4 + 1], 1.0 / 512, ssb, rsb)

                def s2():
                    si = st["si"]
                    dve.op(lambda e: e.scalar_tensor_tensor(
                        out=t3, in0=t1, scalar=rs[:, si * 4:si * 4 + 1], in1=ia16[:, t, :],
                        op0=ALU.mult, op1=ALU.mult), reads=[tbb[1], bf(f"rs{si}"), ia16_b[t]], writes=[tbb[3]])
                return [s0, s1, s1b, s2,
                        lambda: transposes_out(t3, tbb[3], 4, mst[s][:], mst_b[s], bwt[:, 12:16], dve,
                                               mT_s[l][t, :, 4:8, :], mT_b[l][t], 4 + (t % 2))]
            raise AssertionError(gname)

        def olds_for(lo, hi):
            return [b_ for (a_, z_, b_) in regionsA() if a_ < hi and lo < z_]

        pv = lambda r_: (r_ + 2) % NR
        KTv_b = [[None] * NR for _ in range(2)]
        Vv_b = [None] * NR

        def load_kt(h, r_):
            lo = (h * S + r_ * TOK) * 2
            KTv_b[h][r_] = alias(f"KT{l}_{h}_{r_}", olds_for(lo, lo + 2 * TOK))
            sp.dma(KT[:, h, r_ * TOK:(r_ + 1) * TOK], kt_dst[l][(r_ * 2 + h) * 128:(r_ * 2 + h + 1) * 128, :],
                   KTv_b[h][r_], reads=[ktdst_b[l]], writes=[KTv_b[h][r_]])

        def load_v(r_):
            p_ = pv(r_)
            lo = (2 * S + p_ * NT * DV) * 2
            Vv_b[p_] = alias(f"Vv{l}_{p_}", olds_for(lo, lo + NT * DV * 2))
            for i_ in range(2):
                o_ = 2 * S + p_ * NT * DV + i_ * NH * DV
                sp.dma(R2[:, o_:o_ + NH * DV], v_dst[l][i_][r_ * 128:(r_ + 1) * 128, :],
                       Vv_b[p_], reads=[vdst_b[l][i_]], writes=[Vv_b[p_]])

        def group_post(gi, gname):
            if gname == "CB":
                for r_ in range(NR):
                    load_kt(1, r_)
                load_v(0)
                load_v(1)
            if gname == "KV":
                sp.dma(kt_src[l].rearrange("(h d) t -> d h t", h=2), KTst, KTst_b, reads=[KTst_b], writes=[ktsrc_b[l]])
                for i_ in range(2):
                    o_ = 24576 + 2 * TOK + i_ * NH * DV
                    sp.dma(v_src[l][i_][:, :], R2[:, o_:o_ + NH * DV], Vst_b, reads=[Vst_b], writes=[vsrc_b[l][i_]])
            if gname == "KV":
                pool.collective(groups, kt_src[l].opt(), kt_dst[l].opt(), [ktsrc_b[l]], [ktdst_b[l]], f"kt{l}")
                for i_ in range(2):
                    pool.collective(groups, v_src[l][i_].opt(), v_dst[l][i_].opt(), [vsrc_b[l][i_]], [vdst_b[l][i_]], f"v{l}_{i_}")

        items = [(gi, gname, t) for gi, (gname, c0) in enumerate(GROUPS) for t in range(NT)]

        def make_item(idx):
            gi, gname, t = items[idx]
            stages = list(make_stages(t, gi, gname))
            if t == 0:
                f0 = stages[0]
                stages[0] = (lambda f0=f0, gi=gi, gname=gname: (group_pre(gi, gname), f0()))
            if t == NT - 1:
                fl = stages[-1]
                stages[-1] = (lambda fl=fl, gi=gi, gname=gname: (fl(), group_post(gi, gname)))
            return stages

        run_pipeline(len(items), make_item)

        for s_ in range(2):
            B[f"xt{s_}"] = alias(f"xtB{l}_{s_}", tA_b[2 + s_])
        R1_old = hT_b
        R2_old = ia_b + ib_b + ia16_b + [KTst_b, Vst_b]
        R3_old = wring_b + [x for s_ in tA_b[0:2] for x in s_] + gst_b + qst_b + mst_b
        wout_b = [alias(f"wout{l}_{q}", R1_old) for q in range(4)]
        qTt_b = [alias(f"qTt{l}_{i}", R3_old) for i in range(2)]
        gt_b = [alias(f"gt{l}_{i}", R3_old) for i in range(2)]
        mTt_b = [alias(f"mTt{l}_{i}", R3_old) for i in range(2)]
        PT_b = [alias(f"PT{l}_{i}", R3_old) for i in range(3)]
        ao_b = [alias(f"ao{l}_{i}", R3_old) for i in range(2)]
        mTa_b = [alias(f"mTa{l}_{i}", R3_old) for i in range(2)]
        junkB_b = alias(f"junkB{l}", R3_old)
        fwb_b = alias(f"fwb{l}", R3_old)

        def load_wout(half):
            for n in (2 * half, 2 * half + 1):
                for hf in range(2):
                    src = w_out[l, hf * 1024:(hf + 1) * 1024, n * 512:(n + 1) * 512].rearrange("(k p) c -> p k c", p=128)
                    pool.dma(R1[:, hf * 8:(hf + 1) * 8, n * 512:(n + 1) * 512], src, wout_b[n], writes=[wout_b[n]])

        KV_REST = True
        last = (l == DEPTH - 1)
        if last:
            sp.dma(fwb, pbc(fw), fwb_b, writes=[fwb_b])

        def loads_B(t):
            sl = t % 2
            sp.dma(qTt[sl][:], qT_s[l][t], qTt_b[sl], reads=[qT_b[l][t]], writes=[qTt_b[sl]])
            sp.dma(gt[sl], g_s[l][t * 128:(t + 1) * 128, :], gt_b[sl], reads=[g_b[l][t]], writes=[gt_b[sl]])
            sp.dma(mTt[sl][:], mT_s[l][t], mTt_b[sl], reads=[mT_b[l][t]], writes=[mTt_b[sl]])
            xb = bf(f"xt{sl}")
            sp.dma(xt[sl][:], xs_ap[t * 128:(t + 1) * 128, :], xb, reads=[xsb[t]], writes=[xb])

        def post1(t):
            sl = t % 2
            a = ao[sl]
            si = newstat()
            ssb, rsb = bf(f"ss{si}"), bf(f"rs{si}")
            dve.op(lambda e: e.scalar_tensor_tensor(out=junkB[:, 0:1024], in0=a, scalar=1.0, in1=a,
                                                    op0=ALU.mult, op1=ALU.mult, accum_out=ss[:, si * 4:si * 4 + 1]),
                   reads=[ao_b[sl]], writes=[junkB_b, ssb])
            rstd_from_ss(ss[:, si * 4:si * 4 + 1], rs[:, si * 4:si * 4 + 1], 1.0 / 1024, ssb, rsb)
            dve.op(lambda e: e.scalar_tensor_tensor(out=a, in0=a, scalar=rs[:, si * 4:si * 4 + 1], in1=gt[sl],
                                                    op0=ALU.mult, op1=ALU.mult),
                   reads=[ao_b[sl], rsb, gt_b[sl]], writes=[ao_b[sl]])
            for j in range(2):
                pb = 6 + j
                for i in range(4):
                    k = 4 * j + i
                    pe.op(lambda e, k=k, i=i, pb=pb: e.transpose(out=bk(pb)[:, i * 128:(i + 1) * 128],
                                                                 in_=a[:, k * 128:(k + 1) * 128], identity=ident[:]),
                          reads=[ao_b[sl], bf("ident")], writes=[bankb[pb]])
                dve.op(lambda e, j=j, pb=pb: e.tensor_tensor(
                    out=mTa[sl][:, 4 * j:4 * j + 4, :], in0=bk(pb).rearrange("p (i t) -> p i t", i=4),
                    in1=bwt[:, 4 * j:4 * j + 4].unsqueeze(2).broadcast_to([128, 4, 128]), op=ALU.mult),
                    reads=[bankb[pb], bf("bwt")], writes=[mTa_b[sl]])

        def post2(t):
            sl = t % 2
            xb = bf(f"xt{sl}")
            for n in range(4):
                pb = 6 + (n % 2)
                for k in range(16):
                    lhsT = mTa[sl][:, k, :] if k < 8 else mTt[sl][:, k - 8, :]
                    lb = mTa_b[sl] if k < 8 else mTt_b[sl]
                    pe.op(lambda e, k=k, n=n, pb=pb, lhsT=lhsT: e.matmul(
                        bk(pb), lhsT=lhsT, rhs=R1[:, k, n * 512:(n + 1) * 512], start=(k == 0), stop=(k == 15)),
                        reads=[lb, wout_b[n]], writes=[bankb[pb]])
                dve.op(lambda e, n=n, pb=pb: e.tensor_tensor(out=xt[sl][:, n * 512:(n + 1) * 512], in0=bk(pb),
                                                             in1=xt[sl][:, n * 512:(n + 1) * 512], op=ALU.add),
                       reads=[bankb[pb], xb], writes=[xb])
            if not last:
                sp.dma(x1[t * 128:(t + 1) * 128, :], xt[sl][:], xb, reads=[xb], writes=[x_src_b[l + 1][t]])
            else:
                si = newstat()
                ssb, rsb = bf(f"ss{si}"), bf(f"rs{si}")
                dve.op(lambda e: e.scalar_tensor_tensor(out=junkB, in0=xt[sl][:], scalar=1.0, in1=xt[sl][:],
                                                        op0=ALU.mult, op1=ALU.mult, accum_out=ss[:, si * 4:si * 4 + 1]),
                       reads=[xb], writes=[junkB_b, ssb])
                rstd_from_ss(ss[:, si * 4:si * 4 + 1], rs[:, si * 4:si * 4 + 1], 1.0 / D, ssb, rsb)
                dve.op(lambda e: e.scalar_tensor_tensor(out=xt[sl][:], in0=xt[sl][:], scalar=rs[:, si * 4:si * 4 + 1],
                                                        in1=fwb, op0=ALU.mult, op1=ALU.mult),
                       reads=[xb, rsb, fwb_b], writes=[xb])
                sp.dma(y_out[t * 128:(t + 1) * 128, :], xt[sl][:], xb, reads=[xb], writes=[y_b])

        NC2 = NKT // 2
        steps = [(t, g, c) for t in range(NT) for g in (1, 0) for c in range(NC2)]
        scale = float(HD) ** -0.5

        def emit_qk(i):
            t, g, c = steps[i]
            sl = t % 2
            sb_ = i % 2
            for j in range(2):
                kt_i = c * 2 + j
                pe.op(lambda e, j=j, kt_i=kt_i, g=g, sl=sl, sb_=sb_: e.matmul(
                    pp[sb_][:, j * 512:(j + 1) * 512], lhsT=KT[:, g, kt_i * 128:(kt_i + 1) * 128],
                    rhs=qTt[sl][:, 4 * g:4 * g + 4, :].rearrange("p h t -> p (h t)"), start=True, stop=True),
                    reads=[KTv_b[g][kt_i // NT], qTt_b[sl]], writes=[bankb[2 * sb_ + j]])

        def emit_exp(i):
            sb_ = i % 2
            p = i % 3
            act.op(lambda e: e.activation(out=PT[p], in_=pp[sb_][:, :], func=AF.Exp, scale=scale, bias=negc[:, 0:1]),
                   reads=[bankb[2 * sb_], bankb[2 * sb_ + 1], bf("negc")], writes=[PT_b[p]])

        def emit_pv(i):
            t, g, c = steps[i]
            p = i % 3
            for j in range(2):
                kt_i = c * 2 + j
                for h4 in range(4):
                    ob = 4 + h4 // 2
                    c0_ = (h4 % 2) * 256
                    pe.op(lambda e, j=j, kt_i=kt_i, h4=h4, ob=ob, c0_=c0_: e.matmul(
                        bk(ob)[:, c0_:c0_ + 129], lhsT=PT[p][:, j * 512 + h4 * 128:j * 512 + (h4 + 1) * 128],
                        rhs=Vaug[:, pv(kt_i // NT) * NT + kt_i % NT, 128 * g:128 * g + 129],
                        start=(c == 0 and j == 0 and h4 % 2 == 0), stop=(c == NC2 - 1 and j == 1),
                        skip_group_check=True),
                        reads=[PT_b[p], Vv_b[pv(kt_i // NT)]], writes=[bankb[ob]])
            if c == NC2 - 1:
                sl = t % 2
                rb = bf("rden")
                for h4 in range(4):
                    ob = 4 + h4 // 2
                    c0_ = (h4 % 2) * 256
                    h = 4 * g + h4
                    dcol = c0_ + (128 if g == 0 else 0)
                    vcol = c0_ + (0 if g == 0 else 1)
                    dve.op(lambda e, ob=ob, dcol=dcol, h4=h4: e.reciprocal(out=rs[:, 41 + (h4 % 2):42 + (h4 % 2)],
                                                                         in_=bk(ob)[:, dcol:dcol + 1]),
                           reads=[bankb[ob]], writes=[rb])
                    dve.op(lambda e, ob=ob, vcol=vcol, h=h, h4=h4: e.tensor_scalar(
                        out=ao[sl][:, h * 128:(h + 1) * 128], in0=bk(ob)[:, vcol:vcol + 128],
                        scalar1=rs[:, 41 + (h4 % 2):42 + (h4 % 2)], scalar2=None,
                        op0=ALU.mult), reads=[bankb[ob], rb], writes=[ao_b[sl]])

        loads_B(0)
        load_v(2)
        load_v(3)
        for r_ in range(NR):
            load_kt(0, r_)
        emit_qk(0)
        if len(steps) > 1:
            emit_qk(1)
        for i, (t, g, c) in enumerate(steps):
            emit_exp(i)
            if i + 2 < len(steps):
                emit_qk(i + 2)
            emit_pv(i)
            st_ = i % (2 * NC2)
            if t == 0 and st_ == min(8, 2 * NC2 - 4):
                load_wout(0)
            if t == 0 and st_ == min(24, 2 * NC2 - 3):
                load_wout(1)
            if st_ == 1 and t > 0:
                post1(t - 1)
            if st_ == 3 and t > 0:
                post2(t - 1)
            if st_ == 4 and t + 1 < NT:
                loads_B(t + 1)
        post1(NT - 1)
        post2(NT - 1)

        R1_old = wout_b
        R2_old = [b_ for row in KTv_b for b_ in row] + list(Vv_b)
        R3_old = qTt_b + gt_b + mTt_b + PT_b + ao_b + mTa_b + [junkB_b, fwb_b]

    for tok in y_b.w.values():
        sp.wait(tok)

    with nc.Block() as block:
        @block.tensor
        def _(e):
            for f in pe.prog:
                f(e)

        @block.scalar
        def _(e):
            for f in act.prog:
                f(e)

        @block.vector
        def _(e):
            for f in dve.prog:
                f(e)

        @block.gpsimd
        def _(e):
            for f in pool.prog:
                f(e)

        @block.sync
        def _(e):
            for f in sp.prog:
                f(e)
    return nc, es


def rope_tables(S):
    GRID_W = 64
    rows = S // GRID_W
    row = np.repeat(np.arange(rows, dtype=np.float32), GRID_W)
    col = np.tile(np.arange(GRID_W, dtype=np.float32), rows)
    inv_freq = (10000.0 ** (-np.arange(0, 64, 2, dtype=np.float32) / 64)).astype(np.float32)
    ang = np.stack([row, col], axis=-1)[:, :, None] * inv_freq
    ang = np.broadcast_to(ang[:, :, None, :], (S, 2, 2, 32)).reshape(S, 128)
    cos = np.cos(ang).astype(np.float32)
    sin = np.sin(ang).astype(np.float32).reshape(S, 2, 2, 32).copy()
    sin[:, :, 0, :] *= -1.0
    return cos, sin.reshape(S, 128)


def swap_halves(w, DEPTH):
    w = np.asarray(w, dtype=np.float32).reshape(DEPTH, 2, 2, 32)
    return np.ascontiguousarray(w[:, :, ::-1, :]).reshape(DEPTH, 1, 128)


def host_consts():
    ident = np.eye(128, dtype=np.float32)
    shf = np.zeros((128, 4, 128), np.float32)
    for t in range(128):
        if t - 1 >= 0:
            shf[t - 1, 0, t] = 1.0
        if t + 1 < 128:
            shf[t + 1, 1, t] = 1.0
    shf[127, 2, 0] = 1.0
    shf[0, 3, 127] = 1.0
    e2 = np.zeros((2, 2, 128), np.float32)
    e2[0, 0, 0] = 1.0
    e2[1, 1, 127] = 1.0
    return ident, shf, e2


_CACHE = {}


def run(inputs, TOK, DEPTH=2):
    x = np.ascontiguousarray(inputs["x"], dtype=np.float32)
    Bsz, S, _ = x.shape
    assert Bsz == 2 and S == NR * TOK
    key = (TOK, DEPTH)
    if key not in _CACHE:
        _CACHE[key] = build_program(TOK, DEPTH)
    nc, es = _CACHE[key]
    f32 = lambda a: np.ascontiguousarray(np.asarray(a, dtype=np.float32))
    cos, sin = rope_tables(S)
    ident, shf, e2 = host_consts()
    pp_layout = lambda w: f32(np.asarray(w).reshape(DEPTH, 16, 128).transpose(0, 2, 1))
    common = {
        "w_in": f32(inputs["w_in"]), "w_out": f32(inputs["w_out"]),
        "nw_pp": pp_layout(inputs["norm_w"]), "bw_pp": pp_layout(inputs["branch_norm_w"]),
        "qw": f32(inputs["q_norm_w"]).reshape(DEPTH, 1, 128), "kw": f32(inputs["k_norm_w"]).reshape(DEPTH, 1, 128),
        "qws": swap_halves(inputs["q_norm_w"], DEPTH), "kws": swap_halves(inputs["k_norm_w"], DEPTH),
        "cw": f32(np.asarray(inputs["conv_w"]).transpose(0, 2, 1)),
        "snw": f32(inputs["sgu_norm_w"]).reshape(DEPTH, 1, 512),
        "sb_pp": f32(np.asarray(inputs["sgu_b"]).transpose(0, 2, 1)),
        "wsT": f32(np.asarray(inputs["sgu_w"]).transpose(0, 3, 1, 2)),
        "fw": f32(inputs["final_norm_w"]).reshape(1, D),
        "ident": ident,
    }
    in_maps = []
    for c in range(NCORES):
        b, r = c // NR, c % NR
        sel = np.zeros((8, 2), np.float32)
        if r > 0:
            sel[(r - 1) * 2 + 1, 0] = 1.0
        if r < NR - 1:
            sel[(r + 1) * 2 + 0, 1] = 1.0
        m = dict(common)
        m["x"] = np.ascontiguousarray(x[b, r * TOK:(r + 1) * TOK, :])
        m["cs_t"] = np.ascontiguousarray(np.stack([cos[r * TOK:(r + 1) * TOK], sin[r * TOK:(r + 1) * TOK]], axis=1))
        m["sel"] = sel
        in_maps.append(m)
    res = run_bass_kernel_spmd(nc, in_maps, core_ids=list(range(NCORES)))
    out = np.empty((Bsz, S, D), np.float32)
    for c in range(NCORES):
        b, r = c // NR, c % NR
        out[b, r * TOK:(r + 1) * TOK, :] = np.asarray(res.results[c]["y"], dtype=np.float32)
    return out


def kernel(**inputs):
    S = np.asarray(inputs["x"]).shape[1]
    return run(inputs, S // NR, DEPTH=2)
```
